# Optimizing a Trainium2 kernel written in Bass

```python
import math
import jax, jax.numpy as jnp
from jax import lax
import numpy as np

D_MODEL = 1024
BATCH = 2
SEQ = 8192
DEPTH = 1

PLE_DIM = 256
SGU_CHUNK = 128
SGU_GROUPS = 8
SGU_WIDTH = D_MODEL
SGU_GROUP_DIM = SGU_WIDTH // SGU_GROUPS
ATT_HEADS = 16
HEAD_DIM = 64
ATT_WIDTH = ATT_HEADS * HEAD_DIM
MOBA_BLOCK = 256
MOBA_TOPK = 3
Q_CHUNK = 64
REL_BUCKETS = 32
REL_MAX_DIST = 1024
D_FF = 4 * D_MODEL
EPS = 1e-6
SPLITS = (SGU_WIDTH, SGU_WIDTH, ATT_WIDTH, ATT_WIDTH, ATT_WIDTH, D_MODEL, D_MODEL)
IN_COLS = sum(SPLITS)

kernel_name = "hybrid_gmlp_moba_gated_block"


def rms_norm(x, g):
    xf = x.astype(jnp.float32)
    y = xf * lax.rsqrt(jnp.mean(xf * xf, axis=-1, keepdims=True) + EPS)
    return (y * g.astype(jnp.float32)).astype(x.dtype)


def layer_norm(x, g, b):
    xf = x.astype(jnp.float32)
    mu = jnp.mean(xf, axis=-1, keepdims=True)
    xc = xf - mu
    y = xc * lax.rsqrt(jnp.mean(xc * xc, axis=-1, keepdims=True) + EPS)
    return (y * g.astype(jnp.float32) + b.astype(jnp.float32)).astype(x.dtype)


def rel_bucket(dist):
    n = jnp.maximum(dist, 0)
    max_exact = REL_BUCKETS // 2
    nf = jnp.maximum(n, max_exact).astype(jnp.float32)
    large = max_exact + (jnp.log(nf / max_exact) / math.log(REL_MAX_DIST / max_exact)
                         * (REL_BUCKETS - max_exact)).astype(jnp.int32)
    large = jnp.minimum(large, REL_BUCKETS - 1)
    return jnp.where(n < max_exact, n, large)


def sgu_mixer(u, v, w_s, b_s, ln_g, ln_b):
    B, S, _ = u.shape
    nc = S // SGU_CHUNK
    v = layer_norm(v, ln_g, ln_b)
    vg = v.reshape(B, nc, SGU_CHUNK, SGU_GROUPS, SGU_GROUP_DIM)
    causal = jnp.tril(jnp.ones((SGU_CHUNK, SGU_CHUNK), dtype=bool))
    w = jnp.where(causal, w_s, 0).astype(v.dtype)
    mixed = jnp.einsum('gts,bnsgc->bntgc', w, vg) + b_s.T[None, None, :, :, None].astype(v.dtype)
    return u * mixed.reshape(B, S, SGU_WIDTH)


_gather_blocks = jax.vmap(jax.vmap(lambda blocks, idx: blocks[idx]))


def moba_attention(q, k, v, rel_bias):
    B, H, S, dh = q.shape
    nb = S // MOBA_BLOCK
    topk = min(MOBA_TOPK, nb)
    kb = k.reshape(B, H, nb, MOBA_BLOCK, dh)
    vb = v.reshape(B, H, nb, MOBA_BLOCK, dh)
    k_mean = jnp.mean(kb.astype(jnp.float32), axis=3).astype(k.dtype)
    scale = dh ** -0.5
    table = rel_bias.T
    h_idx = jnp.arange(H)[None, :, None, None, None]
    pos_in_block = jnp.arange(MOBA_BLOCK)
    n_chunks = S // Q_CHUNK
    q_chunks = q.reshape(B, H, n_chunks, Q_CHUNK, dh).transpose(2, 0, 1, 3, 4)
    n_sel = topk * MOBA_BLOCK

    def one_chunk(args):
        c, q_c = args
        q_pos = c * Q_CHUNK + jnp.arange(Q_CHUNK)
        own = (c * Q_CHUNK) // MOBA_BLOCK
        gate = jnp.einsum('bhqd,bhnd->bhqn', q_c, k_mean).astype(jnp.float32)
        gate = jnp.where(jnp.arange(nb) < own, gate, -jnp.inf)
        _, sel = lax.top_k(gate, topk)
        sel_valid = sel < own
        k_sel = _gather_blocks(kb, sel)
        v_sel = _gather_blocks(vb, sel)
        s_sel = jnp.einsum('bhqd,bhqkmd->bhqkm', q_c, k_sel).astype(jnp.float32) * scale
        k_pos_sel = sel[..., None] * MOBA_BLOCK + pos_in_block
        dist_sel = q_pos[:, None, None] - k_pos_sel
        s_sel = s_sel + table[h_idx, rel_bucket(dist_sel)].astype(jnp.float32)
        s_sel = jnp.where(sel_valid[..., None], s_sel, -jnp.inf)
        k_own = lax.dynamic_index_in_dim(kb, own, axis=2, keepdims=False)
        v_own = lax.dynamic_index_in_dim(vb, own, axis=2, keepdims=False)
        s_own = jnp.einsum('bhqd,bhmd->bhqm', q_c, k_own).astype(jnp.float32) * scale
        dist_own = q_pos[:, None] - (own * MOBA_BLOCK + pos_in_block)[None, :]
        s_own = s_own + table[:, rel_bucket(dist_own)].astype(jnp.float32)
        s_own = jnp.where(dist_own >= 0, s_own, -jnp.inf)
        s_all = jnp.concatenate([s_sel.reshape(B, H, Q_CHUNK, n_sel), s_own], axis=-1)
        probs = jax.nn.softmax(s_all, axis=-1).astype(v.dtype)
        p_sel = probs[..., :n_sel].reshape(B, H, Q_CHUNK, topk, MOBA_BLOCK)
        p_own = probs[..., n_sel:]
        return (jnp.einsum('bhqkm,bhqkmd->bhqd', p_sel, v_sel)
                + jnp.einsum('bhqm,bhmd->bhqd', p_own, v_own))

    out = lax.map(one_chunk, (jnp.arange(n_chunks), q_chunks))
    return out.transpose(1, 2, 0, 3, 4).reshape(B, H, S, dh)


def setup_inputs(seed: int = 0) -> dict:
    key = jax.random.key(seed)
    ks = jax.random.split(key, 20)
    f32 = jnp.float32
    nrm = lambda k, shape, s: jax.random.normal(k, shape, f32) * s
    L = DEPTH
    return {
        "x": nrm(ks[0], (BATCH, SEQ, D_MODEL), 1.0),
        "p": nrm(ks[1], (DEPTH, BATCH, SEQ, PLE_DIM), 1.0),
        "norm_mix_g": 1.0 + nrm(ks[2], (L, D_MODEL), 0.05),
        "w_in": nrm(ks[3], (L, D_MODEL, IN_COLS), D_MODEL ** -0.5),
        "w_sgu_spatial": nrm(ks[4], (L, SGU_GROUPS, SGU_CHUNK, SGU_CHUNK), SGU_CHUNK ** -0.5),
        "b_sgu_spatial": 1.0 + nrm(ks[5], (L, SGU_GROUPS, SGU_CHUNK), 0.1),
        "ln_v_g": 1.0 + nrm(ks[6], (L, SGU_WIDTH), 0.05),
        "ln_v_b": nrm(ks[7], (L, SGU_WIDTH), 0.02),
        "rel_bias": nrm(ks[8], (REL_BUCKETS, ATT_HEADS), 0.5),
        "w_out": nrm(ks[9], (L, D_MODEL, D_MODEL), D_MODEL ** -0.5),
        "norm_ffn_g": 1.0 + nrm(ks[10], (L, D_MODEL), 0.05),
        "w_ff1": nrm(ks[11], (L, D_MODEL, D_FF), D_MODEL ** -0.5),
        "w_ff2": nrm(ks[12], (L, D_FF, D_MODEL), D_FF ** -0.5),
        "norm_ple_g": 1.0 + nrm(ks[13], (L, D_MODEL), 0.05),
        "w_ple_gate": nrm(ks[14], (L, D_MODEL, D_MODEL), D_MODEL ** -0.5),
        "w_ple_proj": nrm(ks[15], (L, PLE_DIM, D_MODEL), PLE_DIM ** -0.5),
        "norm_final_g": 1.0 + nrm(ks[16], (D_MODEL,), 0.05),
    }


def reference(x, p, norm_mix_g, w_in, w_sgu_spatial, b_sgu_spatial, ln_v_g, ln_v_b, rel_bias,
              w_out, norm_ffn_g, w_ff1, w_ff2, norm_ple_g, w_ple_gate, w_ple_proj, norm_final_g):
    B, S, _ = x.shape
    offsets = np.cumsum(SPLITS)[:-1].tolist()
    for i in range(DEPTH):
        h = rms_norm(x, norm_mix_g[i])
        proj = h @ w_in[i]
        za_u, za_v, q, k, v_att, g_a, g_b = jnp.split(proj, offsets, axis=-1)
        y_a = sgu_mixer(jax.nn.gelu(za_u), jax.nn.gelu(za_v), w_sgu_spatial[i], b_sgu_spatial[i],
                        ln_v_g[i], ln_v_b[i])
        heads = lambda t: t.reshape(B, S, ATT_HEADS, HEAD_DIM).transpose(0, 2, 1, 3)
        y_b = moba_attention(heads(q), heads(k), heads(v_att), rel_bias)
        y_b = y_b.transpose(0, 2, 1, 3).reshape(B, S, ATT_WIDTH)
        merged = jax.nn.sigmoid(g_a) * y_a + jax.nn.sigmoid(g_b) * y_b
        x = x + merged @ w_out[i]
        h = rms_norm(x, norm_ffn_g[i])
        x = x + jnp.square(jax.nn.relu(h @ w_ff1[i])) @ w_ff2[i]
        gate = jax.nn.sigmoid(rms_norm(x, norm_ple_g[i]) @ w_ple_gate[i])
        x = x + gate * (p[i] @ w_ple_proj[i])
    return rms_norm(x, norm_final_g)
```

```python
import math
import os
from contextlib import ExitStack
import numpy as np
import concourse.bass as bass
import concourse.mybir as mybir
from concourse.bass_utils import run_bass_kernel_spmd

F32 = mybir.dt.float32
BF16 = mybir.dt.bfloat16
AF = mybir.ActivationFunctionType
ALU = mybir.AluOpType
AX = mybir.AxisListType

NEG = -30000.0
STRIPW = 1792


class _Op:
    __slots__ = ("eng", "fn", "waits", "signal", "count", "dma_sem", "dma_count")

    def __init__(self, eng, fn):
        self.eng = eng
        self.fn = fn
        self.waits = []
        self.signal = False
        self.count = None
        self.dma_sem = None
        self.dma_count = None


class Prog:
    STREAMS = ("pe", "act", "dve", "pool", "sp")

    def __init__(self):
        self.ops = []
        self.stream_ops = {s: [] for s in self.STREAMS}
        self.last_write = {}
        self.readers = {}
        self.dma_groups = {}

    def _tok(self, i):
        op = self.ops[i]
        if op.dma_sem is not None:
            return ("dma:" + op.dma_sem, op.dma_count)
        op.signal = True
        return ("eng:" + op.eng, i)

    MAXOPS = int(os.environ.get("K_MAXOPS", "100000000"))

    def add(self, eng, fn, reads=(), writes=(), dsem=None):
        if len(self.ops) >= self.MAXOPS:
            return None
        op = _Op(eng, fn)
        idx = len(self.ops)
        deps = set()
        for k in reads:
            w = self.last_write.get(k)
            if w is not None:
                deps.add(w)
        for k in writes:
            w = self.last_write.get(k)
            if w is not None:
                deps.add(w)
            for r in self.readers.get(k, ()):
                deps.add(r)
        if dsem is not None:
            c = self.dma_groups.get(dsem, 0) + 16
            self.dma_groups[dsem] = c
            op.dma_sem = dsem
            op.dma_count = c
        self.ops.append(op)
        self.stream_ops[eng].append(idx)
        for d in deps:
            dop = self.ops[d]
            if dop.dma_sem is None and dop.eng == eng and eng in ("pe", "sp"):
                continue
            op.waits.append(self._tok(d))
        for k in reads:
            self.readers.setdefault(k, []).append(idx)
        for k in writes:
            self.last_write[k] = idx
            self.readers[k] = []
        return idx

    def barrier(self, exclude=()):
        toks = []
        for s, lst in self.stream_ops.items():
            for i in reversed(lst):
                if self.ops[i].fn is not None and self.ops[i].dma_sem is None:
                    toks.append(self._tok(i))
                    break
        for name, c in self.dma_groups.items():
            if name in exclude:
                continue
            toks.append(("dma:" + name, c))
        for s in self.STREAMS:
            op = _Op(s, None)
            op.waits = list(toks)
            self.ops.append(op)
            self.stream_ops[s].append(len(self.ops) - 1)
        self.last_write = {}
        self.readers = {}

    def finalize(self):
        cnt = {s: 0 for s in self.STREAMS}
        for op in self.ops:
            if op.dma_sem is None and op.signal:
                cnt[op.eng] += 1
                op.count = cnt[op.eng]

    def sem_keys(self):
        return ["eng:" + s for s in self.STREAMS] + ["dma:" + n for n in self.dma_groups]

    def emit_stream(self, stream, eng, sems):
        waited = {}
        for i in self.stream_ops[stream]:
            op = self.ops[i]
            need = {}
            for (k, v) in op.waits:
                if k.startswith("eng:"):
                    v = self.ops[v].count
                if v > need.get(k, 0):
                    need[k] = v
            for k, v in need.items():
                if waited.get(k, 0) >= v:
                    continue
                if k == "eng:" + stream and stream in ("pe", "sp"):
                    continue
                eng.wait_ge(sems[k], v)
                waited[k] = v
            if op.fn is None:
                continue
            ins = op.fn(eng)
            if op.dma_sem is not None:
                ins.then_inc(sems["dma:" + op.dma_sem], 16)
            elif op.signal:
                ins.then_inc(sems["eng:" + stream], 1)


class Arena:
    def __init__(self, ap, nwords):
        self.ar = ap
        self.n = nwords
        self.off = 0

    def f32(self, cols, parts=128):
        a = self.ar[0:parts, self.off:self.off + cols]
        self.off += cols
        assert self.off <= self.n, (self.off, self.n)
        return a

    def bf16(self, cols, parts=128):
        w = (cols + 1) // 2
        a = self.ar[0:parts, self.off:self.off + w].bitcast(BF16)
        self.off += w
        assert self.off <= self.n, (self.off, self.n)
        return a


def build_program(stop_phase=None, dbg=False):
    nc = bass.Bass("TRN2", target_bir_lowering=False)

    def din(name, shape, dt=F32):
        return nc.dram_tensor(name, shape, dt, kind="ExternalInput").ap()

    def dscr(name, shape, dt=BF16):
        ext = dbg and name in ("Kscr", "Vscr", "Qscr", "Yscr")
        return nc.dram_tensor(name, shape, dt, kind=("ExternalOutput" if ext else "Internal")).ap()

    xk = din("xk", [16, 128, 8, 512])
    pT = din("pT", [4, 128, 2, 512])
    w_in = din("w_in", [1024, 7168])
    w_out = din("w_out", [1024, 1024])
    w_ff1 = din("w_ff1", [1024, 4096])
    w_ff2 = din("w_ff2", [4096, 1024])
    w_pg = din("w_pg", [1024, 1024])
    w_pp = din("w_pp", [256, 1024])
    gcols = din("gcols", [128, 32])
    lnv = din("lnv", [2, 1024])
    bsg = din("bsg", [1, 1024])
    wsT = din("wsT", [128, 1024])
    trilT = din("trilT", [128, 128])
    strip = din("strip", [16, 128, STRIPW])
    onehot = din("onehot", [32, 8192], BF16)
    ident = din("ident", [128, 128])
    cfar = din("cfar", [128, 16])
    gatebias = din("gatebias", [128, 512])
    ownmask = din("ownmask", [128, 512])
    outT = nc.dram_tensor("outT", [4, 128, 8, 512], F32, kind="ExternalOutput").ap()

    Kscr = dscr("Kscr", [8, 128, 8192])
    Vscr = dscr("Vscr", [128, 16, 64, 128])
    Qscr = dscr("Qscr", [16, 96, 2048])
    Yscr = dscr("Yscr", [8, 128, 2048])
    Wa = dscr("Wa", [1024, 4096])
    Wo = dscr("Wo", [1024, 1024])
    W1 = dscr("W1", [1024, 4096])
    W2 = dscr("W2", [4096, 1024])
    Wg = dscr("Wg", [1024, 1024])
    Wp = dscr("Wp", [256, 1024])

    P = Prog()
    es = ExitStack()
    NW = 53000
    arena_t = es.enter_context(nc.sbuf_tensor("arena", [128, NW], F32))
    A = Arena(arena_t, NW)
    ps = []
    psall = es.enter_context(nc.psum_tensor("psall", [128, 4096], F32))
    for i in range(8):
        ps.append((psall[:, i * 512:(i + 1) * 512], "ps%d" % i))
    bank_ctr = [0]

    def nextbank(lo=int(os.environ.get("K_BANKLO", "0")), hi=8):
        n = hi - lo
        b = ps[lo + bank_ctr[0] % n]
        bank_ctr[0] += 1
        return b

    def mm(out, lhsT, rhs, start, stop, reads, writes):
        P.add("pe", lambda e: e.matmul(out, lhsT=lhsT, rhs=rhs, start=start, stop=stop), reads, writes)

    def act(out, in_, func, reads, writes, bias=None, scale=1.0):
        if bias is None:
            P.add("act", lambda e: e.activation(out=out, in_=in_, func=func, scale=scale), reads, writes)
        else:
            P.add("act", lambda e: e.activation(out=out, in_=in_, func=func, bias=bias, scale=scale), reads, writes)

    def tt(eng, out, in0, in1, op, reads, writes):
        P.add(eng, lambda e: e.tensor_tensor(out=out, in0=in0, in1=in1, op=op), reads, writes)

    def ts(eng, out, in0, s1, s2, op0, op1, reads, writes):
        if s2 is None:
            P.add(eng, lambda e: e.tensor_scalar(out=out, in0=in0, scalar1=s1, scalar2=None, op0=op0), reads, writes)
        else:
            P.add(eng, lambda e: e.tensor_scalar(out=out, in0=in0, scalar1=s1, scalar2=s2, op0=op0, op1=op1), reads, writes)

    def stt(eng, out, in0, scalar, in1, op0, op1, reads, writes):
        P.add(eng, lambda e: e.scalar_tensor_tensor(out=out, in0=in0, scalar=scalar, in1=in1, op0=op0, op1=op1), reads, writes)

    def cp(eng, out, in_, reads, writes):
        if eng == "act":
            P.add(eng, lambda e: e.activation(out=out, in_=in_, func=AF.Copy), reads, writes)
        else:
            P.add(eng, lambda e: e.tensor_copy(out=out, in_=in_), reads, writes)

    def dma(q, out, in_, reads, writes, dsem):
        if q == "pool":
            writes = list(writes) + ["swq"]
        P.add(q, lambda e: e.dma_start(out=out, in_=in_), reads, writes, dsem=dsem)

    ones_bf = A.bf16(128)
    ident_bf = A.bf16(128)
    epsc = A.f32(1)
    zeroc = A.f32(1)
    gcols_sb = A.f32(32)
    cfar_sb = A.f32(16)
    base = A.off

    P.add("pool", lambda e: e.memset(ones_bf, 1.0), writes=["ones"])
    P.add("pool", lambda e: e.memset(epsc, 1e-6), writes=["consts"])
    P.add("pool", lambda e: e.memset(zeroc, 0.0), writes=["consts"])
    ident_f = A.f32(128)
    base = A.off
    dma("sp", ident_f, ident[:, :], [], ["ident_f"], "c_ident")
    cp("pool", ident_bf, ident_f, ["ident_f"], ["ident"])
    dma("sp", gcols_sb, gcols[:, :], [], ["gcols"], "c_gcols")
    dma("sp", cfar_sb, cfar[:, :], [], ["cfar"], "c_cfar")

    def rmsnorm(X, xkeys, gi, out, outkeys, sq, sd, rstd):
        act(sq.rearrange("p k t -> p (k t)"), X.rearrange("p k t -> p (k t)"), AF.Square, xkeys, ["sq"])
        b, bk = nextbank()
        for c in range(8):
            mm(b, ones_bf, sq[:, c, :], c == 0, c == 7, ["sq", "ones"], [bk])
        act(sd, b, AF.Sqrt, [bk, "consts"], ["sd"], bias=epsc[:, 0:1], scale=1.0 / 1024.0)
        P.add("dve", lambda e: e.reciprocal(out=rstd, in_=sd), ["sd"], ["rstd"])
        for c in range(8):
            stt("dve", out[:, c, :], X[:, c, :], gcols_sb[:, gi * 8 + c: gi * 8 + c + 1], rstd, ALU.mult, ALU.mult,
                [xkeys[c], "rstd", "gcols"], [outkeys[c]])

    Wqkv = A.bf16(8 * 3072).rearrange("p (k c) -> p k c", k=8)
    Xk = [A.f32(4096).rearrange("p (k t) -> p k t", k=8) for _ in range(2)]
    sq = A.bf16(4096).rearrange("p (k t) -> p k t", k=8)
    hTs = [A.bf16(4096).rearrange("p (k t) -> p k t", k=8) for _ in range(2)]
    Kst = [A.bf16(4096).rearrange("p (k t) -> p k t", k=8) for _ in range(2)]
    Vst = [A.bf16(8192).rearrange("p (j two a d) -> p j two a d", j=8, two=2, a=4) for _ in range(2)]
    Qst = A.bf16(8192, parts=96)
    Qst3 = Qst.rearrange("p (h t) -> p h t", h=16)
    Qst4 = Qst.rearrange("p (g a t) -> p g a t", g=4, a=4)
    sd = A.f32(512)
    rstd = A.f32(512)
    ksum = A.f32(256).rearrange("p (j n) -> p j n", j=8)
    kmeanH = A.bf16(512, parts=64).rearrange("p (j a n) -> p j a n", j=8, a=2)
    gb_sb = A.f32(512).rearrange("p (i n) -> p i n", i=16)
    om_sb = A.f32(512).rearrange("p (i n) -> p i n", i=16)
    gms = [A.f32(512) for _ in range(4)]
    m8 = A.f32(128).rearrange("p (h n) -> p h n", h=16)
    thr = A.f32(16)
    sel = A.f32(512)
    sel3 = sel.rearrange("p (h n) -> p h n", h=16)
    selbs = [A.bf16(512) for _ in range(4)]
    if os.environ.get("K_TRACE"): print("phase1 arena words", A.off)

    w_in_v = w_in.rearrange("(k p) c -> p k c", p=128)
    wstg = [A.f32(1024) for _ in range(2)]
    n_ = 0
    for i, nm in (1, "wk"), (2, "wv"), (0, "wq"):
        for k8 in range(8):
            st_ = n_ % 2
            n_ += 1
            dma("sp", wstg[st_], w_in_v[:, k8, 2048 + i * 1024: 2048 + (i + 1) * 1024], [], ["wstg%d" % st_], "c_wstg%d" % st_)
            cp("pool" if k8 % 2 else "act", Wqkv[:, k8, i * 1024:(i + 1) * 1024], wstg[st_], ["wstg%d" % st_], [nm + str(k8)])
    dma("sp", gb_sb.rearrange("p i n -> p (i n)"), gatebias[:, :], [], ["gconst"], "c_gb")
    dma("sp", om_sb.rearrange("p i n -> p (i n)"), ownmask[:, :], [], ["gconst2"], "c_om")
    P.add("pool", lambda e: e.memset(ksum.rearrange("p j n -> p (j n)"), 0.0), writes=["ksum%d" % j for j in range(8)])
    for s_ in range(2):
        P.add("pool", (lambda o: (lambda e: e.memset(o, 1.0)))(Vst[s_].rearrange("p j two a d -> p (j two a d)")),
              writes=["Vst%d_%d_%d" % (s_, a, h) for a in range(4) for h in range(2)])

    def wcast(dst, src, rows, cols, rblk):
        for r0 in range(0, rows, rblk):
            dma("pool", dst[r0:r0 + rblk, :], src[r0:r0 + rblk, :], [], [], "wcast")

    def xkeys_of(s):
        return ["Xk%d_%d" % (s, c) for c in range(8)]

    def hkeys_of(s):
        return ["hT%d_%d" % (s, c) for c in range(8)]

    NT1 = int(os.environ.get('K_NT1', '16'))
    for m0 in range(min(2, NT1)):
        dma("sp", Xk[m0].rearrange("p k t -> p (k t)"), xk[m0].rearrange("p k t -> p (k t)"), [], xkeys_of(m0), "xk%d" % m0)
    wcast_done = False
    rmsnorm(Xk[0], xkeys_of(0), 0, hTs[0], hkeys_of(0), sq, sd, rstd)
    for m in range(NT1):
        s = m % 2
        if False:
            wcast_done = True
            wcast(Wa[:, 0:2048], w_in[:, 0:2048], 1024, 2048, 512)
            wcast(Wa[:, 2048:4096], w_in[:, 5120:7168], 1024, 2048, 512)
            wcast(Wo, w_out, 1024, 1024, 1024)
            wcast(W1, w_ff1, 1024, 4096, 256)
            wcast(W2, w_ff2, 4096, 1024, 1024)
            wcast(Wg, w_pg, 1024, 1024, 1024)
            wcast(Wp, w_pp, 256, 1024, 256)
        hT = hTs[s]
        hk = hkeys_of(s)
        own = (m % 4 == 3)
        p = m // 4
        for j in range(8):
            b, bk = nextbank()
            for c in range(8):
                mm(b, Wqkv[:, c, 1024 + j * 128: 1024 + (j + 1) * 128], hT[:, c, :], c == 0, c == 7, [hk[c], "wk%d" % c], [bk])
            act(Kst[s][:, j, :], b, AF.Copy, [bk], ["Kst%d_%d" % (s, j)])
            P.add("dve", (lambda o, i: (lambda e: e.tensor_reduce(out=o, in_=i, axis=AX.X, op=ALU.add)))(
                ksum[:, j, 2 * m:2 * m + 2], Kst[s][:, j, :].rearrange("p (a t) -> p a t", a=2)), ["Kst%d_%d" % (s, j)], ["ksum%d" % j])
        if m + 1 < NT1:
            s1 = (m + 1) % 2
            rmsnorm(Xk[s1], xkeys_of(s1), 0, hTs[s1], hkeys_of(s1), sq, sd, rstd)
        if m + 2 < NT1:
            dma("sp", Xk[s].rearrange("p k t -> p (k t)"), xk[m + 2].rearrange("p k t -> p (k t)"), [], xkeys_of(s), "xk%d" % s)
        def emit_V(ev):
            for t4 in range(4):
                for half in range(2):
                    b, bk = nextbank()
                    for c in range(8):
                        mm(b, hT[:, c, t4 * 128:(t4 + 1) * 128], Wqkv[:, c, 2048 + half * 512: 2048 + (half + 1) * 512],
                           c == 0, c == 7, [hk[c], "wv%d" % c], [bk])
                    b4 = b.rearrange("p (j two d) -> p j two d", j=4, two=2)
                    cp(ev, Vst[s][:, half * 4:(half + 1) * 4, 0, t4, 0:64], b4[:, :, 0, :], [bk], ["Vst%d_%d_%d" % (s, t4, half)])
                    cp(ev, Vst[s][:, half * 4:(half + 1) * 4, 1, t4, 64:128], b4[:, :, 1, :], [bk], ["Vst%d_%d_%d" % (s, t4, half)])

        if not own:
            emit_V("dve")
        else:
            for j in range(8):
                b, bk = nextbank()
                for c in range(8):
                    mm(b, Wqkv[:, c, j * 128:(j + 1) * 128], hT[:, c, :], c == 0, c == 7, [hk[c], "wq%d" % c], [bk])
                act(Qst3[0:64, 2 * j, :], b[0:64, :], AF.Copy, [bk], ["Qst_%d" % (2 * j)], scale=0.125)
                act(Qst3[0:64, 2 * j + 1, :], b[64:128, :], AF.Copy, [bk], ["Qst_%d" % (2 * j + 1)], scale=0.125)
            cp("dve", kmeanH[:, :, 0, :], ksum[0:64, :, :], ["ksum%d" % j for j in range(8)], ["kmeanH0"])
            cp("dve", kmeanH[:, :, 1, :], ksum[64:128, :, :], ["ksum%d" % j for j in range(8)], ["kmeanH1"])
            for qt in range(4):
                idx = p * 4 + qt
                b, bk = nextbank()
                for h in range(16):
                    mm(b[:, h * 32:(h + 1) * 32], Qst3[0:64, h, qt * 128:(qt + 1) * 128], kmeanH[0:64, h // 2, h % 2, :],
                       True, True, ["Qst_%d" % h, "kmeanH%d" % (h % 2)], [bk])
                tt("dve", gms[qt].rearrange("p (h n) -> p h n", h=16), b.rearrange("p (h n) -> p h n", h=16),
                   gb_sb[:, idx, :].unsqueeze(1).to_broadcast([128, 16, 32]), ALU.add, [bk, "gconst"], ["gm%d" % qt])
            emit_V("act")
            for qt in range(4):
                idx = p * 4 + qt
                gm3 = gms[qt].rearrange("p (h n) -> p h n", h=16)
                gk = "gm%d" % qt
                for h in range(16):
                    P.add("dve", (lambda o, i: (lambda e: e.max(out=o, in_=i)))(m8[:, h, :], gm3[:, h, :]), [gk], ["m8_%d" % h])
                ts("dve", thr.unsqueeze(2), m8[:, :, 2:3], -1e29, None, ALU.max, None, ["m8_%d" % h for h in range(16)], ["thr"])
                tt("dve", sel3, gm3, thr.unsqueeze(2).to_broadcast([128, 16, 32]), ALU.is_ge, [gk, "thr"], ["sel"])
                tt("dve", sel3, sel3, om_sb[:, idx, :].unsqueeze(1).to_broadcast([128, 16, 32]), ALU.add, ["sel", "gconst2"], ["sel"])
                ts("dve", selbs[qt], sel, -NEG, NEG, ALU.mult, ALU.add, ["sel"], ["selb%d" % qt])
            for qt in range(4):
                b2, bk2 = nextbank()
                for g in range(4):
                    mm(b2[:, g * 128:(g + 1) * 128], selbs[qt][:, g * 128:(g + 1) * 128], ident_bf, True, True, ["selb%d" % qt, "ident"], [bk2])
                for hp in range(4):
                    act(Qst4[64:96, :, hp, qt * 128:(qt + 1) * 128], b2[32 * hp:32 * hp + 32, :].rearrange("p (g q) -> p g q", g=4),
                        AF.Copy, [bk2], ["Qsel_%d" % hp])
        if os.environ.get("K_TRACE"): print("tile", m, "after V/Q/gate", len(P.ops))
        if os.environ.get('K_SKIP_ST'):
            continue
        dma("sp", Kscr[:, :, m * 512:(m + 1) * 512].rearrange("j p t -> p j t"), Kst[s],
            ["Kst%d_%d" % (s, j) for j in range(8)], [], "kst%d" % s)
        dma("sp", Vscr[:, :, 4 * m:4 * m + 4, :], Vst[s].rearrange("p j two a d -> p (j two) a d"),
            ["Vst%d_%d_%d" % (s, a, h) for a in range(4) for h in range(2)], [], "vst%d" % s)
        if own:
            dma("sp", Qscr[:, :, p * 512:(p + 1) * 512].rearrange("h r t -> r h t"), Qst3,
                ["Qst_%d" % h for h in range(16)] + ["Qsel_%d" % h for h in range(4)], [], "qst")

    P.barrier(exclude=("wcast",))
    if stop_phase == 1:
        P.barrier()
        return _finish(nc, P, es)

    A.off = base
    KK = [[A.bf16(2048, parts=96) for c in range(5)] for hd in range(2)]
    VV = [[A.bf16(2048).rearrange("p (k d) -> p k d", k=16) for c in range(5)] for hd in range(2)]
    QQ = [[A.bf16(2048, parts=96) for sl in range(2)] for hd in range(2)]
    stripf = [A.f32(STRIPW) for hd in range(2)]
    stripb = [[A.bf16(STRIPW) for sl in range(2)] for hd in range(2)]
    NPT = 6
    PT = [A.bf16(1024) for _ in range(NPT)]
    yst = [A.bf16(512) for _ in range(2)]
    Rr = A.f32(512)
    Ocp = [A.f32(512) for _ in range(2)]

    for c in range(5):
        cc = c % 4
        for hd in range(2):
            dma("sp" if hd == 0 else "act", KK[hd][c][64:96, :], onehot[:, cc * 2048:(cc + 1) * 2048], [], ["KOH%d%d" % (hd, c)], "oh%d%d" % (hd, c))
    wcf = [A.f32(4096) for _ in range(2)]
    wcb = [A.bf16(4096) for _ in range(2)]
    wpieces = []
    for r in range(8):
        wpieces.append([(Wa[r * 128:(r + 1) * 128, hf * 2048:(hf + 1) * 2048], w_in[r * 128:(r + 1) * 128, c0:c0 + 2048], hf * 2048, 2048, None)
                        for hf, c0 in ((0, 0), (1, 5120))])
    for r in range(8):
        wpieces.append([(W1[r * 128:(r + 1) * 128, :], w_ff1[r * 128:(r + 1) * 128, :], 0, 4096, None)])
    for r in range(8):
        wpieces.append([(W2[r * 512:(r + 1) * 512, :], w_ff2[r * 512:(r + 1) * 512, :], 0, 4096, 4)])
    for r in range(2):
        wpieces.append([(Wo[r * 512:(r + 1) * 512, :], w_out[r * 512:(r + 1) * 512, :], 0, 4096, 4)])
        wpieces.append([(Wg[r * 512:(r + 1) * 512, :], w_pg[r * 512:(r + 1) * 512, :], 0, 4096, 4)])
    wpieces.append([(Wp[:, :], w_pp[:, :], 0, 2048, 2)])
    wp_ctr = [0]
    cur_it = [0]

    def emit_wpiece():
        if wp_ctr[0] >= len(wpieces):
            return
        st_ = wp_ctr[0] % 2
        parts_ = wpieces[wp_ctr[0]]
        wp_ctr[0] += 1
        ntot = 0
        outs_ = []
        for (dst, src, c0, n, a) in parts_:
            if a is not None:
                dst = dst.rearrange("(a p) c -> p a c", p=128)
                src = src.rearrange("(a p) c -> p a c", p=128)
                f_ = wcf[st_][:, c0:c0 + n].rearrange("p (a c) -> p a c", a=a)
                b_ = wcb[st_][:, c0:c0 + n].rearrange("p (a c) -> p a c", a=a)
            else:
                f_ = wcf[st_][:, c0:c0 + n]
                b_ = wcb[st_][:, c0:c0 + n]
            dma("sp", f_, src, [], ["wcf%d" % st_], "wcf%d" % st_)
            outs_.append((dst, b_))
            ntot = max(ntot, c0 + n)
        hn = ntot // 2
        cp("dve", wcb[st_][:, 0:hn], wcf[st_][:, 0:hn], ["wcf%d" % st_], ["wcb%da" % st_])
        cp("dve", wcb[st_][:, hn:ntot], wcf[st_][:, hn:ntot], ["wcf%d" % st_], ["wcb%db" % st_])
        for (dst, b_) in outs_:
            deferred.append((cur_it[0] + 1, (lambda dst=dst, b_=b_, st_=st_: dma("act", dst, b_, ["wcb%da" % st_, "wcb%db" % st_], [], "wcb%d" % st_))))

    deferred = []

    def flush_deferred(force=False):
        keep = []
        while deferred:
            rdy, fn_ = deferred.pop(0)
            if force or rdy <= cur_it[0]:
                fn_()
            else:
                keep.append((rdy, fn_))
        deferred.extend(keep)

    pt_ctr = [0]
    sp_ctr = [0]
    for j in range(8):
        sl = j % 2
        for hd in range(2):
            h = 2 * j + hd
            dma("sp", QQ[hd][sl][0:96, :], Qscr[h], [], ["Q%d%d" % (hd, sl)], "q%d%d" % (hd, sl))
        cslot = [4 if (j % 2) else 0, 1, 2, 3]
        for c in range(4):
            cs = cslot[c]
            for hd in range(2):
                dma("sp", KK[hd][cs][0:64, :], Kscr[j, 64 * hd:64 * hd + 64, c * 2048:(c + 1) * 2048], [],
                    ["K%d%d" % (hd, cs)], "k%d%d" % (hd, cs))
                dma("sp", VV[hd][cs], Vscr[:, 2 * j + hd, 16 * c:16 * c + 16, :], [],
                    ["V%d%d" % (hd, cs)], "v%d%d" % (hd, cs))
            if c == 0:
                for hd in range(2):
                    h = 2 * j + hd
                    dma("sp", stripf[hd], strip[h], [], ["stripf%d" % hd], "sf%d" % hd)
                    cp("dve", stripb[hd][sl], stripf[hd], ["stripf%d" % hd], ["stripb%d%d" % (hd, sl)])
        for p in range(4):
            it = j * 4 + p
            cur_it[0] = it
            emit_wpiece()
            Ob = [ps[0], ps[1]]
            ng = 16 * (p + 1)
            LAG = 1
            pend = []

            nfar = 16 * p + 5
            groups = [[2 * i, 2 * i + 1] for i in range(nfar // 2)] + [[nfar - 1]]
            groups += [[nfar, nfar + 1], [nfar + 2]] + [[g] for g in range(ng - 4, ng)] + [[ng - 8, ng - 7], [ng - 6, ng - 5]]
            first_g = groups[0][0]
            last_g = groups[-1][-1]

            def tinfo(g, p=p, cslot=cslot):
                c, t = divmod(g, 16)
                d = 4 * (4 * p + 3) - g
                return cslot[c], t, d, (128 * (-d) if d < 0 else 0)

            def emit_pv(hd, grp, pts, Ob=Ob, first_g=first_g, last_g=last_g):
                for idx, g in enumerate(grp):
                    c, t, d, off = tinfo(g)
                    mm(Ob[hd][0][:, off:512], VV[hd][c][:, t, :], PT[pts][:, idx * 512 + off:(idx + 1) * 512], g == first_g, g == last_g,
                       ["V%d%d" % (hd, c), "PT%d" % pts], [Ob[hd][1]])

            for gi, grp in enumerate(groups):
                if gi == 6:
                    flush_deferred()
                for hd in range(2):
                    h = 2 * j + hd
                    b0 = 2 + 2 * (sp_ctr[0] % 3)
                    sp_ctr[0] += 1
                    bkeys = []
                    near = False
                    for idx, g in enumerate(grp):
                        c, t, d, off = tinfo(g)
                        near = d <= 7
                        sb_, sbk = ps[b0 + idx]
                        bkeys.append(sbk)
                        mm(sb_[:, off:512], KK[hd][c][0:96, t * 128:(t + 1) * 128], QQ[hd][sl][0:96, p * 512 + off:(p + 1) * 512], True, not near,
                           ["K%d%d" % (hd, c), "KOH%d%d" % (hd, c), "Q%d%d" % (hd, sl)], [sbk])
                        if near:
                            mm(sb_[:, off:512], ident_bf, stripb[hd][sl][:, 128 * (d + 3) + off:128 * (d + 3) + 512], False, True,
                               ["stripb%d%d" % (hd, sl), "ident"], [sbk])
                    pts = pt_ctr[0] % NPT
                    pt_ctr[0] += 1
                    if len(grp) == 2:
                        act(PT[pts][:, 0:1024], psall[:, b0 * 512:(b0 + 2) * 512], AF.Exp, bkeys + ["consts", "cfar"], ["PT%d" % pts],
                            bias=(zeroc[:, 0:1] if near else cfar_sb[:, h:h + 1]))
                    else:
                        bias = zeroc[:, 0:1] if near else cfar_sb[:, h:h + 1]
                        act(PT[pts][:, off:512], ps[b0][0][:, off:512], AF.Exp, bkeys + ["consts", "cfar"], ["PT%d" % pts], bias=bias)
                    pend.append((hd, gi, grp, pts))
                while pend and pend[0][1] <= gi - LAG:
                    hd_, gi_, grp_, pts_ = pend.pop(0)
                    emit_pv(hd_, grp_, pts_)
            while pend:
                hd_, gi_, grp_, pts_ = pend.pop(0)
                emit_pv(hd_, grp_, pts_)
            ys = it % 2
            cp("dve", Ocp[0], Ob[0][0], [Ob[0][1]], ["Ocp0"])
            cp("dve", Ocp[1], Ob[1][0], [Ob[1][1]], ["Ocp1"])
            P.add("dve", (lambda o, i: (lambda e: e.reciprocal(out=o, in_=i)))(Rr[0:64, :], Ocp[0][64:128, :]), ["Ocp0"], ["R0"])
            P.add("dve", (lambda o, i: (lambda e: e.reciprocal(out=o, in_=i)))(Rr[64:128, :], Ocp[1][0:64, :]), ["Ocp1"], ["R1"])
            tt("dve", yst[ys][0:64, :], Ocp[0][0:64, :], Rr[0:64, :], ALU.mult, ["Ocp0", "R0"], ["yst%da" % ys])
            tt("dve", yst[ys][64:128, :], Ocp[1][64:128, :], Rr[64:128, :], ALU.mult, ["Ocp1", "R1"], ["yst%db" % ys])
            deferred.append((it + 1, (lambda j=j, p=p, ys=ys: dma("act", Yscr[j, :, p * 512:(p + 1) * 512], yst[ys],
                                                                   ["yst%da" % ys, "yst%db" % ys], [], "yst%d" % ys))))

    while wp_ctr[0] < len(wpieces):
        emit_wpiece()
    flush_deferred(force=True)
    P.barrier()
    if stop_phase == 3:
        return _finish(nc, P, es)

    A.off = base
    NS = 5
    slots = [A.bf16(4096).rearrange("p (k c) -> p k c", k=8) for _ in range(NS)]
    hT = A.bf16(4096).rearrange("p (k t) -> p k t", k=8)
    sq = A.bf16(4096).rearrange("p (k t) -> p k t", k=8)
    yb = A.bf16(4096).rearrange("p (k t) -> p k t", k=8)
    vn = A.bf16(4096).rearrange("p (a c) -> p a c", a=4)
    merged = A.bf16(4096).rearrange("p (k t) -> p k t", k=8)
    aT = A.bf16(16384).rearrange("p (k t) -> p k t", k=32)
    pTb = A.bf16(1024).rearrange("p (k t) -> p k t", k=2)
    WsTb = A.bf16(1024).rearrange("p (g t) -> p g t", g=8)
    X = A.f32(4096).rearrange("p (k t) -> p k t", k=8)
    M = A.f32(4096).rearrange("p (k t) -> p k t", k=8)
    vg = [A.f32(1024) for _ in range(2)]
    vc = [A.f32(1024) for _ in range(2)]
    sqv = A.f32(1024)
    lng = A.f32(1024)
    lnb = A.f32(1024)
    bs_sb = A.f32(1024).rearrange("p (g t) -> p g t", g=8)
    NTMP = 6
    tmps = [A.f32(512) for _ in range(NTMP)]
    sd = A.f32(512)
    rstd = A.f32(512)
    wsf = A.f32(1024).rearrange("p (g t) -> p g t", g=8)
    trl = A.f32(128)
    pTf = A.f32(1024)
    s1 = A.f32(8)
    nmn = A.f32(8)
    s2 = A.f32(8)
    sdv = A.f32(8)
    rv = A.f32(8)

    dma("sp", lng, lnv[0:1, :].partition_broadcast(128), [], ["lng"], "c_lng")
    dma("sp", lnb, lnv[1:2, :].partition_broadcast(128), [], ["lnb"], "c_lnb")
    dma("sp", bs_sb.rearrange("p g t -> p (g t)"), bsg[0:1, :].partition_broadcast(128), [], ["bs"], "c_bs")
    dma("sp", wsf.rearrange("p g t -> p (g t)"), wsT[:, :], [], ["wsf"], "c_wsf")
    dma("sp", trl, trilT[:, :], [], ["trl"], "c_trl")
    tt("dve", WsTb, wsf, trl.unsqueeze(1).to_broadcast([128, 8, 128]), ALU.mult, ["wsf", "trl"], ["WsT"])

    slot_ctr = [0]

    def load_piece(src, nk):
        s = slot_ctr[0] % NS
        slot_ctr[0] += 1
        dma("sp", slots[s][:, 0:nk, :], src.rearrange("(k p) c -> p k c", p=128), [], ["ws%d" % s], "ws%d" % s)
        return slots[s], "ws%d" % s

    tmp_ctr = [0]

    def nexttmp():
        i = tmp_ctr[0] % NTMP
        tmp_ctr[0] += 1
        return tmps[i], "tmp%d" % i

    Xk_ = ["X_%d" % c for c in range(8)]
    hk = ["hT_%d" % c for c in range(8)]
    Mk = ["M_%d" % c for c in range(8)]
    mgk = ["mg_%d" % c for c in range(8)]

    def proj_fm(piece, pkey, nk, rhs3, rhs_keys, evac):
        for o in range(4):
            b, bk = nextbank()
            for k in range(nk):
                mm(b, piece[:, k, o * 128:(o + 1) * 128], rhs3[:, k, :], k == 0, k == nk - 1, [pkey, rhs_keys[k]], [bk])
            evac(o, b, bk)

    for p in range(4):
        dma("sp", X[:, 0:4, :], xk[4 * p + 3][:, 0:4, :], [], Xk_[0:4], "x4a")
        dma("act", X[:, 4:8, :], xk[4 * p + 3][:, 4:8, :], [], Xk_[4:8], "x4b")
        dma("sp", yb, Yscr[:, :, p * 512:(p + 1) * 512].rearrange("j p t -> p j t"), [], ["yb"], "yb")
        dma("sp", pTf, pT[p].rearrange("p k t -> p (k t)"), [], ["pTf"], "ptf")
        cp("pool", pTb.rearrange("p k t -> p (k t)"), pTf, ["pTf"], ["pTb"])
        rmsnorm(X, Xk_, 0, hT, hk, sq, sd, rstd)
        pv = [load_piece(Wa[:, 1024 + half * 512: 1024 + (half + 1) * 512], 8) for half in range(2)]
        for t4 in range(4):
            vgi = vg[t4 % 2]
            vci = vc[t4 % 2]
            vgk = "vg%d" % (t4 % 2)
            vck = "vc%d" % (t4 % 2)
            for half in range(2):
                b, bk = nextbank()
                for c in range(8):
                    mm(b, hT[:, c, t4 * 128:(t4 + 1) * 128], pv[half][0][:, c, :], c == 0, c == 7, [hk[c], pv[half][1]], [bk])
                act(vgi[:, half * 512:(half + 1) * 512], b, AF.Gelu_apprx_tanh, [bk], [vgk + "_%d" % half])
            vgks = [vgk + "_0", vgk + "_1"]
            P.add("dve", (lambda o, i: (lambda e: e.reduce_sum(out=o, in_=i, axis=AX.X)))(s1[:, t4:t4 + 1], vgi), vgks, ["s1"])
            ts("dve", nmn[:, t4:t4 + 1], s1[:, t4:t4 + 1], -1.0 / 1024.0, None, ALU.mult, None, ["s1"], ["nmn"])
            act(vci, vgi, AF.Identity, vgks + ["nmn"], [vck], bias=nmn[:, t4:t4 + 1])
            tt("dve", sqv, vci, vci, ALU.mult, [vck], ["sqv"])
            P.add("dve", (lambda o, i: (lambda e: e.reduce_sum(out=o, in_=i, axis=AX.X)))(s2[:, t4:t4 + 1], sqv), ["sqv"], ["s2"])
            act(sdv[:, t4:t4 + 1], s2[:, t4:t4 + 1], AF.Sqrt, ["s2", "consts"], ["sdv"], bias=epsc[:, 0:1], scale=1.0 / 1024.0)
            P.add("dve", (lambda o, i: (lambda e: e.reciprocal(out=o, in_=i)))(rv[:, t4:t4 + 1], sdv[:, t4:t4 + 1]), ["sdv"], ["rv"])
            stt("dve", vci, vci, rv[:, t4:t4 + 1], lng, ALU.mult, ALU.mult, [vck, "rv", "lng"], [vck])
            tt("dve", vn[:, t4, :], vci, lnb, ALU.add, [vck, "lnb"], ["vn%d" % t4])
        for half in range(2):
            pc, pk = load_piece(Wa[:, half * 512:(half + 1) * 512], 8)

            def ev_u(o, b, bk, half=half):
                g = half * 4 + o
                act(M[:, g, :], b, AF.Gelu_apprx_tanh, [bk], [Mk[g]])
                pass
            proj_fm(pc, pk, 8, hT, hk, ev_u)
        for half in range(2):
            pc, pk = load_piece(Wa[:, 2048 + half * 512: 2048 + (half + 1) * 512], 8)

            def ev_ga(o, b, bk, half=half):
                g = half * 4 + o
                tm, tk = nexttmp()
                act(tm, b, AF.Sigmoid, [bk], [tk])
                tt("dve", M[:, g, :], M[:, g, :], tm, ALU.mult, [Mk[g], tk], [Mk[g]])
            proj_fm(pc, pk, 8, hT, hk, ev_ga)
        for g in range(8):
            b, bk = nextbank()
            for t4 in range(4):
                mm(b[:, t4 * 128:(t4 + 1) * 128], vn[:, t4, g * 128:(g + 1) * 128], WsTb[:, g, :], True, True,
                   ["vn%d" % t4, "WsT"], [bk])
            tm, tk = nexttmp()
            tt("dve", tm.rearrange("p (a t) -> p a t", a=4), b.rearrange("p (a t) -> p a t", a=4),
               bs_sb[:, g, :].unsqueeze(1).to_broadcast([128, 4, 128]), ALU.add, [bk, "bs"], [tk])
            tt("dve", M[:, g, :], M[:, g, :], tm, ALU.mult, [Mk[g], tk], [Mk[g]])
        for half in range(2):
            pc, pk = load_piece(Wa[:, 3072 + half * 512: 3072 + (half + 1) * 512], 8)

            def ev_gb(o, b, bk, half=half):
                g = half * 4 + o
                tm, tk = nexttmp()
                act(tm, b, AF.Sigmoid, [bk], [tk])
                tt("pool", tm, tm, yb[:, g, :], ALU.mult, [tk, "yb"], [tk])
                tt("dve", merged[:, g, :], M[:, g, :], tm, ALU.add, [Mk[g], tk], [mgk[g]])
            proj_fm(pc, pk, 8, hT, hk, ev_gb)
        for half in range(2):
            pc, pk = load_piece(Wo[:, half * 512:(half + 1) * 512], 8)

            def ev_o(o, b, bk, half=half):
                g = half * 4 + o
                tt("dve", X[:, g, :], X[:, g, :], b, ALU.add, [Xk_[g], bk], [Xk_[g]])
            proj_fm(pc, pk, 8, merged, mgk, ev_o)
        rmsnorm(X, Xk_, 1, hT, hk, sq, sd, rstd)
        for q in range(8):
            pc, pk = load_piece(W1[:, q * 512:(q + 1) * 512], 8)

            def ev_f1(o, b, bk, q=q):
                fc = q * 4 + o
                tm, tk = nexttmp()
                act(tm, b, AF.Relu, [bk], [tk])
                tt("pool" if (o % 2) else "dve", aT[:, fc, :], tm, tm, ALU.mult, [tk], ["aT_%d" % fc])
            proj_fm(pc, pk, 8, hT, hk, ev_f1)
        for i in range(4):
            for half in range(2):
                pc, pk = load_piece(W2[i * 1024:(i + 1) * 1024, half * 512:(half + 1) * 512], 8)
                for o in range(4):
                    b, bk = ps[half * 4 + o]
                    for k in range(8):
                        mm(b, pc[:, k, o * 128:(o + 1) * 128], aT[:, i * 8 + k, :], (i == 0 and k == 0), (i == 3 and k == 7),
                           [pk, "aT_%d" % (i * 8 + k)], [bk])
        for oc in range(8):
            b, bk = ps[oc]
            tt("dve", X[:, oc, :], X[:, oc, :], b, ALU.add, [Xk_[oc], bk], [Xk_[oc]])
        bank_ctr[0] = 0
        rmsnorm(X, Xk_, 2, hT, hk, sq, sd, rstd)
        for half in range(2):
            pg_, pgk = load_piece(Wg[:, half * 512:(half + 1) * 512], 8)
            pp_, ppk = load_piece(Wp[:, half * 512:(half + 1) * 512], 2)
            for o in range(4):
                g = half * 4 + o
                b, bk = nextbank()
                for k in range(8):
                    mm(b, pg_[:, k, o * 128:(o + 1) * 128], hT[:, k, :], k == 0, k == 7, [pgk, hk[k]], [bk])
                tm, tk = nexttmp()
                act(tm, b, AF.Sigmoid, [bk], [tk])
                b2, bk2 = nextbank()
                for k in range(2):
                    mm(b2, pp_[:, k, o * 128:(o + 1) * 128], pTb[:, k, :], k == 0, k == 1, [ppk, "pTb"], [bk2])
                tt("dve", tm, tm, b2, ALU.mult, [tk, bk2], [tk])
                tt("dve", X[:, g, :], X[:, g, :], tm, ALU.add, [Xk_[g], tk], [Xk_[g]])
        rmsnorm(X, Xk_, 3, M, Mk, sq, sd, rstd)
        dma("act", outT[p].rearrange("p k t -> p (k t)"), M.rearrange("p k t -> p (k t)"), Mk, [], "outst")

    P.barrier()
    return _finish(nc, P, es)


def _finish(nc, P, es):
    P.finalize()
    if os.environ.get("K_TRACE"):
        cnt = {}
        for op in P.ops:
            if op.count is not None:
                cnt[op.eng] = max(cnt.get(op.eng, 0), op.count)
        print("sem counts", cnt, "dma", P.dma_groups, "nops", {k: len(v) for k, v in P.stream_ops.items()})

    sems = {}
    for k in P.sem_keys():
        sems[k] = es.enter_context(nc.semaphore(k.replace(":", "_")))
    block = es.enter_context(nc.Block())

    @block.sync
    def _(e):
        P.emit_stream("sp", e, sems)

    @block.vector
    def _(e):
        P.emit_stream("dve", e, sems)

    @block.scalar
    def _(e):
        P.emit_stream("act", e, sems)

    @block.gpsimd
    def _(e):
        P.emit_stream("pool", e, sems)

    @block.tensor
    def _(e):
        P.emit_stream("pe", e, sems)

    es.close()
    return nc


def _rel_bucket_np(dist):
    n = np.maximum(dist, 0)
    nf = np.maximum(n, 16).astype(np.float32)
    large = 16 + (np.log(nf / np.float32(16)) / np.float32(math.log(1024 / 16)) * np.float32(16)).astype(np.int32)
    large = np.minimum(large, 31)
    return np.where(n < 16, n, large)


_NC_CACHE = {}


def kernel(x, p, norm_mix_g, w_in, w_sgu_spatial, b_sgu_spatial, ln_v_g, ln_v_b, rel_bias,
           w_out, norm_ffn_g, w_ff1, w_ff2, norm_ple_g, w_ple_gate, w_ple_proj, norm_final_g):
    f32 = np.float32
    x = np.asarray(x, f32)
    p = np.asarray(p, f32)
    rel_bias = np.asarray(rel_bias, f32)
    in_maps = make_inputs(x, p, norm_mix_g, w_in, w_sgu_spatial, b_sgu_spatial, ln_v_g, ln_v_b, rel_bias,
                          w_out, norm_ffn_g, w_ff1, w_ff2, norm_ple_g, w_ple_gate, w_ple_proj, norm_final_g)
    if "nc" not in _NC_CACHE:
        _NC_CACHE["nc"] = build_program()
    nc = _NC_CACHE["nc"]
    res = run_bass_kernel_spmd(nc, in_maps, core_ids=list(range(8)))
    out = np.empty((2, 8192, 1024), f32)
    for c in range(8):
        b, j = divmod(c, 4)
        oT = res.results[c]["outT"]
        for pp in range(4):
            m = 4 * pp + j
            out[b, m * 512:(m + 1) * 512, :] = oT[pp].transpose(2, 1, 0).reshape(512, 1024)
    return out


def make_inputs(x, p, norm_mix_g, w_in, w_sgu_spatial, b_sgu_spatial, ln_v_g, ln_v_b, rel_bias,
                w_out, norm_ffn_g, w_ff1, w_ff2, norm_ple_g, w_ple_gate, w_ple_proj, norm_final_g):
    f32 = np.float32
    x = np.asarray(x, f32)
    p = np.asarray(p, f32)
    rel_bias = np.asarray(rel_bias, f32)

    def fm(a):
        t, F = a.shape
        return np.ascontiguousarray(a.T.reshape(F // 128, 128, t).transpose(1, 0, 2))

    gcols = np.zeros((128, 32), f32)
    for i, g in enumerate((norm_mix_g[0], norm_ffn_g[0], norm_ple_g[0], norm_final_g)):
        gcols[:, i * 8:(i + 1) * 8] = np.asarray(g, f32).reshape(8, 128).T
    lnv = np.stack([np.asarray(ln_v_g[0], f32), np.asarray(ln_v_b[0], f32)])
    bsg = np.ascontiguousarray(np.asarray(b_sgu_spatial[0], f32).reshape(1, 1024))
    wsT = np.ascontiguousarray(np.asarray(w_sgu_spatial[0], f32).transpose(2, 0, 1).reshape(128, 1024))
    trilT = (np.arange(128)[:, None] <= np.arange(128)[None, :]).astype(f32)
    ki = np.arange(128)[:, None]
    mm_ = np.arange(STRIPW)[None, :]
    dist = mm_ - 384 - ki
    bidx = _rel_bucket_np(dist)
    strip = np.empty((16, 128, STRIPW), f32)
    for h in range(16):
        strip[h] = np.where(dist >= 0, rel_bias[bidx, h], f32(NEG))
    import ml_dtypes
    onehot = (np.arange(8192)[None, :] // 256 == np.arange(32)[:, None]).astype(ml_dtypes.bfloat16)
    ident = np.eye(128, dtype=f32)
    cfar = np.ascontiguousarray(np.broadcast_to(rel_bias[31][None, :], (128, 16))).astype(f32)

    in_maps = []
    for c in range(8):
        b, j = divmod(c, 4)
        shift = 3 - j
        xkc = np.zeros((16, 128, 8, 512), f32)
        for m in range(16):
            ms = m - shift
            if ms >= 0:
                xkc[m] = fm(x[b, ms * 512:(ms + 1) * 512, :])
        pTc = np.stack([fm(p[0, b, (4 * pp + j) * 512:(4 * pp + j + 1) * 512, :]) for pp in range(4)])
        gbias = np.zeros((128, 16, 32), f32)
        omask = np.zeros((128, 16, 32), f32)
        n = np.arange(32)
        for pp in range(4):
            for qt in range(4):
                own = 8 * pp + 6 + qt // 2
                valid = (n >= 2 * shift) & (n < own)
                gbias[:, pp * 4 + qt, :] = np.where(valid, f32(0), f32(-1e30))[None, :]
                omask[:, pp * 4 + qt, :] = (n == own).astype(f32)[None, :]
        in_maps.append({
            "xk": xkc, "pT": pTc,
            "w_in": np.asarray(w_in[0], f32), "w_out": np.asarray(w_out[0], f32),
            "w_ff1": np.asarray(w_ff1[0], f32), "w_ff2": np.asarray(w_ff2[0], f32),
            "w_pg": np.asarray(w_ple_gate[0], f32), "w_pp": np.asarray(w_ple_proj[0], f32),
            "gcols": gcols, "lnv": lnv, "bsg": bsg, "wsT": wsT, "trilT": trilT, "strip": strip,
            "onehot": onehot, "ident": ident, "cfar": cfar,
            "gatebias": gbias.reshape(128, 512), "ownmask": omask.reshape(128, 512),
        })
    return in_maps
```

```python
import math
import os
from contextlib import ExitStack
import numpy as np
import concourse.bass as bass
import concourse.mybir as mybir
from concourse.bass_utils import run_bass_kernel_spmd

F32 = mybir.dt.float32
BF16 = mybir.dt.bfloat16
AF = mybir.ActivationFunctionType
ALU = mybir.AluOpType
AX = mybir.AxisListType

NEG = -30000.0
STRIPW = 1792


class _Op:
    __slots__ = ("eng", "fn", "waits", "signal", "count", "dma_sem", "dma_count")

    def __init__(self, eng, fn):
        self.eng = eng
        self.fn = fn
        self.waits = []
        self.signal = False
        self.count = None
        self.dma_sem = None
        self.dma_count = None


class Prog:
    STREAMS = ("pe", "act", "dve", "pool", "sp")

    def __init__(self):
        self.ops = []
        self.stream_ops = {s: [] for s in self.STREAMS}
        self.last_write = {}
        self.readers = {}
        self.dma_groups = {}

    def _tok(self, i):
        op = self.ops[i]
        if op.dma_sem is not None:
            return ("dma:" + op.dma_sem, op.dma_count)
        op.signal = True
        return ("eng:" + op.eng, i)

    MAXOPS = int(os.environ.get("K_MAXOPS", "100000000"))

    def add(self, eng, fn, reads=(), writes=(), dsem=None):
        if len(self.ops) >= self.MAXOPS:
            return None
        op = _Op(eng, fn)
        idx = len(self.ops)
        deps = set()
        for k in reads:
            w = self.last_write.get(k)
            if w is not None:
                deps.add(w)
        for k in writes:
            w = self.last_write.get(k)
            if w is not None:
                deps.add(w)
            for r in self.readers.get(k, ()):
                deps.add(r)
        if dsem is not None:
            c = self.dma_groups.get(dsem, 0) + 16
            self.dma_groups[dsem] = c
            op.dma_sem = dsem
            op.dma_count = c
        self.ops.append(op)
        self.stream_ops[eng].append(idx)
        for d in deps:
            dop = self.ops[d]
            if dop.dma_sem is None and dop.eng == eng and eng in ("pe", "sp"):
                continue
            op.waits.append(self._tok(d))
        for k in reads:
            self.readers.setdefault(k, []).append(idx)
        for k in writes:
            self.last_write[k] = idx
            self.readers[k] = []
        return idx

    def barrier(self, exclude=()):
        toks = []
        for s, lst in self.stream_ops.items():
            for i in reversed(lst):
                if self.ops[i].fn is not None and self.ops[i].dma_sem is None:
                    toks.append(self._tok(i))
                    break
        for name, c in self.dma_groups.items():
            if name in exclude:
                continue
            toks.append(("dma:" + name, c))
        for s in self.STREAMS:
            op = _Op(s, None)
            op.waits = list(toks)
            self.ops.append(op)
            self.stream_ops[s].append(len(self.ops) - 1)
        self.last_write = {}
        self.readers = {}

    def finalize(self):
        cnt = {s: 0 for s in self.STREAMS}
        for op in self.ops:
            if op.dma_sem is None and op.signal:
                cnt[op.eng] += 1
                op.count = cnt[op.eng]

    def sem_keys(self):
        return ["eng:" + s for s in self.STREAMS] + ["dma:" + n for n in self.dma_groups]

    def emit_stream(self, stream, eng, sems):
        waited = {}
        for i in self.stream_ops[stream]:
            op = self.ops[i]
            need = {}
            for (k, v) in op.waits:
                if k.startswith("eng:"):
                    v = self.ops[v].count
                if v > need.get(k, 0):
                    need[k] = v
            for k, v in need.items():
                if waited.get(k, 0) >= v:
                    continue
                if k == "eng:" + stream and stream in ("pe", "sp"):
                    continue
                eng.wait_ge(sems[k], v)
                waited[k] = v
            if op.fn is None:
                continue
            ins = op.fn(eng)
            if op.dma_sem is not None:
                ins.then_inc(sems["dma:" + op.dma_sem], 16)
            elif op.signal:
                ins.then_inc(sems["eng:" + stream], 1)


class Arena:
    def __init__(self, ap, nwords):
        self.ar = ap
        self.n = nwords
        self.off = 0

    def f32(self, cols, parts=128):
        a = self.ar[0:parts, self.off:self.off + cols]
        self.off += cols
        assert self.off <= self.n, (self.off, self.n)
        return a

    def bf16(self, cols, parts=128):
        w = (cols + 1) // 2
        a = self.ar[0:parts, self.off:self.off + w].bitcast(BF16)
        self.off += w
        assert self.off <= self.n, (self.off, self.n)
        return a


def build_program(stop_phase=None, dbg=False):
    nc = bass.Bass("TRN2", target_bir_lowering=False)

    def din(name, shape, dt=F32):
        return nc.dram_tensor(name, shape, dt, kind="ExternalInput").ap()

    def dscr(name, shape, dt=BF16):
        ext = dbg and name in ("Kscr", "Vscr", "Qscr", "Yscr")
        return nc.dram_tensor(name, shape, dt, kind=("ExternalOutput" if ext else "Internal")).ap()

    xk = din("xk", [16, 128, 8, 512])
    pT = din("pT", [4, 128, 2, 512])
    w_in = din("w_in", [1024, 7168])
    w_out = din("w_out", [1024, 1024])
    w_ff1 = din("w_ff1", [1024, 4096])
    w_ff2 = din("w_ff2", [4096, 1024])
    w_pg = din("w_pg", [1024, 1024])
    w_pp = din("w_pp", [256, 1024])
    gcols = din("gcols", [128, 32])
    lnv = din("lnv", [2, 1024])
    bsg = din("bsg", [1, 1024])
    wsT = din("wsT", [128, 1024])
    trilT = din("trilT", [128, 128])
    strip = din("strip", [16, 128, STRIPW])
    onehot = din("onehot", [32, 8192], BF16)
    ident = din("ident", [128, 128])
    cfar = din("cfar", [128, 16])
    gatebias = din("gatebias", [128, 512])
    ownmask = din("ownmask", [128, 512])
    outT = nc.dram_tensor("outT", [4, 128, 8, 512], F32, kind="ExternalOutput").ap()

    Kscr = dscr("Kscr", [8, 128, 8192])
    Vscr = dscr("Vscr", [128, 16, 64, 128])
    Qscr = dscr("Qscr", [16, 96, 2048])
    Yscr = dscr("Yscr", [8, 128, 2048])
    Wa = dscr("Wa", [1024, 4096])
    Wo = dscr("Wo", [1024, 1024])
    W1 = dscr("W1", [1024, 4096])
    W2 = dscr("W2", [4096, 1024])
    Wg = dscr("Wg", [1024, 1024])
    Wp = dscr("Wp", [256, 1024])

    P = Prog()
    es = ExitStack()
    NW = 53000
    arena_t = es.enter_context(nc.sbuf_tensor("arena", [128, NW], F32))
    A = Arena(arena_t, NW)
    ps = []
    psall = es.enter_context(nc.psum_tensor("psall", [128, 4096], F32))
    for i in range(8):
        ps.append((psall[:, i * 512:(i + 1) * 512], "ps%d" % i))
    bank_ctr = [0]

    def nextbank(lo=int(os.environ.get("K_BANKLO", "0")), hi=8):
        n = hi - lo
        b = ps[lo + bank_ctr[0] % n]
        bank_ctr[0] += 1
        return b

    def mm(out, lhsT, rhs, start, stop, reads, writes):
        P.add("pe", lambda e: e.matmul(out, lhsT=lhsT, rhs=rhs, start=start, stop=stop), reads, writes)

    def act(out, in_, func, reads, writes, bias=None, scale=1.0):
        if bias is None:
            P.add("act", lambda e: e.activation(out=out, in_=in_, func=func, scale=scale), reads, writes)
        else:
            P.add("act", lambda e: e.activation(out=out, in_=in_, func=func, bias=bias, scale=scale), reads, writes)

    def tt(eng, out, in0, in1, op, reads, writes):
        P.add(eng, lambda e: e.tensor_tensor(out=out, in0=in0, in1=in1, op=op), reads, writes)

    def ts(eng, out, in0, s1, s2, op0, op1, reads, writes):
        if s2 is None:
            P.add(eng, lambda e: e.tensor_scalar(out=out, in0=in0, scalar1=s1, scalar2=None, op0=op0), reads, writes)
        else:
            P.add(eng, lambda e: e.tensor_scalar(out=out, in0=in0, scalar1=s1, scalar2=s2, op0=op0, op1=op1), reads, writes)

    def stt(eng, out, in0, scalar, in1, op0, op1, reads, writes):
        P.add(eng, lambda e: e.scalar_tensor_tensor(out=out, in0=in0, scalar=scalar, in1=in1, op0=op0, op1=op1), reads, writes)

    def cp(eng, out, in_, reads, writes):
        if eng == "act":
            P.add(eng, lambda e: e.activation(out=out, in_=in_, func=AF.Copy), reads, writes)
        else:
            P.add(eng, lambda e: e.tensor_copy(out=out, in_=in_), reads, writes)

    def dma(q, out, in_, reads, writes, dsem):
        if q == "pool":
            writes = list(writes) + ["swq"]
        P.add(q, lambda e: e.dma_start(out=out, in_=in_), reads, writes, dsem=dsem)

    ones_bf = A.bf16(128)
    ident_bf = A.bf16(128)
    epsc = A.f32(1)
    zeroc = A.f32(1)
    gcols_sb = A.f32(32)
    cfar_sb = A.f32(16)
    base = A.off

    P.add("pool", lambda e: e.memset(ones_bf, 1.0), writes=["ones"])
    P.add("pool", lambda e: e.memset(epsc, 1e-6), writes=["consts"])
    P.add("pool", lambda e: e.memset(zeroc, 0.0), writes=["consts"])
    ident_f = A.f32(128)
    base = A.off
    dma("sp", ident_f, ident[:, :], [], ["ident_f"], "c_ident")
    cp("pool", ident_bf, ident_f, ["ident_f"], ["ident"])
    dma("sp", gcols_sb, gcols[:, :], [], ["gcols"], "c_gcols")
    dma("sp", cfar_sb, cfar[:, :], [], ["cfar"], "c_cfar")

    def rmsnorm(X, xkeys, gi, out, outkeys, sq, sd, rstd):
        act(sq.rearrange("p k t -> p (k t)"), X.rearrange("p k t -> p (k t)"), AF.Square, xkeys, ["sq"])
        b, bk = nextbank()
        for c in range(8):
            mm(b, ones_bf, sq[:, c, :], c == 0, c == 7, ["sq", "ones"], [bk])
        act(sd, b, AF.Sqrt, [bk, "consts"], ["sd"], bias=epsc[:, 0:1], scale=1.0 / 1024.0)
        P.add("dve", lambda e: e.reciprocal(out=rstd, in_=sd), ["sd"], ["rstd"])
        for c in range(8):
            stt("dve", out[:, c, :], X[:, c, :], gcols_sb[:, gi * 8 + c: gi * 8 + c + 1], rstd, ALU.mult, ALU.mult,
                [xkeys[c], "rstd", "gcols"], [outkeys[c]])

    Wqkv = A.bf16(8 * 3072).rearrange("p (k c) -> p k c", k=8)
    Xk = [A.f32(4096).rearrange("p (k t) -> p k t", k=8) for _ in range(2)]
    sq = A.bf16(4096).rearrange("p (k t) -> p k t", k=8)
    hTs = [A.bf16(4096).rearrange("p (k t) -> p k t", k=8) for _ in range(2)]
    Kst = [A.bf16(4096).rearrange("p (k t) -> p k t", k=8) for _ in range(2)]
    Vst = [A.bf16(8192).rearrange("p (j two a d) -> p j two a d", j=8, two=2, a=4) for _ in range(2)]
    Qst = A.bf16(8192, parts=96)
    Qst3 = Qst.rearrange("p (h t) -> p h t", h=16)
    Qst4 = Qst.rearrange("p (g a t) -> p g a t", g=4, a=4)
    sd = A.f32(512)
    rstd = A.f32(512)
    ksum = A.f32(256).rearrange("p (j n) -> p j n", j=8)
    kmeanH = A.bf16(512, parts=64).rearrange("p (j a n) -> p j a n", j=8, a=2)
    gb_sb = A.f32(512).rearrange("p (i n) -> p i n", i=16)
    om_sb = A.f32(512).rearrange("p (i n) -> p i n", i=16)
    gms = [A.f32(512) for _ in range(4)]
    m8 = A.f32(128).rearrange("p (h n) -> p h n", h=16)
    thr = A.f32(16)
    sel = A.f32(512)
    sel3 = sel.rearrange("p (h n) -> p h n", h=16)
    selbs = [A.bf16(512) for _ in range(4)]
    if os.environ.get("K_TRACE"): print("phase1 arena words", A.off)

    w_in_v = w_in.rearrange("(k p) c -> p k c", p=128)
    wstg = [A.f32(1024) for _ in range(2)]
    n_ = 0
    for i, nm in (1, "wk"), (2, "wv"), (0, "wq"):
        for k8 in range(8):
            st_ = n_ % 2
            n_ += 1
            dma("sp", wstg[st_], w_in_v[:, k8, 2048 + i * 1024: 2048 + (i + 1) * 1024], [], ["wstg%d" % st_], "c_wstg%d" % st_)
            cp("pool" if k8 % 2 else "act", Wqkv[:, k8, i * 1024:(i + 1) * 1024], wstg[st_], ["wstg%d" % st_], [nm + str(k8)])
    dma("sp", gb_sb.rearrange("p i n -> p (i n)"), gatebias[:, :], [], ["gconst"], "c_gb")
    dma("sp", om_sb.rearrange("p i n -> p (i n)"), ownmask[:, :], [], ["gconst2"], "c_om")
    P.add("pool", lambda e: e.memset(ksum.rearrange("p j n -> p (j n)"), 0.0), writes=["ksum%d" % j for j in range(8)])
    for s_ in range(2):
        P.add("pool", (lambda o: (lambda e: e.memset(o, 1.0)))(Vst[s_].rearrange("p j two a d -> p (j two a d)")),
              writes=["Vst%d_%d_%d" % (s_, a, h) for a in range(4) for h in range(2)])

    def wcast(dst, src, rows, cols, rblk):
        for r0 in range(0, rows, rblk):
            dma("pool", dst[r0:r0 + rblk, :], src[r0:r0 + rblk, :], [], [], "wcast")

    def xkeys_of(s):
        return ["Xk%d_%d" % (s, c) for c in range(8)]

    def hkeys_of(s):
        return ["hT%d_%d" % (s, c) for c in range(8)]

    NT1 = int(os.environ.get('K_NT1', '16'))
    for m0 in range(min(2, NT1)):
        dma("sp", Xk[m0].rearrange("p k t -> p (k t)"), xk[m0].rearrange("p k t -> p (k t)"), [], xkeys_of(m0), "xk%d" % m0)
    wcast_done = False
    rmsnorm(Xk[0], xkeys_of(0), 0, hTs[0], hkeys_of(0), sq, sd, rstd)
    for m in range(NT1):
        s = m % 2
        if False:
            wcast_done = True
            wcast(Wa[:, 0:2048], w_in[:, 0:2048], 1024, 2048, 512)
            wcast(Wa[:, 2048:4096], w_in[:, 5120:7168], 1024, 2048, 512)
            wcast(Wo, w_out, 1024, 1024, 1024)
            wcast(W1, w_ff1, 1024, 4096, 256)
            wcast(W2, w_ff2, 4096, 1024, 1024)
            wcast(Wg, w_pg, 1024, 1024, 1024)
            wcast(Wp, w_pp, 256, 1024, 256)
        hT = hTs[s]
        hk = hkeys_of(s)
        own = (m % 4 == 3)
        p = m // 4
        for j in range(8):
            b, bk = nextbank()
            for c in range(8):
                mm(b, Wqkv[:, c, 1024 + j * 128: 1024 + (j + 1) * 128], hT[:, c, :], c == 0, c == 7, [hk[c], "wk%d" % c], [bk])
            act(Kst[s][:, j, :], b, AF.Copy, [bk], ["Kst%d_%d" % (s, j)])
            P.add("dve", (lambda o, i: (lambda e: e.tensor_reduce(out=o, in_=i, axis=AX.X, op=ALU.add)))(
                ksum[:, j, 2 * m:2 * m + 2], Kst[s][:, j, :].rearrange("p (a t) -> p a t", a=2)), ["Kst%d_%d" % (s, j)], ["ksum%d" % j])
        if m + 1 < NT1:
            s1 = (m + 1) % 2
            rmsnorm(Xk[s1], xkeys_of(s1), 0, hTs[s1], hkeys_of(s1), sq, sd, rstd)
        if m + 2 < NT1:
            dma("sp", Xk[s].rearrange("p k t -> p (k t)"), xk[m + 2].rearrange("p k t -> p (k t)"), [], xkeys_of(s), "xk%d" % s)
        def emit_V(ev):
            for t4 in range(4):
                for half in range(2):
                    b, bk = nextbank()
                    for c in range(8):
                        mm(b, hT[:, c, t4 * 128:(t4 + 1) * 128], Wqkv[:, c, 2048 + half * 512: 2048 + (half + 1) * 512],
                           c == 0, c == 7, [hk[c], "wv%d" % c], [bk])
                    b4 = b.rearrange("p (j two d) -> p j two d", j=4, two=2)
                    cp(ev, Vst[s][:, half * 4:(half + 1) * 4, 0, t4, 0:64], b4[:, :, 0, :], [bk], ["Vst%d_%d_%d" % (s, t4, half)])
                    cp(ev, Vst[s][:, half * 4:(half + 1) * 4, 1, t4, 64:128], b4[:, :, 1, :], [bk], ["Vst%d_%d_%d" % (s, t4, half)])

        if not own:
            emit_V("dve")
        else:
            for j in range(8):
                b, bk = nextbank()
                for c in range(8):
                    mm(b, Wqkv[:, c, j * 128:(j + 1) * 128], hT[:, c, :], c == 0, c == 7, [hk[c], "wq%d" % c], [bk])
                act(Qst3[0:64, 2 * j, :], b[0:64, :], AF.Copy, [bk], ["Qst_%d" % (2 * j)], scale=0.125)
                act(Qst3[0:64, 2 * j + 1, :], b[64:128, :], AF.Copy, [bk], ["Qst_%d" % (2 * j + 1)], scale=0.125)
            cp("dve", kmeanH[:, :, 0, :], ksum[0:64, :, :], ["ksum%d" % j for j in range(8)], ["kmeanH0"])
            cp("dve", kmeanH[:, :, 1, :], ksum[64:128, :, :], ["ksum%d" % j for j in range(8)], ["kmeanH1"])
            for qt in range(4):
                idx = p * 4 + qt
                b, bk = nextbank()
                for h in range(16):
                    mm(b[:, h * 32:(h + 1) * 32], Qst3[0:64, h, qt * 128:(qt + 1) * 128], kmeanH[0:64, h // 2, h % 2, :],
                       True, True, ["Qst_%d" % h, "kmeanH%d" % (h % 2)], [bk])
                tt("dve", gms[qt].rearrange("p (h n) -> p h n", h=16), b.rearrange("p (h n) -> p h n", h=16),
                   gb_sb[:, idx, :].unsqueeze(1).to_broadcast([128, 16, 32]), ALU.add, [bk, "gconst"], ["gm%d" % qt])
            emit_V("act")
            for qt in range(4):
                idx = p * 4 + qt
                gm3 = gms[qt].rearrange("p (h n) -> p h n", h=16)
                gk = "gm%d" % qt
                for h in range(16):
                    P.add("dve", (lambda o, i: (lambda e: e.max(out=o, in_=i)))(m8[:, h, :], gm3[:, h, :]), [gk], ["m8_%d" % h])
                ts("dve", thr.unsqueeze(2), m8[:, :, 2:3], -1e29, None, ALU.max, None, ["m8_%d" % h for h in range(16)], ["thr"])
                tt("dve", sel3, gm3, thr.unsqueeze(2).to_broadcast([128, 16, 32]), ALU.is_ge, [gk, "thr"], ["sel"])
                tt("dve", sel3, sel3, om_sb[:, idx, :].unsqueeze(1).to_broadcast([128, 16, 32]), ALU.add, ["sel", "gconst2"], ["sel"])
                ts("dve", selbs[qt], sel, -NEG, NEG, ALU.mult, ALU.add, ["sel"], ["selb%d" % qt])
            for qt in range(4):
                b2, bk2 = nextbank()
                for g in range(4):
                    mm(b2[:, g * 128:(g + 1) * 128], selbs[qt][:, g * 128:(g + 1) * 128], ident_bf, True, True, ["selb%d" % qt, "ident"], [bk2])
                for hp in range(4):
                    act(Qst4[64:96, :, hp, qt * 128:(qt + 1) * 128], b2[32 * hp:32 * hp + 32, :].rearrange("p (g q) -> p g q", g=4),
                        AF.Copy, [bk2], ["Qsel_%d" % hp])
        if os.environ.get("K_TRACE"): print("tile", m, "after V/Q/gate", len(P.ops))
        if os.environ.get('K_SKIP_ST'):
            continue
        dma("sp", Kscr[:, :, m * 512:(m + 1) * 512].rearrange("j p t -> p j t"), Kst[s],
            ["Kst%d_%d" % (s, j) for j in range(8)], [], "kst%d" % s)
        dma("sp", Vscr[:, :, 4 * m:4 * m + 4, :], Vst[s].rearrange("p j two a d -> p (j two) a d"),
            ["Vst%d_%d_%d" % (s, a, h) for a in range(4) for h in range(2)], [], "vst%d" % s)
        if own:
            dma("sp", Qscr[:, :, p * 512:(p + 1) * 512].rearrange("h r t -> r h t"), Qst3,
                ["Qst_%d" % h for h in range(16)] + ["Qsel_%d" % h for h in range(4)], [], "qst")

    P.barrier(exclude=("wcast",))
    if stop_phase == 1:
        P.barrier()
        return _finish(nc, P, es)

    A.off = base
    KK = [[A.bf16(2048, parts=96) for c in range(5)] for hd in range(2)]
    VV = [[A.bf16(2048).rearrange("p (k d) -> p k d", k=16) for c in range(5)] for hd in range(2)]
    QQ = [[A.bf16(2048, parts=96) for sl in range(2)] for hd in range(2)]
    stripf = [A.f32(STRIPW) for hd in range(2)]
    stripb = [[A.bf16(STRIPW) for sl in range(2)] for hd in range(2)]
    NPT = 6
    PT = [A.bf16(1024) for _ in range(NPT)]
    yst = [A.bf16(512) for _ in range(2)]
    Rr = A.f32(512)
    Ocp = [A.f32(512) for _ in range(2)]

    for c in range(5):
        cc = c % 4
        for hd in range(2):
            dma("sp" if hd == 0 else "act", KK[hd][c][64:96, :], onehot[:, cc * 2048:(cc + 1) * 2048], [], ["KOH%d%d" % (hd, c)], "oh%d%d" % (hd, c))
    wcf = [A.f32(4096) for _ in range(2)]
    wcb = [A.bf16(4096) for _ in range(2)]
    wpieces = []
    for r in range(8):
        wpieces.append([(Wa[r * 128:(r + 1) * 128, hf * 2048:(hf + 1) * 2048], w_in[r * 128:(r + 1) * 128, c0:c0 + 2048], hf * 2048, 2048, None)
                        for hf, c0 in ((0, 0), (1, 5120))])
    for r in range(8):
        wpieces.append([(W1[r * 128:(r + 1) * 128, :], w_ff1[r * 128:(r + 1) * 128, :], 0, 4096, None)])
    for r in range(8):
        wpieces.append([(W2[r * 512:(r + 1) * 512, :], w_ff2[r * 512:(r + 1) * 512, :], 0, 4096, 4)])
    for r in range(2):
        wpieces.append([(Wo[r * 512:(r + 1) * 512, :], w_out[r * 512:(r + 1) * 512, :], 0, 4096, 4)])
        wpieces.append([(Wg[r * 512:(r + 1) * 512, :], w_pg[r * 512:(r + 1) * 512, :], 0, 4096, 4)])
    wpieces.append([(Wp[:, :], w_pp[:, :], 0, 2048, 2)])
    wp_ctr = [0]
    cur_it = [0]

    def emit_wpiece():
        if wp_ctr[0] >= len(wpieces):
            return
        st_ = wp_ctr[0] % 2
        parts_ = wpieces[wp_ctr[0]]
        wp_ctr[0] += 1
        ntot = 0
        outs_ = []
        for (dst, src, c0, n, a) in parts_:
            if a is not None:
                dst = dst.rearrange("(a p) c -> p a c", p=128)
                src = src.rearrange("(a p) c -> p a c", p=128)
                f_ = wcf[st_][:, c0:c0 + n].rearrange("p (a c) -> p a c", a=a)
                b_ = wcb[st_][:, c0:c0 + n].rearrange("p (a c) -> p a c", a=a)
            else:
                f_ = wcf[st_][:, c0:c0 + n]
                b_ = wcb[st_][:, c0:c0 + n]
            dma("sp", f_, src, [], ["wcf%d" % st_], "wcf%d" % st_)
            outs_.append((dst, b_))
            ntot = max(ntot, c0 + n)
        hn = ntot // 2
        cp("dve", wcb[st_][:, 0:hn], wcf[st_][:, 0:hn], ["wcf%d" % st_], ["wcb%da" % st_])
        cp("dve", wcb[st_][:, hn:ntot], wcf[st_][:, hn:ntot], ["wcf%d" % st_], ["wcb%db" % st_])
        for (dst, b_) in outs_:
            deferred.append((cur_it[0] + 1, (lambda dst=dst, b_=b_, st_=st_: dma("act", dst, b_, ["wcb%da" % st_, "wcb%db" % st_], [], "wcb%d" % st_))))

    deferred = []

    def flush_deferred(force=False):
        keep = []
        while deferred:
            rdy, fn_ = deferred.pop(0)
            if force or rdy <= cur_it[0]:
                fn_()
            else:
                keep.append((rdy, fn_))
        deferred.extend(keep)

    pt_ctr = [0]
    sp_ctr = [0]
    for j in range(8):
        sl = j % 2
        for hd in range(2):
            h = 2 * j + hd
            dma("sp", QQ[hd][sl][0:96, :], Qscr[h], [], ["Q%d%d" % (hd, sl)], "q%d%d" % (hd, sl))
            dma("sp", stripf[hd], strip[h], [], ["stripf%d" % hd], "sf%d" % hd)
            cp("dve", stripb[hd][sl], stripf[hd], ["stripf%d" % hd], ["stripb%d%d" % (hd, sl)])
        cslot = [4 if (j % 2) else 0, 1, 2, 3]
        for c in range(4):
            cs = cslot[c]
            for hd in range(2):
                dma("sp", KK[hd][cs][0:64, :], Kscr[j, 64 * hd:64 * hd + 64, c * 2048:(c + 1) * 2048], [],
                    ["K%d%d" % (hd, cs)], "k%d%d" % (hd, cs))
                dma("sp", VV[hd][cs], Vscr[:, 2 * j + hd, 16 * c:16 * c + 16, :], [],
                    ["V%d%d" % (hd, cs)], "v%d%d" % (hd, cs))
        for p in range(4):
            it = j * 4 + p
            cur_it[0] = it
            emit_wpiece()
            Ob = [ps[0], ps[1]]
            ng = 16 * (p + 1)
            LAG = 1
            pend = []

            nfar = 16 * p + 5
            groups = [[2 * i, 2 * i + 1] for i in range(nfar // 2)] + [[nfar - 1]]
            groups += [[g] for g in range(nfar, ng - 8)] + [[g] for g in range(ng - 4, ng)] + [[g] for g in range(ng - 8, ng - 4)]
            first_g = groups[0][0]
            last_g = groups[-1][-1]

            def tinfo(g, p=p, cslot=cslot):
                c, t = divmod(g, 16)
                d = 4 * (4 * p + 3) - g
                return cslot[c], t, d, (128 * (-d) if d < 0 else 0)

            def emit_pv(hd, grp, pts, Ob=Ob, first_g=first_g, last_g=last_g):
                for idx, g in enumerate(grp):
                    c, t, d, off = tinfo(g)
                    mm(Ob[hd][0][:, off:512], VV[hd][c][:, t, :], PT[pts][:, idx * 512 + off:(idx + 1) * 512], g == first_g, g == last_g,
                       ["V%d%d" % (hd, c), "PT%d" % pts], [Ob[hd][1]])

            for gi, grp in enumerate(groups):
                if gi == 6:
                    flush_deferred()
                for hd in range(2):
                    h = 2 * j + hd
                    b0 = 2 + 2 * (sp_ctr[0] % 3)
                    sp_ctr[0] += 1
                    bkeys = []
                    near = False
                    for idx, g in enumerate(grp):
                        c, t, d, off = tinfo(g)
                        near = d <= 7
                        sb_, sbk = ps[b0 + idx]
                        bkeys.append(sbk)
                        mm(sb_[:, off:512], KK[hd][c][0:96, t * 128:(t + 1) * 128], QQ[hd][sl][0:96, p * 512 + off:(p + 1) * 512], True, not near,
                           ["K%d%d" % (hd, c), "KOH%d%d" % (hd, c), "Q%d%d" % (hd, sl)], [sbk])
                        if near:
                            mm(sb_[:, off:512], ident_bf, stripb[hd][sl][:, 128 * (d + 3) + off:128 * (d + 3) + 512], False, True,
                               ["stripb%d%d" % (hd, sl), "ident"], [sbk])
                    pts = pt_ctr[0] % NPT
                    pt_ctr[0] += 1
                    if len(grp) == 2:
                        act(PT[pts][:, 0:1024], psall[:, b0 * 512:(b0 + 2) * 512], AF.Exp, bkeys + ["consts", "cfar"], ["PT%d" % pts],
                            bias=cfar_sb[:, h:h + 1])
                    else:
                        bias = zeroc[:, 0:1] if near else cfar_sb[:, h:h + 1]
                        act(PT[pts][:, off:512], ps[b0][0][:, off:512], AF.Exp, bkeys + ["consts", "cfar"], ["PT%d" % pts], bias=bias)
                    pend.append((hd, gi, grp, pts))
                while pend and pend[0][1] <= gi - LAG:
                    hd_, gi_, grp_, pts_ = pend.pop(0)
                    emit_pv(hd_, grp_, pts_)
            while pend:
                hd_, gi_, grp_, pts_ = pend.pop(0)
                emit_pv(hd_, grp_, pts_)
            ys = it % 2
            cp("dve", Ocp[0], Ob[0][0], [Ob[0][1]], ["Ocp0"])
            cp("dve", Ocp[1], Ob[1][0], [Ob[1][1]], ["Ocp1"])
            P.add("dve", (lambda o, i: (lambda e: e.reciprocal(out=o, in_=i)))(Rr[0:64, :], Ocp[0][64:128, :]), ["Ocp0"], ["R0"])
            P.add("dve", (lambda o, i: (lambda e: e.reciprocal(out=o, in_=i)))(Rr[64:128, :], Ocp[1][0:64, :]), ["Ocp1"], ["R1"])
            tt("dve", yst[ys][0:64, :], Ocp[0][0:64, :], Rr[0:64, :], ALU.mult, ["Ocp0", "R0"], ["yst%da" % ys])
            tt("dve", yst[ys][64:128, :], Ocp[1][64:128, :], Rr[64:128, :], ALU.mult, ["Ocp1", "R1"], ["yst%db" % ys])
            deferred.append((it + 1, (lambda j=j, p=p, ys=ys: dma("act", Yscr[j, :, p * 512:(p + 1) * 512], yst[ys],
                                                                   ["yst%da" % ys, "yst%db" % ys], [], "yst%d" % ys))))

    while wp_ctr[0] < len(wpieces):
        emit_wpiece()
    flush_deferred(force=True)
    P.barrier()
    if stop_phase == 3:
        return _finish(nc, P, es)

    A.off = base
    NS = 5
    slots = [A.bf16(4096).rearrange("p (k c) -> p k c", k=8) for _ in range(NS)]
    hT = A.bf16(4096).rearrange("p (k t) -> p k t", k=8)
    sq = A.bf16(4096).rearrange("p (k t) -> p k t", k=8)
    yb = A.bf16(4096).rearrange("p (k t) -> p k t", k=8)
    vn = A.bf16(4096).rearrange("p (a c) -> p a c", a=4)
    merged = A.bf16(4096).rearrange("p (k t) -> p k t", k=8)
    aT = A.bf16(16384).rearrange("p (k t) -> p k t", k=32)
    pTb = A.bf16(1024).rearrange("p (k t) -> p k t", k=2)
    WsTb = A.bf16(1024).rearrange("p (g t) -> p g t", g=8)
    X = A.f32(4096).rearrange("p (k t) -> p k t", k=8)
    M = A.f32(4096).rearrange("p (k t) -> p k t", k=8)
    vg = [A.f32(1024) for _ in range(2)]
    vc = [A.f32(1024) for _ in range(2)]
    sqv = A.f32(1024)
    lng = A.f32(1024)
    lnb = A.f32(1024)
    bs_sb = A.f32(1024).rearrange("p (g t) -> p g t", g=8)
    NTMP = 6
    tmps = [A.f32(512) for _ in range(NTMP)]
    sd = A.f32(512)
    rstd = A.f32(512)
    wsf = A.f32(1024).rearrange("p (g t) -> p g t", g=8)
    trl = A.f32(128)
    pTf = A.f32(1024)
    s1 = A.f32(8)
    nmn = A.f32(8)
    s2 = A.f32(8)
    sdv = A.f32(8)
    rv = A.f32(8)

    dma("sp", lng, lnv[0:1, :].partition_broadcast(128), [], ["lng"], "c_lng")
    dma("sp", lnb, lnv[1:2, :].partition_broadcast(128), [], ["lnb"], "c_lnb")
    dma("sp", bs_sb.rearrange("p g t -> p (g t)"), bsg[0:1, :].partition_broadcast(128), [], ["bs"], "c_bs")
    dma("sp", wsf.rearrange("p g t -> p (g t)"), wsT[:, :], [], ["wsf"], "c_wsf")
    dma("sp", trl, trilT[:, :], [], ["trl"], "c_trl")
    tt("dve", WsTb, wsf, trl.unsqueeze(1).to_broadcast([128, 8, 128]), ALU.mult, ["wsf", "trl"], ["WsT"])

    slot_ctr = [0]

    def load_piece(src, nk):
        s = slot_ctr[0] % NS
        slot_ctr[0] += 1
        dma("sp", slots[s][:, 0:nk, :], src.rearrange("(k p) c -> p k c", p=128), [], ["ws%d" % s], "ws%d" % s)
        return slots[s], "ws%d" % s

    tmp_ctr = [0]

    def nexttmp():
        i = tmp_ctr[0] % NTMP
        tmp_ctr[0] += 1
        return tmps[i], "tmp%d" % i

    Xk_ = ["X_%d" % c for c in range(8)]
    hk = ["hT_%d" % c for c in range(8)]
    Mk = ["M_%d" % c for c in range(8)]
    mgk = ["mg_%d" % c for c in range(8)]

    def proj_fm(piece, pkey, nk, rhs3, rhs_keys, evac):
        for o in range(4):
            b, bk = nextbank()
            for k in range(nk):
                mm(b, piece[:, k, o * 128:(o + 1) * 128], rhs3[:, k, :], k == 0, k == nk - 1, [pkey, rhs_keys[k]], [bk])
            evac(o, b, bk)

    for p in range(4):
        dma("sp", X.rearrange("p k t -> p (k t)"), xk[4 * p + 3].rearrange("p k t -> p (k t)"), [], Xk_, "x4")
        dma("sp", yb, Yscr[:, :, p * 512:(p + 1) * 512].rearrange("j p t -> p j t"), [], ["yb"], "yb")
        dma("sp", pTf, pT[p].rearrange("p k t -> p (k t)"), [], ["pTf"], "ptf")
        cp("pool", pTb.rearrange("p k t -> p (k t)"), pTf, ["pTf"], ["pTb"])
        rmsnorm(X, Xk_, 0, hT, hk, sq, sd, rstd)
        pv = [load_piece(Wa[:, 1024 + half * 512: 1024 + (half + 1) * 512], 8) for half in range(2)]
        for t4 in range(4):
            vgi = vg[t4 % 2]
            vci = vc[t4 % 2]
            vgk = "vg%d" % (t4 % 2)
            vck = "vc%d" % (t4 % 2)
            for half in range(2):
                b, bk = nextbank()
                for c in range(8):
                    mm(b, hT[:, c, t4 * 128:(t4 + 1) * 128], pv[half][0][:, c, :], c == 0, c == 7, [hk[c], pv[half][1]], [bk])
                act(vgi[:, half * 512:(half + 1) * 512], b, AF.Gelu_apprx_tanh, [bk], [vgk + "_%d" % half])
            vgks = [vgk + "_0", vgk + "_1"]
            P.add("dve", (lambda o, i: (lambda e: e.reduce_sum(out=o, in_=i, axis=AX.X)))(s1[:, t4:t4 + 1], vgi), vgks, ["s1"])
            ts("dve", nmn[:, t4:t4 + 1], s1[:, t4:t4 + 1], -1.0 / 1024.0, None, ALU.mult, None, ["s1"], ["nmn"])
            act(vci, vgi, AF.Identity, vgks + ["nmn"], [vck], bias=nmn[:, t4:t4 + 1])
            tt("dve", sqv, vci, vci, ALU.mult, [vck], ["sqv"])
            P.add("dve", (lambda o, i: (lambda e: e.reduce_sum(out=o, in_=i, axis=AX.X)))(s2[:, t4:t4 + 1], sqv), ["sqv"], ["s2"])
            act(sdv[:, t4:t4 + 1], s2[:, t4:t4 + 1], AF.Sqrt, ["s2", "consts"], ["sdv"], bias=epsc[:, 0:1], scale=1.0 / 1024.0)
            P.add("dve", (lambda o, i: (lambda e: e.reciprocal(out=o, in_=i)))(rv[:, t4:t4 + 1], sdv[:, t4:t4 + 1]), ["sdv"], ["rv"])
            stt("dve", vci, vci, rv[:, t4:t4 + 1], lng, ALU.mult, ALU.mult, [vck, "rv", "lng"], [vck])
            tt("dve", vn[:, t4, :], vci, lnb, ALU.add, [vck, "lnb"], ["vn%d" % t4])
        for half in range(2):
            pc, pk = load_piece(Wa[:, half * 512:(half + 1) * 512], 8)

            def ev_u(o, b, bk, half=half):
                g = half * 4 + o
                act(M[:, g, :], b, AF.Gelu_apprx_tanh, [bk], [Mk[g]])
                pass
            proj_fm(pc, pk, 8, hT, hk, ev_u)
        for half in range(2):
            pc, pk = load_piece(Wa[:, 2048 + half * 512: 2048 + (half + 1) * 512], 8)

            def ev_ga(o, b, bk, half=half):
                g = half * 4 + o
                tm, tk = nexttmp()
                act(tm, b, AF.Sigmoid, [bk], [tk])
                tt("dve", M[:, g, :], M[:, g, :], tm, ALU.mult, [Mk[g], tk], [Mk[g]])
            proj_fm(pc, pk, 8, hT, hk, ev_ga)
        for g in range(8):
            b, bk = nextbank()
            for t4 in range(4):
                mm(b[:, t4 * 128:(t4 + 1) * 128], vn[:, t4, g * 128:(g + 1) * 128], WsTb[:, g, :], True, True,
                   ["vn%d" % t4, "WsT"], [bk])
            tm, tk = nexttmp()
            tt("dve", tm.rearrange("p (a t) -> p a t", a=4), b.rearrange("p (a t) -> p a t", a=4),
               bs_sb[:, g, :].unsqueeze(1).to_broadcast([128, 4, 128]), ALU.add, [bk, "bs"], [tk])
            tt("dve", M[:, g, :], M[:, g, :], tm, ALU.mult, [Mk[g], tk], [Mk[g]])
        for half in range(2):
            pc, pk = load_piece(Wa[:, 3072 + half * 512: 3072 + (half + 1) * 512], 8)

            def ev_gb(o, b, bk, half=half):
                g = half * 4 + o
                tm, tk = nexttmp()
                act(tm, b, AF.Sigmoid, [bk], [tk])
                tt("pool", tm, tm, yb[:, g, :], ALU.mult, [tk, "yb"], [tk])
                tt("dve", merged[:, g, :], M[:, g, :], tm, ALU.add, [Mk[g], tk], [mgk[g]])
            proj_fm(pc, pk, 8, hT, hk, ev_gb)
        for half in range(2):
            pc, pk = load_piece(Wo[:, half * 512:(half + 1) * 512], 8)

            def ev_o(o, b, bk, half=half):
                g = half * 4 + o
                tt("dve", X[:, g, :], X[:, g, :], b, ALU.add, [Xk_[g], bk], [Xk_[g]])
            proj_fm(pc, pk, 8, merged, mgk, ev_o)
        rmsnorm(X, Xk_, 1, hT, hk, sq, sd, rstd)
        for q in range(8):
            pc, pk = load_piece(W1[:, q * 512:(q + 1) * 512], 8)

            def ev_f1(o, b, bk, q=q):
                fc = q * 4 + o
                tm, tk = nexttmp()
                act(tm, b, AF.Relu, [bk], [tk])
                tt("pool" if (o % 2) else "dve", aT[:, fc, :], tm, tm, ALU.mult, [tk], ["aT_%d" % fc])
            proj_fm(pc, pk, 8, hT, hk, ev_f1)
        for i in range(4):
            for half in range(2):
                pc, pk = load_piece(W2[i * 1024:(i + 1) * 1024, half * 512:(half + 1) * 512], 8)
                for o in range(4):
                    b, bk = ps[half * 4 + o]
                    for k in range(8):
                        mm(b, pc[:, k, o * 128:(o + 1) * 128], aT[:, i * 8 + k, :], (i == 0 and k == 0), (i == 3 and k == 7),
                           [pk, "aT_%d" % (i * 8 + k)], [bk])
        for oc in range(8):
            b, bk = ps[oc]
            tt("dve", X[:, oc, :], X[:, oc, :], b, ALU.add, [Xk_[oc], bk], [Xk_[oc]])
        bank_ctr[0] = 0
        rmsnorm(X, Xk_, 2, hT, hk, sq, sd, rstd)
        for half in range(2):
            pg_, pgk = load_piece(Wg[:, half * 512:(half + 1) * 512], 8)
            pp_, ppk = load_piece(Wp[:, half * 512:(half + 1) * 512], 2)
            for o in range(4):
                g = half * 4 + o
                b, bk = nextbank()
                for k in range(8):
                    mm(b, pg_[:, k, o * 128:(o + 1) * 128], hT[:, k, :], k == 0, k == 7, [pgk, hk[k]], [bk])
                tm, tk = nexttmp()
                act(tm, b, AF.Sigmoid, [bk], [tk])
                b2, bk2 = nextbank()
                for k in range(2):
                    mm(b2, pp_[:, k, o * 128:(o + 1) * 128], pTb[:, k, :], k == 0, k == 1, [ppk, "pTb"], [bk2])
                tt("dve", tm, tm, b2, ALU.mult, [tk, bk2], [tk])
                tt("dve", M[:, g, :], X[:, g, :], tm, ALU.add, [Xk_[g], tk], [Mk[g]])
        rmsnorm(M, Mk, 3, M, Mk, sq, sd, rstd)
        dma("act", outT[p].rearrange("p k t -> p (k t)"), M.rearrange("p k t -> p (k t)"), Mk, [], "outst")

    P.barrier()
    return _finish(nc, P, es)


def _finish(nc, P, es):
    P.finalize()
    if os.environ.get("K_TRACE"):
        cnt = {}
        for op in P.ops:
            if op.count is not None:
                cnt[op.eng] = max(cnt.get(op.eng, 0), op.count)
        print("sem counts", cnt, "dma", P.dma_groups, "nops", {k: len(v) for k, v in P.stream_ops.items()})

    sems = {}
    for k in P.sem_keys():
        sems[k] = es.enter_context(nc.semaphore(k.replace(":", "_")))
    block = es.enter_context(nc.Block())

    @block.sync
    def _(e):
        P.emit_stream("sp", e, sems)

    @block.vector
    def _(e):
        P.emit_stream("dve", e, sems)

    @block.scalar
    def _(e):
        P.emit_stream("act", e, sems)

    @block.gpsimd
    def _(e):
        P.emit_stream("pool", e, sems)

    @block.tensor
    def _(e):
        P.emit_stream("pe", e, sems)

    es.close()
    return nc


def _rel_bucket_np(dist):
    n = np.maximum(dist, 0)
    nf = np.maximum(n, 16).astype(np.float32)
    large = 16 + (np.log(nf / np.float32(16)) / np.float32(math.log(1024 / 16)) * np.float32(16)).astype(np.int32)
    large = np.minimum(large, 31)
    return np.where(n < 16, n, large)


_NC_CACHE = {}


def kernel(x, p, norm_mix_g, w_in, w_sgu_spatial, b_sgu_spatial, ln_v_g, ln_v_b, rel_bias,
           w_out, norm_ffn_g, w_ff1, w_ff2, norm_ple_g, w_ple_gate, w_ple_proj, norm_final_g):
    f32 = np.float32
    x = np.asarray(x, f32)
    p = np.asarray(p, f32)
    rel_bias = np.asarray(rel_bias, f32)
    in_maps = make_inputs(x, p, norm_mix_g, w_in, w_sgu_spatial, b_sgu_spatial, ln_v_g, ln_v_b, rel_bias,
                          w_out, norm_ffn_g, w_ff1, w_ff2, norm_ple_g, w_ple_gate, w_ple_proj, norm_final_g)
    if "nc" not in _NC_CACHE:
        _NC_CACHE["nc"] = build_program()
    nc = _NC_CACHE["nc"]
    res = run_bass_kernel_spmd(nc, in_maps, core_ids=list(range(8)))
    out = np.empty((2, 8192, 1024), f32)
    for c in range(8):
        b, j = divmod(c, 4)
        oT = res.results[c]["outT"]
        for pp in range(4):
            m = 4 * pp + j
            out[b, m * 512:(m + 1) * 512, :] = oT[pp].transpose(2, 1, 0).reshape(512, 1024)
    return out


def make_inputs(x, p, norm_mix_g, w_in, w_sgu_spatial, b_sgu_spatial, ln_v_g, ln_v_b, rel_bias,
                w_out, norm_ffn_g, w_ff1, w_ff2, norm_ple_g, w_ple_gate, w_ple_proj, norm_final_g):
    f32 = np.float32
    x = np.asarray(x, f32)
    p = np.asarray(p, f32)
    rel_bias = np.asarray(rel_bias, f32)

    def fm(a):
        t, F = a.shape
        return np.ascontiguousarray(a.T.reshape(F // 128, 128, t).transpose(1, 0, 2))

    gcols = np.zeros((128, 32), f32)
    for i, g in enumerate((norm_mix_g[0], norm_ffn_g[0], norm_ple_g[0], norm_final_g)):
        gcols[:, i * 8:(i + 1) * 8] = np.asarray(g, f32).reshape(8, 128).T
    lnv = np.stack([np.asarray(ln_v_g[0], f32), np.asarray(ln_v_b[0], f32)])
    bsg = np.ascontiguousarray(np.asarray(b_sgu_spatial[0], f32).reshape(1, 1024))
    wsT = np.ascontiguousarray(np.asarray(w_sgu_spatial[0], f32).transpose(2, 0, 1).reshape(128, 1024))
    trilT = (np.arange(128)[:, None] <= np.arange(128)[None, :]).astype(f32)
    ki = np.arange(128)[:, None]
    mm_ = np.arange(STRIPW)[None, :]
    dist = mm_ - 384 - ki
    bidx = _rel_bucket_np(dist)
    strip = np.empty((16, 128, STRIPW), f32)
    for h in range(16):
        strip[h] = np.where(dist >= 0, rel_bias[bidx, h], f32(NEG))
    import ml_dtypes
    onehot = (np.arange(8192)[None, :] // 256 == np.arange(32)[:, None]).astype(ml_dtypes.bfloat16)
    ident = np.eye(128, dtype=f32)
    cfar = np.ascontiguousarray(np.broadcast_to(rel_bias[31][None, :], (128, 16))).astype(f32)

    in_maps = []
    for c in range(8):
        b, j = divmod(c, 4)
        shift = 3 - j
        xkc = np.zeros((16, 128, 8, 512), f32)
        for m in range(16):
            ms = m - shift
            if ms >= 0:
                xkc[m] = fm(x[b, ms * 512:(ms + 1) * 512, :])
        pTc = np.stack([fm(p[0, b, (4 * pp + j) * 512:(4 * pp + j + 1) * 512, :]) for pp in range(4)])
        gbias = np.zeros((128, 16, 32), f32)
        omask = np.zeros((128, 16, 32), f32)
        n = np.arange(32)
        for pp in range(4):
            for qt in range(4):
                own = 8 * pp + 6 + qt // 2
                valid = (n >= 2 * shift) & (n < own)
                gbias[:, pp * 4 + qt, :] = np.where(valid, f32(0), f32(-1e30))[None, :]
                omask[:, pp * 4 + qt, :] = (n == own).astype(f32)[None, :]
        in_maps.append({
            "xk": xkc, "pT": pTc,
            "w_in": np.asarray(w_in[0], f32), "w_out": np.asarray(w_out[0], f32),
            "w_ff1": np.asarray(w_ff1[0], f32), "w_ff2": np.asarray(w_ff2[0], f32),
            "w_pg": np.asarray(w_ple_gate[0], f32), "w_pp": np.asarray(w_ple_proj[0], f32),
            "gcols": gcols, "lnv": lnv, "bsg": bsg, "wsT": wsT, "trilT": trilT, "strip": strip,
            "onehot": onehot, "ident": ident, "cfar": cfar,
            "gatebias": gbias.reshape(128, 512), "ownmask": omask.reshape(128, 512),
        })
    return in_maps
```

```python
import math
import os
from contextlib import ExitStack
import numpy as np
import concourse.bass as bass
import concourse.mybir as mybir
from concourse.bass_utils import run_bass_kernel_spmd

F32 = mybir.dt.float32
BF16 = mybir.dt.bfloat16
AF = mybir.ActivationFunctionType
ALU = mybir.AluOpType
AX = mybir.AxisListType

NEG = -30000.0
STRIPW = 1792


class _Op:
    __slots__ = ("eng", "fn", "waits", "signal", "count", "dma_sem", "dma_count")

    def __init__(self, eng, fn):
        self.eng = eng
        self.fn = fn
        self.waits = []
        self.signal = False
        self.count = None
        self.dma_sem = None
        self.dma_count = None


class Prog:
    STREAMS = ("pe", "act", "dve", "pool", "sp")

    def __init__(self):
        self.ops = []
        self.stream_ops = {s: [] for s in self.STREAMS}
        self.last_write = {}
        self.readers = {}
        self.dma_groups = {}

    def _tok(self, i):
        op = self.ops[i]
        if op.dma_sem is not None:
            return ("dma:" + op.dma_sem, op.dma_count)
        op.signal = True
        return ("eng:" + op.eng, i)

    MAXOPS = int(os.environ.get("K_MAXOPS", "100000000"))

    def add(self, eng, fn, reads=(), writes=(), dsem=None):
        if len(self.ops) >= self.MAXOPS:
            return None
        op = _Op(eng, fn)
        idx = len(self.ops)
        deps = set()
        for k in reads:
            w = self.last_write.get(k)
            if w is not None:
                deps.add(w)
        for k in writes:
            w = self.last_write.get(k)
            if w is not None:
                deps.add(w)
            for r in self.readers.get(k, ()):
                deps.add(r)
        if dsem is not None:
            c = self.dma_groups.get(dsem, 0) + 16
            self.dma_groups[dsem] = c
            op.dma_sem = dsem
            op.dma_count = c
        self.ops.append(op)
        self.stream_ops[eng].append(idx)
        for d in deps:
            dop = self.ops[d]
            if dop.dma_sem is None and dop.eng == eng and eng in ("pe", "sp"):
                continue
            op.waits.append(self._tok(d))
        for k in reads:
            self.readers.setdefault(k, []).append(idx)
        for k in writes:
            self.last_write[k] = idx
            self.readers[k] = []
        return idx

    def barrier(self, exclude=()):
        toks = []
        for s, lst in self.stream_ops.items():
            for i in reversed(lst):
                if self.ops[i].fn is not None and self.ops[i].dma_sem is None:
                    toks.append(self._tok(i))
                    break
        for name, c in self.dma_groups.items():
            if name in exclude:
                continue
            toks.append(("dma:" + name, c))
        for s in self.STREAMS:
            op = _Op(s, None)
            op.waits = list(toks)
            self.ops.append(op)
            self.stream_ops[s].append(len(self.ops) - 1)
        self.last_write = {}
        self.readers = {}

    def finalize(self):
        cnt = {s: 0 for s in self.STREAMS}
        for op in self.ops:
            if op.dma_sem is None and op.signal:
                cnt[op.eng] += 1
                op.count = cnt[op.eng]

    def sem_keys(self):
        return ["eng:" + s for s in self.STREAMS] + ["dma:" + n for n in self.dma_groups]

    def emit_stream(self, stream, eng, sems):
        waited = {}
        for i in self.stream_ops[stream]:
            op = self.ops[i]
            need = {}
            for (k, v) in op.waits:
                if k.startswith("eng:"):
                    v = self.ops[v].count
                if v > need.get(k, 0):
                    need[k] = v
            for k, v in need.items():
                if waited.get(k, 0) >= v:
                    continue
                if k == "eng:" + stream and stream in ("pe", "sp"):
                    continue
                eng.wait_ge(sems[k], v)
                waited[k] = v
            if op.fn is None:
                continue
            ins = op.fn(eng)
            if op.dma_sem is not None:
                ins.then_inc(sems["dma:" + op.dma_sem], 16)
            elif op.signal:
                ins.then_inc(sems["eng:" + stream], 1)


class Arena:
    def __init__(self, ap, nwords):
        self.ar = ap
        self.n = nwords
        self.off = 0

    def f32(self, cols, parts=128):
        a = self.ar[0:parts, self.off:self.off + cols]
        self.off += cols
        assert self.off <= self.n, (self.off, self.n)
        return a

    def bf16(self, cols, parts=128):
        w = (cols + 1) // 2
        a = self.ar[0:parts, self.off:self.off + w].bitcast(BF16)
        self.off += w
        assert self.off <= self.n, (self.off, self.n)
        return a


def build_program(stop_phase=None, dbg=False):
    nc = bass.Bass("TRN2", target_bir_lowering=False)

    def din(name, shape, dt=F32):
        return nc.dram_tensor(name, shape, dt, kind="ExternalInput").ap()

    def dscr(name, shape, dt=BF16):
        ext = dbg and name in ("Kscr", "Vscr", "Qscr", "Yscr")
        return nc.dram_tensor(name, shape, dt, kind=("ExternalOutput" if ext else "Internal")).ap()

    xk = din("xk", [16, 128, 8, 512])
    pT = din("pT", [4, 128, 2, 512])
    w_in = din("w_in", [1024, 7168])
    w_out = din("w_out", [1024, 1024])
    w_ff1 = din("w_ff1", [1024, 4096])
    w_ff2 = din("w_ff2", [4096, 1024])
    w_pg = din("w_pg", [1024, 1024])
    w_pp = din("w_pp", [256, 1024])
    gcols = din("gcols", [128, 32])
    lnv = din("lnv", [2, 1024])
    bsg = din("bsg", [1, 1024])
    wsT = din("wsT", [128, 1024])
    trilT = din("trilT", [128, 128])
    strip = din("strip", [16, 128, STRIPW])
    onehot = din("onehot", [32, 8192], BF16)
    ident = din("ident", [128, 128])
    cfar = din("cfar", [128, 16])
    gatebias = din("gatebias", [128, 512])
    ownmask = din("ownmask", [128, 512])
    outT = nc.dram_tensor("outT", [4, 128, 8, 512], F32, kind="ExternalOutput").ap()

    Kscr = dscr("Kscr", [8, 128, 8192])
    Vscr = dscr("Vscr", [128, 16, 64, 128])
    Qscr = dscr("Qscr", [16, 96, 2048])
    Yscr = dscr("Yscr", [8, 128, 2048])
    Wa = dscr("Wa", [1024, 4096])
    Wo = dscr("Wo", [1024, 1024])
    W1 = dscr("W1", [1024, 4096])
    W2 = dscr("W2", [4096, 1024])
    Wg = dscr("Wg", [1024, 1024])
    Wp = dscr("Wp", [256, 1024])

    P = Prog()
    es = ExitStack()
    NW = 53000
    arena_t = es.enter_context(nc.sbuf_tensor("arena", [128, NW], F32))
    A = Arena(arena_t, NW)
    ps = []
    psall = es.enter_context(nc.psum_tensor("psall", [128, 4096], F32))
    for i in range(8):
        ps.append((psall[:, i * 512:(i + 1) * 512], "ps%d" % i))
    bank_ctr = [0]

    def nextbank(lo=int(os.environ.get("K_BANKLO", "0")), hi=8):
        n = hi - lo
        b = ps[lo + bank_ctr[0] % n]
        bank_ctr[0] += 1
        return b

    def mm(out, lhsT, rhs, start, stop, reads, writes):
        P.add("pe", lambda e: e.matmul(out, lhsT=lhsT, rhs=rhs, start=start, stop=stop), reads, writes)

    def act(out, in_, func, reads, writes, bias=None, scale=1.0):
        if bias is None:
            P.add("act", lambda e: e.activation(out=out, in_=in_, func=func, scale=scale), reads, writes)
        else:
            P.add("act", lambda e: e.activation(out=out, in_=in_, func=func, bias=bias, scale=scale), reads, writes)

    def tt(eng, out, in0, in1, op, reads, writes):
        P.add(eng, lambda e: e.tensor_tensor(out=out, in0=in0, in1=in1, op=op), reads, writes)

    def ts(eng, out, in0, s1, s2, op0, op1, reads, writes):
        if s2 is None:
            P.add(eng, lambda e: e.tensor_scalar(out=out, in0=in0, scalar1=s1, scalar2=None, op0=op0), reads, writes)
        else:
            P.add(eng, lambda e: e.tensor_scalar(out=out, in0=in0, scalar1=s1, scalar2=s2, op0=op0, op1=op1), reads, writes)

    def stt(eng, out, in0, scalar, in1, op0, op1, reads, writes):
        P.add(eng, lambda e: e.scalar_tensor_tensor(out=out, in0=in0, scalar=scalar, in1=in1, op0=op0, op1=op1), reads, writes)

    def cp(eng, out, in_, reads, writes):
        if eng == "act":
            P.add(eng, lambda e: e.activation(out=out, in_=in_, func=AF.Copy), reads, writes)
        else:
            P.add(eng, lambda e: e.tensor_copy(out=out, in_=in_), reads, writes)

    def dma(q, out, in_, reads, writes, dsem):
        if q == "pool":
            writes = list(writes) + ["swq"]
        P.add(q, lambda e: e.dma_start(out=out, in_=in_), reads, writes, dsem=dsem)

    ones_bf = A.bf16(128)
    ident_bf = A.bf16(128)
    epsc = A.f32(1)
    zeroc = A.f32(1)
    gcols_sb = A.f32(32)
    cfar_sb = A.f32(16)
    base = A.off

    P.add("pool", lambda e: e.memset(ones_bf, 1.0), writes=["ones"])
    P.add("pool", lambda e: e.memset(epsc, 1e-6), writes=["consts"])
    P.add("pool", lambda e: e.memset(zeroc, 0.0), writes=["consts"])
    ident_f = A.f32(128)
    base = A.off
    dma("sp", ident_f, ident[:, :], [], ["ident_f"], "c_ident")
    cp("pool", ident_bf, ident_f, ["ident_f"], ["ident"])
    dma("sp", gcols_sb, gcols[:, :], [], ["gcols"], "c_gcols")
    dma("sp", cfar_sb, cfar[:, :], [], ["cfar"], "c_cfar")

    def rmsnorm(X, xkeys, gi, out, outkeys, sq, sd, rstd):
        act(sq[:, 0:4, :], X[:, 0:4, :], AF.Square, xkeys[0:4], ["sq0"])
        act(sq[:, 4:8, :], X[:, 4:8, :], AF.Square, xkeys[4:8], ["sq1"])
        b, bk = nextbank()
        for c in range(8):
            mm(b, ones_bf, sq[:, c, :], c == 0, c == 7, ["sq%d" % (c // 4), "ones"], [bk])
        act(sd, b, AF.Sqrt, [bk, "consts"], ["sd"], bias=epsc[:, 0:1], scale=1.0 / 1024.0)
        P.add("dve", lambda e: e.reciprocal(out=rstd, in_=sd), ["sd"], ["rstd"])
        for c in range(8):
            stt("dve", out[:, c, :], X[:, c, :], gcols_sb[:, gi * 8 + c: gi * 8 + c + 1], rstd, ALU.mult, ALU.mult,
                [xkeys[c], "rstd", "gcols"], [outkeys[c]])

    Wqkv = A.bf16(8 * 3072).rearrange("p (k c) -> p k c", k=8)
    Xk = [A.f32(4096).rearrange("p (k t) -> p k t", k=8) for _ in range(2)]
    sq = A.bf16(4096).rearrange("p (k t) -> p k t", k=8)
    hTs = [A.bf16(4096).rearrange("p (k t) -> p k t", k=8) for _ in range(2)]
    Kst = [A.bf16(4096).rearrange("p (k t) -> p k t", k=8) for _ in range(2)]
    Vst = [A.bf16(8192).rearrange("p (j two a d) -> p j two a d", j=8, two=2, a=4) for _ in range(2)]
    Qst = A.bf16(8192, parts=96)
    Qst3 = Qst.rearrange("p (h t) -> p h t", h=16)
    Qst4 = Qst.rearrange("p (g a t) -> p g a t", g=4, a=4)
    sd = A.f32(512)
    rstd = A.f32(512)
    ksum = A.f32(256).rearrange("p (j n) -> p j n", j=8)
    kmeanH = A.bf16(512, parts=64).rearrange("p (j a n) -> p j a n", j=8, a=2)
    gb_sb = A.f32(512).rearrange("p (i n) -> p i n", i=16)
    om_sb = A.f32(512).rearrange("p (i n) -> p i n", i=16)
    gms = [A.f32(512) for _ in range(4)]
    m8 = A.f32(128).rearrange("p (h n) -> p h n", h=16)
    thr = A.f32(16)
    sel = A.f32(512)
    sel3 = sel.rearrange("p (h n) -> p h n", h=16)
    selbs = [A.bf16(512) for _ in range(4)]
    if os.environ.get("K_TRACE"): print("phase1 arena words", A.off)

    w_in_v = w_in.rearrange("(k p) c -> p k c", p=128)
    wstg = [A.f32(1024) for _ in range(2)]
    n_ = 0
    for i, nm in (1, "wk"), (2, "wv"), (0, "wq"):
        for k8 in range(8):
            st_ = n_ % 2
            n_ += 1
            dma("sp", wstg[st_], w_in_v[:, k8, 2048 + i * 1024: 2048 + (i + 1) * 1024], [], ["wstg%d" % st_], "c_wstg%d" % st_)
            cp("pool" if k8 % 2 else "act", Wqkv[:, k8, i * 1024:(i + 1) * 1024], wstg[st_], ["wstg%d" % st_], [nm + str(k8)])
    dma("sp", gb_sb.rearrange("p i n -> p (i n)"), gatebias[:, :], [], ["gconst"], "c_gb")
    dma("sp", om_sb.rearrange("p i n -> p (i n)"), ownmask[:, :], [], ["gconst2"], "c_om")
    P.add("pool", lambda e: e.memset(ksum.rearrange("p j n -> p (j n)"), 0.0), writes=["ksum%d" % j for j in range(8)])
    for s_ in range(2):
        P.add("pool", (lambda o: (lambda e: e.memset(o, 1.0)))(Vst[s_].rearrange("p j two a d -> p (j two a d)")),
              writes=["Vst%d_%d_%d" % (s_, a, h) for a in range(4) for h in range(2)])

    def wcast(dst, src, rows, cols, rblk):
        for r0 in range(0, rows, rblk):
            dma("pool", dst[r0:r0 + rblk, :], src[r0:r0 + rblk, :], [], [], "wcast")

    def xkeys_of(s):
        return ["Xk%d_%d" % (s, c) for c in range(8)]

    def hkeys_of(s):
        return ["hT%d_%d" % (s, c) for c in range(8)]

    NT1 = int(os.environ.get('K_NT1', '16'))
    for m0 in range(min(2, NT1)):
        dma("sp", Xk[m0].rearrange("p k t -> p (k t)"), xk[m0].rearrange("p k t -> p (k t)"), [], xkeys_of(m0), "xk%d" % m0)
    wcast_done = False
    rmsnorm(Xk[0], xkeys_of(0), 0, hTs[0], hkeys_of(0), sq, sd, rstd)
    for m in range(NT1):
        s = m % 2
        if False:
            wcast_done = True
            wcast(Wa[:, 0:2048], w_in[:, 0:2048], 1024, 2048, 512)
            wcast(Wa[:, 2048:4096], w_in[:, 5120:7168], 1024, 2048, 512)
            wcast(Wo, w_out, 1024, 1024, 1024)
            wcast(W1, w_ff1, 1024, 4096, 256)
            wcast(W2, w_ff2, 4096, 1024, 1024)
            wcast(Wg, w_pg, 1024, 1024, 1024)
            wcast(Wp, w_pp, 256, 1024, 256)
        hT = hTs[s]
        hk = hkeys_of(s)
        own = (m % 4 == 3)
        p = m // 4
        for j in range(8):
            b, bk = nextbank()
            for c in range(8):
                mm(b, Wqkv[:, c, 1024 + j * 128: 1024 + (j + 1) * 128], hT[:, c, :], c == 0, c == 7, [hk[c], "wk%d" % c], [bk])
            act(Kst[s][:, j, :], b, AF.Copy, [bk], ["Kst%d_%d" % (s, j)])
            P.add("dve", (lambda o, i: (lambda e: e.tensor_reduce(out=o, in_=i, axis=AX.X, op=ALU.add)))(
                ksum[:, j, 2 * m:2 * m + 2], Kst[s][:, j, :].rearrange("p (a t) -> p a t", a=2)), ["Kst%d_%d" % (s, j)], ["ksum%d" % j])
        if m + 1 < NT1:
            s1 = (m + 1) % 2
            rmsnorm(Xk[s1], xkeys_of(s1), 0, hTs[s1], hkeys_of(s1), sq, sd, rstd)
        if m + 2 < NT1:
            dma("sp", Xk[s].rearrange("p k t -> p (k t)"), xk[m + 2].rearrange("p k t -> p (k t)"), [], xkeys_of(s), "xk%d" % s)
        def emit_V(ev):
            for t4 in range(4):
                for half in range(2):
                    b, bk = nextbank()
                    for c in range(8):
                        mm(b, hT[:, c, t4 * 128:(t4 + 1) * 128], Wqkv[:, c, 2048 + half * 512: 2048 + (half + 1) * 512],
                           c == 0, c == 7, [hk[c], "wv%d" % c], [bk])
                    b4 = b.rearrange("p (j two d) -> p j two d", j=4, two=2)
                    cp(ev, Vst[s][:, half * 4:(half + 1) * 4, 0, t4, 0:64], b4[:, :, 0, :], [bk], ["Vst%d_%d_%d" % (s, t4, half)])
                    cp(ev, Vst[s][:, half * 4:(half + 1) * 4, 1, t4, 64:128], b4[:, :, 1, :], [bk], ["Vst%d_%d_%d" % (s, t4, half)])

        if not own:
            emit_V("dve")
        else:
            for j in range(8):
                b, bk = nextbank()
                for c in range(8):
                    mm(b, Wqkv[:, c, j * 128:(j + 1) * 128], hT[:, c, :], c == 0, c == 7, [hk[c], "wq%d" % c], [bk])
                act(Qst3[0:64, 2 * j, :], b[0:64, :], AF.Copy, [bk], ["Qst_%d" % (2 * j)], scale=0.125)
                act(Qst3[0:64, 2 * j + 1, :], b[64:128, :], AF.Copy, [bk], ["Qst_%d" % (2 * j + 1)], scale=0.125)
            cp("dve", kmeanH[:, :, 0, :], ksum[0:64, :, :], ["ksum%d" % j for j in range(8)], ["kmeanH0"])
            cp("dve", kmeanH[:, :, 1, :], ksum[64:128, :, :], ["ksum%d" % j for j in range(8)], ["kmeanH1"])
            for qt in range(4):
                idx = p * 4 + qt
                b, bk = nextbank()
                for h in range(16):
                    mm(b[:, h * 32:(h + 1) * 32], Qst3[0:64, h, qt * 128:(qt + 1) * 128], kmeanH[0:64, h // 2, h % 2, :],
                       True, True, ["Qst_%d" % h, "kmeanH%d" % (h % 2)], [bk])
                tt("dve", gms[qt].rearrange("p (h n) -> p h n", h=16), b.rearrange("p (h n) -> p h n", h=16),
                   gb_sb[:, idx, :].unsqueeze(1).to_broadcast([128, 16, 32]), ALU.add, [bk, "gconst"], ["gm%d" % qt])
            emit_V("act")
            for qt in range(4):
                idx = p * 4 + qt
                gm3 = gms[qt].rearrange("p (h n) -> p h n", h=16)
                gk = "gm%d" % qt
                for h in range(16):
                    P.add("dve", (lambda o, i: (lambda e: e.max(out=o, in_=i)))(m8[:, h, :], gm3[:, h, :]), [gk], ["m8_%d" % h])
                ts("dve", thr.unsqueeze(2), m8[:, :, 2:3], -1e29, None, ALU.max, None, ["m8_%d" % h for h in range(16)], ["thr"])
                tt("dve", sel3, gm3, thr.unsqueeze(2).to_broadcast([128, 16, 32]), ALU.is_ge, [gk, "thr"], ["sel"])
                tt("dve", sel3, sel3, om_sb[:, idx, :].unsqueeze(1).to_broadcast([128, 16, 32]), ALU.add, ["sel", "gconst2"], ["sel"])
                ts("dve", selbs[qt], sel, -NEG, NEG, ALU.mult, ALU.add, ["sel"], ["selb%d" % qt])
            for qt in range(4):
                b2, bk2 = nextbank()
                for g in range(4):
                    mm(b2[:, g * 128:(g + 1) * 128], selbs[qt][:, g * 128:(g + 1) * 128], ident_bf, True, True, ["selb%d" % qt, "ident"], [bk2])
                for hp in range(4):
                    act(Qst4[64:96, :, hp, qt * 128:(qt + 1) * 128], b2[32 * hp:32 * hp + 32, :].rearrange("p (g q) -> p g q", g=4),
                        AF.Copy, [bk2], ["Qsel_%d" % hp])
        if os.environ.get("K_TRACE"): print("tile", m, "after V/Q/gate", len(P.ops))
        if os.environ.get('K_SKIP_ST'):
            continue
        dma("sp", Kscr[:, :, m * 512:(m + 1) * 512].rearrange("j p t -> p j t"), Kst[s],
            ["Kst%d_%d" % (s, j) for j in range(8)], [], "kst%d" % s)
        dma("sp", Vscr[:, :, 4 * m:4 * m + 4, :], Vst[s].rearrange("p j two a d -> p (j two) a d"),
            ["Vst%d_%d_%d" % (s, a, h) for a in range(4) for h in range(2)], [], "vst%d" % s)
        if own:
            dma("sp", Qscr[:, :, p * 512:(p + 1) * 512].rearrange("h r t -> r h t"), Qst3,
                ["Qst_%d" % h for h in range(16)] + ["Qsel_%d" % h for h in range(4)], [], "qst")

    P.barrier(exclude=("wcast",))
    if stop_phase == 1:
        P.barrier()
        return _finish(nc, P, es)

    A.off = base
    KK = [[A.bf16(2048, parts=96) for c in range(5)] for hd in range(2)]
    VV = [[A.bf16(2048).rearrange("p (k d) -> p k d", k=16) for c in range(5)] for hd in range(2)]
    QQ = [[A.bf16(2048, parts=96) for sl in range(2)] for hd in range(2)]
    stripf = [A.f32(STRIPW) for hd in range(2)]
    stripb = [[A.bf16(STRIPW) for sl in range(2)] for hd in range(2)]
    NPT = 6
    PT = [A.bf16(1024) for _ in range(NPT)]
    yst = [A.bf16(512) for _ in range(2)]
    Rr = A.f32(512)
    Ocp = [A.f32(512) for _ in range(2)]

    for c in range(5):
        cc = c % 4
        for hd in range(2):
            dma("sp" if hd == 0 else "act", KK[hd][c][64:96, :], onehot[:, cc * 2048:(cc + 1) * 2048], [], ["KOH%d%d" % (hd, c)], "oh%d%d" % (hd, c))
    wcf = [A.f32(4096) for _ in range(2)]
    wcb = [A.bf16(4096) for _ in range(2)]
    wpieces = []
    for r in range(8):
        wpieces.append([(Wa[r * 128:(r + 1) * 128, hf * 2048:(hf + 1) * 2048], w_in[r * 128:(r + 1) * 128, c0:c0 + 2048], hf * 2048, 2048, None)
                        for hf, c0 in ((0, 0), (1, 5120))])
    for r in range(8):
        wpieces.append([(W1[r * 128:(r + 1) * 128, :], w_ff1[r * 128:(r + 1) * 128, :], 0, 4096, None)])
    for r in range(8):
        wpieces.append([(W2[r * 512:(r + 1) * 512, :], w_ff2[r * 512:(r + 1) * 512, :], 0, 4096, 4)])
    for r in range(2):
        wpieces.append([(Wo[r * 512:(r + 1) * 512, :], w_out[r * 512:(r + 1) * 512, :], 0, 4096, 4)])
        wpieces.append([(Wg[r * 512:(r + 1) * 512, :], w_pg[r * 512:(r + 1) * 512, :], 0, 4096, 4)])
    wpieces.append([(Wp[:, :], w_pp[:, :], 0, 2048, 2)])
    wp_ctr = [0]
    cur_it = [0]

    def emit_wpiece():
        if wp_ctr[0] >= len(wpieces):
            return
        st_ = wp_ctr[0] % 2
        parts_ = wpieces[wp_ctr[0]]
        wp_ctr[0] += 1
        ntot = 0
        outs_ = []
        for (dst, src, c0, n, a) in parts_:
            if a is not None:
                dst = dst.rearrange("(a p) c -> p a c", p=128)
                src = src.rearrange("(a p) c -> p a c", p=128)
                f_ = wcf[st_][:, c0:c0 + n].rearrange("p (a c) -> p a c", a=a)
                b_ = wcb[st_][:, c0:c0 + n].rearrange("p (a c) -> p a c", a=a)
            else:
                f_ = wcf[st_][:, c0:c0 + n]
                b_ = wcb[st_][:, c0:c0 + n]
            dma("sp", f_, src, [], ["wcf%d" % st_], "wcf%d" % st_)
            outs_.append((dst, b_))
            ntot = max(ntot, c0 + n)
        hn = ntot // 2
        cp("dve", wcb[st_][:, 0:hn], wcf[st_][:, 0:hn], ["wcf%d" % st_], ["wcb%da" % st_])
        cp("dve", wcb[st_][:, hn:ntot], wcf[st_][:, hn:ntot], ["wcf%d" % st_], ["wcb%db" % st_])
        for (dst, b_) in outs_:
            deferred.append((cur_it[0] + 1, (lambda dst=dst, b_=b_, st_=st_: dma("act", dst, b_, ["wcb%da" % st_, "wcb%db" % st_], [], "wcb%d" % st_))))

    deferred = []

    def flush_deferred(force=False):
        keep = []
        while deferred:
            rdy, fn_ = deferred.pop(0)
            if force or rdy <= cur_it[0]:
                fn_()
            else:
                keep.append((rdy, fn_))
        deferred.extend(keep)

    pt_ctr = [0]
    sp_ctr = [0]
    for j in range(8):
        sl = j % 2
        for hd in range(2):
            h = 2 * j + hd
            dma("sp", QQ[hd][sl][0:96, :], Qscr[h], [], ["Q%d%d" % (hd, sl)], "q%d%d" % (hd, sl))
            dma("sp", stripf[hd], strip[h], [], ["stripf%d" % hd], "sf%d" % hd)
            cp("dve", stripb[hd][sl], stripf[hd], ["stripf%d" % hd], ["stripb%d%d" % (hd, sl)])
        cslot = [4 if (j % 2) else 0, 1, 2, 3]
        for c in range(4):
            cs = cslot[c]
            for hd in range(2):
                dma("sp", KK[hd][cs][0:64, :], Kscr[j, 64 * hd:64 * hd + 64, c * 2048:(c + 1) * 2048], [],
                    ["K%d%d" % (hd, cs)], "k%d%d" % (hd, cs))
                dma("sp", VV[hd][cs], Vscr[:, 2 * j + hd, 16 * c:16 * c + 16, :], [],
                    ["V%d%d" % (hd, cs)], "v%d%d" % (hd, cs))
        for p in range(4):
            it = j * 4 + p
            cur_it[0] = it
            emit_wpiece()
            Ob = [ps[0], ps[1]]
            ng = 16 * (p + 1)
            LAG = 1
            pend = []

            nfar = 16 * p + 5
            groups = [[2 * i, 2 * i + 1] for i in range(nfar // 2)] + [[nfar - 1]]
            groups += [[g] for g in range(nfar, ng - 8)] + [[g] for g in range(ng - 4, ng)] + [[g] for g in range(ng - 8, ng - 4)]
            first_g = groups[0][0]
            last_g = groups[-1][-1]

            def tinfo(g, p=p, cslot=cslot):
                c, t = divmod(g, 16)
                d = 4 * (4 * p + 3) - g
                return cslot[c], t, d, (128 * (-d) if d < 0 else 0)

            def emit_pv(hd, grp, pts, Ob=Ob, first_g=first_g, last_g=last_g):
                for idx, g in enumerate(grp):
                    c, t, d, off = tinfo(g)
                    mm(Ob[hd][0][:, off:512], VV[hd][c][:, t, :], PT[pts][:, idx * 512 + off:(idx + 1) * 512], g == first_g, g == last_g,
                       ["V%d%d" % (hd, c), "PT%d" % pts], [Ob[hd][1]])

            for gi, grp in enumerate(groups):
                if gi == 6:
                    flush_deferred()
                for hd in range(2):
                    h = 2 * j + hd
                    b0 = 2 + 2 * (sp_ctr[0] % 3)
                    sp_ctr[0] += 1
                    bkeys = []
                    near = False
                    for idx, g in enumerate(grp):
                        c, t, d, off = tinfo(g)
                        near = d <= 7
                        sb_, sbk = ps[b0 + idx]
                        bkeys.append(sbk)
                        mm(sb_[:, off:512], KK[hd][c][0:96, t * 128:(t + 1) * 128], QQ[hd][sl][0:96, p * 512 + off:(p + 1) * 512], True, not near,
                           ["K%d%d" % (hd, c), "KOH%d%d" % (hd, c), "Q%d%d" % (hd, sl)], [sbk])
                        if near:
                            mm(sb_[:, off:512], ident_bf, stripb[hd][sl][:, 128 * (d + 3) + off:128 * (d + 3) + 512], False, True,
                               ["stripb%d%d" % (hd, sl), "ident"], [sbk])
                    pts = pt_ctr[0] % NPT
                    pt_ctr[0] += 1
                    if len(grp) == 2:
                        act(PT[pts][:, 0:1024], psall[:, b0 * 512:(b0 + 2) * 512], AF.Exp, bkeys + ["consts", "cfar"], ["PT%d" % pts],
                            bias=cfar_sb[:, h:h + 1])
                    else:
                        bias = zeroc[:, 0:1] if near else cfar_sb[:, h:h + 1]
                        act(PT[pts][:, off:512], ps[b0][0][:, off:512], AF.Exp, bkeys + ["consts", "cfar"], ["PT%d" % pts], bias=bias)
                    pend.append((hd, gi, grp, pts))
                while pend and pend[0][1] <= gi - LAG:
                    hd_, gi_, grp_, pts_ = pend.pop(0)
                    emit_pv(hd_, grp_, pts_)
            while pend:
                hd_, gi_, grp_, pts_ = pend.pop(0)
                emit_pv(hd_, grp_, pts_)
            ys = it % 2
            cp("dve", Ocp[0], Ob[0][0], [Ob[0][1]], ["Ocp0"])
            cp("dve", Ocp[1], Ob[1][0], [Ob[1][1]], ["Ocp1"])
            P.add("dve", (lambda o, i: (lambda e: e.reciprocal(out=o, in_=i)))(Rr[0:64, :], Ocp[0][64:128, :]), ["Ocp0"], ["R0"])
            P.add("dve", (lambda o, i: (lambda e: e.reciprocal(out=o, in_=i)))(Rr[64:128, :], Ocp[1][0:64, :]), ["Ocp1"], ["R1"])
            tt("dve", yst[ys][0:64, :], Ocp[0][0:64, :], Rr[0:64, :], ALU.mult, ["Ocp0", "R0"], ["yst%da" % ys])
            tt("dve", yst[ys][64:128, :], Ocp[1][64:128, :], Rr[64:128, :], ALU.mult, ["Ocp1", "R1"], ["yst%db" % ys])
            deferred.append((it + 1, (lambda j=j, p=p, ys=ys: dma("act", Yscr[j, :, p * 512:(p + 1) * 512], yst[ys],
                                                                   ["yst%da" % ys, "yst%db" % ys], [], "yst%d" % ys))))

    while wp_ctr[0] < len(wpieces):
        emit_wpiece()
    flush_deferred(force=True)
    P.barrier()
    if stop_phase == 3:
        return _finish(nc, P, es)

    A.off = base
    NS = 5
    slots = [A.bf16(4096).rearrange("p (k c) -> p k c", k=8) for _ in range(NS)]
    hT = A.bf16(4096).rearrange("p (k t) -> p k t", k=8)
    sq = A.bf16(4096).rearrange("p (k t) -> p k t", k=8)
    yb = A.bf16(4096).rearrange("p (k t) -> p k t", k=8)
    vn = A.bf16(4096).rearrange("p (a c) -> p a c", a=4)
    merged = A.bf16(4096).rearrange("p (k t) -> p k t", k=8)
    aT = A.bf16(16384).rearrange("p (k t) -> p k t", k=32)
    pTb = A.bf16(1024).rearrange("p (k t) -> p k t", k=2)
    WsTb = A.bf16(1024).rearrange("p (g t) -> p g t", g=8)
    X = A.f32(4096).rearrange("p (k t) -> p k t", k=8)
    M = A.f32(4096).rearrange("p (k t) -> p k t", k=8)
    vg = [A.f32(1024) for _ in range(2)]
    vc = [A.f32(1024) for _ in range(2)]
    sqv = A.f32(1024)
    lng = A.f32(1024)
    lnb = A.f32(1024)
    bs_sb = A.f32(1024).rearrange("p (g t) -> p g t", g=8)
    NTMP = 6
    tmps = [A.f32(512) for _ in range(NTMP)]
    sd = A.f32(512)
    rstd = A.f32(512)
    wsf = A.f32(1024).rearrange("p (g t) -> p g t", g=8)
    trl = A.f32(128)
    pTf = A.f32(1024)
    s1 = A.f32(8)
    nmn = A.f32(8)
    s2 = A.f32(8)
    sdv = A.f32(8)
    rv = A.f32(8)

    dma("sp", lng, lnv[0:1, :].partition_broadcast(128), [], ["lng"], "c_lng")
    dma("sp", lnb, lnv[1:2, :].partition_broadcast(128), [], ["lnb"], "c_lnb")
    dma("sp", bs_sb.rearrange("p g t -> p (g t)"), bsg[0:1, :].partition_broadcast(128), [], ["bs"], "c_bs")
    dma("sp", wsf.rearrange("p g t -> p (g t)"), wsT[:, :], [], ["wsf"], "c_wsf")
    dma("sp", trl, trilT[:, :], [], ["trl"], "c_trl")
    tt("dve", WsTb, wsf, trl.unsqueeze(1).to_broadcast([128, 8, 128]), ALU.mult, ["wsf", "trl"], ["WsT"])

    slot_ctr = [0]

    def load_piece(src, nk):
        s = slot_ctr[0] % NS
        slot_ctr[0] += 1
        dma("sp", slots[s][:, 0:nk, :], src.rearrange("(k p) c -> p k c", p=128), [], ["ws%d" % s], "ws%d" % s)
        return slots[s], "ws%d" % s

    tmp_ctr = [0]

    def nexttmp():
        i = tmp_ctr[0] % NTMP
        tmp_ctr[0] += 1
        return tmps[i], "tmp%d" % i

    Xk_ = ["X_%d" % c for c in range(8)]
    hk = ["hT_%d" % c for c in range(8)]
    Mk = ["M_%d" % c for c in range(8)]
    mgk = ["mg_%d" % c for c in range(8)]

    def proj_fm(piece, pkey, nk, rhs3, rhs_keys, evac):
        for o in range(4):
            b, bk = nextbank()
            for k in range(nk):
                mm(b, piece[:, k, o * 128:(o + 1) * 128], rhs3[:, k, :], k == 0, k == nk - 1, [pkey, rhs_keys[k]], [bk])
            evac(o, b, bk)

    for p in range(4):
        dma("sp", X.rearrange("p k t -> p (k t)"), xk[4 * p + 3].rearrange("p k t -> p (k t)"), [], Xk_, "x4")
        dma("sp", yb, Yscr[:, :, p * 512:(p + 1) * 512].rearrange("j p t -> p j t"), [], ["yb"], "yb")
        dma("sp", pTf, pT[p].rearrange("p k t -> p (k t)"), [], ["pTf"], "ptf")
        cp("pool", pTb.rearrange("p k t -> p (k t)"), pTf, ["pTf"], ["pTb"])
        rmsnorm(X, Xk_, 0, hT, hk, sq, sd, rstd)
        pv = [load_piece(Wa[:, 1024 + half * 512: 1024 + (half + 1) * 512], 8) for half in range(2)]
        for t4 in range(4):
            vgi = vg[t4 % 2]
            vci = vc[t4 % 2]
            vgk = "vg%d" % (t4 % 2)
            vck = "vc%d" % (t4 % 2)
            for half in range(2):
                b, bk = nextbank()
                for c in range(8):
                    mm(b, hT[:, c, t4 * 128:(t4 + 1) * 128], pv[half][0][:, c, :], c == 0, c == 7, [hk[c], pv[half][1]], [bk])
                act(vgi[:, half * 512:(half + 1) * 512], b, AF.Gelu_apprx_tanh, [bk], [vgk + "_%d" % half])
            vgks = [vgk + "_0", vgk + "_1"]
            P.add("dve", (lambda o, i: (lambda e: e.reduce_sum(out=o, in_=i, axis=AX.X)))(s1[:, t4:t4 + 1], vgi), vgks, ["s1"])
            ts("dve", nmn[:, t4:t4 + 1], s1[:, t4:t4 + 1], -1.0 / 1024.0, None, ALU.mult, None, ["s1"], ["nmn"])
            act(vci, vgi, AF.Identity, vgks + ["nmn"], [vck], bias=nmn[:, t4:t4 + 1])
            tt("dve", sqv, vci, vci, ALU.mult, [vck], ["sqv"])
            P.add("dve", (lambda o, i: (lambda e: e.reduce_sum(out=o, in_=i, axis=AX.X)))(s2[:, t4:t4 + 1], sqv), ["sqv"], ["s2"])
            act(sdv[:, t4:t4 + 1], s2[:, t4:t4 + 1], AF.Sqrt, ["s2", "consts"], ["sdv"], bias=epsc[:, 0:1], scale=1.0 / 1024.0)
            P.add("dve", (lambda o, i: (lambda e: e.reciprocal(out=o, in_=i)))(rv[:, t4:t4 + 1], sdv[:, t4:t4 + 1]), ["sdv"], ["rv"])
            stt("dve", vci, vci, rv[:, t4:t4 + 1], lng, ALU.mult, ALU.mult, [vck, "rv", "lng"], [vck])
            tt("dve", vn[:, t4, :], vci, lnb, ALU.add, [vck, "lnb"], ["vn%d" % t4])
        for half in range(2):
            pc, pk = load_piece(Wa[:, half * 512:(half + 1) * 512], 8)

            def ev_u(o, b, bk, half=half):
                g = half * 4 + o
                act(M[:, g, :], b, AF.Gelu_apprx_tanh, [bk], [Mk[g]])
                pass
            proj_fm(pc, pk, 8, hT, hk, ev_u)
        for half in range(2):
            pc, pk = load_piece(Wa[:, 2048 + half * 512: 2048 + (half + 1) * 512], 8)

            def ev_ga(o, b, bk, half=half):
                g = half * 4 + o
                tm, tk = nexttmp()
                act(tm, b, AF.Sigmoid, [bk], [tk])
                tt("dve", M[:, g, :], M[:, g, :], tm, ALU.mult, [Mk[g], tk], [Mk[g]])
            proj_fm(pc, pk, 8, hT, hk, ev_ga)
        for g in range(8):
            b, bk = nextbank()
            for t4 in range(4):
                mm(b[:, t4 * 128:(t4 + 1) * 128], vn[:, t4, g * 128:(g + 1) * 128], WsTb[:, g, :], True, True,
                   ["vn%d" % t4, "WsT"], [bk])
            tm, tk = nexttmp()
            tt("dve", tm.rearrange("p (a t) -> p a t", a=4), b.rearrange("p (a t) -> p a t", a=4),
               bs_sb[:, g, :].unsqueeze(1).to_broadcast([128, 4, 128]), ALU.add, [bk, "bs"], [tk])
            tt("dve", M[:, g, :], M[:, g, :], tm, ALU.mult, [Mk[g], tk], [Mk[g]])
        for half in range(2):
            pc, pk = load_piece(Wa[:, 3072 + half * 512: 3072 + (half + 1) * 512], 8)

            def ev_gb(o, b, bk, half=half):
                g = half * 4 + o
                tm, tk = nexttmp()
                act(tm, b, AF.Sigmoid, [bk], [tk])
                tt("pool", tm, tm, yb[:, g, :], ALU.mult, [tk, "yb"], [tk])
                tt("dve", merged[:, g, :], M[:, g, :], tm, ALU.add, [Mk[g], tk], [mgk[g]])
            proj_fm(pc, pk, 8, hT, hk, ev_gb)
        for half in range(2):
            pc, pk = load_piece(Wo[:, half * 512:(half + 1) * 512], 8)

            def ev_o(o, b, bk, half=half):
                g = half * 4 + o
                tt("dve", X[:, g, :], X[:, g, :], b, ALU.add, [Xk_[g], bk], [Xk_[g]])
            proj_fm(pc, pk, 8, merged, mgk, ev_o)
        rmsnorm(X, Xk_, 1, hT, hk, sq, sd, rstd)
        for q in range(8):
            pc, pk = load_piece(W1[:, q * 512:(q + 1) * 512], 8)

            def ev_f1(o, b, bk, q=q):
                fc = q * 4 + o
                tm, tk = nexttmp()
                act(tm, b, AF.Relu, [bk], [tk])
                tt("pool" if (o % 2) else "dve", aT[:, fc, :], tm, tm, ALU.mult, [tk], ["aT_%d" % fc])
            proj_fm(pc, pk, 8, hT, hk, ev_f1)
        for i in range(4):
            for half in range(2):
                pc, pk = load_piece(W2[i * 1024:(i + 1) * 1024, half * 512:(half + 1) * 512], 8)
                for o in range(4):
                    b, bk = ps[half * 4 + o]
                    for k in range(8):
                        mm(b, pc[:, k, o * 128:(o + 1) * 128], aT[:, i * 8 + k, :], (i == 0 and k == 0), (i == 3 and k == 7),
                           [pk, "aT_%d" % (i * 8 + k)], [bk])
        for oc in range(8):
            b, bk = ps[oc]
            tt("dve", X[:, oc, :], X[:, oc, :], b, ALU.add, [Xk_[oc], bk], [Xk_[oc]])
        bank_ctr[0] = 0
        rmsnorm(X, Xk_, 2, hT, hk, sq, sd, rstd)
        for half in range(2):
            pg_, pgk = load_piece(Wg[:, half * 512:(half + 1) * 512], 8)
            pp_, ppk = load_piece(Wp[:, half * 512:(half + 1) * 512], 2)
            for o in range(4):
                g = half * 4 + o
                b, bk = nextbank()
                for k in range(8):
                    mm(b, pg_[:, k, o * 128:(o + 1) * 128], hT[:, k, :], k == 0, k == 7, [pgk, hk[k]], [bk])
                tm, tk = nexttmp()
                act(tm, b, AF.Sigmoid, [bk], [tk])
                b2, bk2 = nextbank()
                for k in range(2):
                    mm(b2, pp_[:, k, o * 128:(o + 1) * 128], pTb[:, k, :], k == 0, k == 1, [ppk, "pTb"], [bk2])
                tt("dve", tm, tm, b2, ALU.mult, [tk, bk2], [tk])
                tt("dve", M[:, g, :], X[:, g, :], tm, ALU.add, [Xk_[g], tk], [Mk[g]])
        rmsnorm(M, Mk, 3, M, Mk, sq, sd, rstd)
        dma("act", outT[p].rearrange("p k t -> p (k t)"), M.rearrange("p k t -> p (k t)"), Mk, [], "outst")

    P.barrier()
    return _finish(nc, P, es)


def _finish(nc, P, es):
    P.finalize()
    if os.environ.get("K_TRACE"):
        cnt = {}
        for op in P.ops:
            if op.count is not None:
                cnt[op.eng] = max(cnt.get(op.eng, 0), op.count)
        print("sem counts", cnt, "dma", P.dma_groups, "nops", {k: len(v) for k, v in P.stream_ops.items()})

    sems = {}
    for k in P.sem_keys():
        sems[k] = es.enter_context(nc.semaphore(k.replace(":", "_")))
    block = es.enter_context(nc.Block())

    @block.sync
    def _(e):
        P.emit_stream("sp", e, sems)

    @block.vector
    def _(e):
        P.emit_stream("dve", e, sems)

    @block.scalar
    def _(e):
        P.emit_stream("act", e, sems)

    @block.gpsimd
    def _(e):
        P.emit_stream("pool", e, sems)

    @block.tensor
    def _(e):
        P.emit_stream("pe", e, sems)

    es.close()
    return nc


def _rel_bucket_np(dist):
    n = np.maximum(dist, 0)
    nf = np.maximum(n, 16).astype(np.float32)
    large = 16 + (np.log(nf / np.float32(16)) / np.float32(math.log(1024 / 16)) * np.float32(16)).astype(np.int32)
    large = np.minimum(large, 31)
    return np.where(n < 16, n, large)


_NC_CACHE = {}


def kernel(x, p, norm_mix_g, w_in, w_sgu_spatial, b_sgu_spatial, ln_v_g, ln_v_b, rel_bias,
           w_out, norm_ffn_g, w_ff1, w_ff2, norm_ple_g, w_ple_gate, w_ple_proj, norm_final_g):
    f32 = np.float32
    x = np.asarray(x, f32)
    p = np.asarray(p, f32)
    rel_bias = np.asarray(rel_bias, f32)
    in_maps = make_inputs(x, p, norm_mix_g, w_in, w_sgu_spatial, b_sgu_spatial, ln_v_g, ln_v_b, rel_bias,
                          w_out, norm_ffn_g, w_ff1, w_ff2, norm_ple_g, w_ple_gate, w_ple_proj, norm_final_g)
    if "nc" not in _NC_CACHE:
        _NC_CACHE["nc"] = build_program()
    nc = _NC_CACHE["nc"]
    res = run_bass_kernel_spmd(nc, in_maps, core_ids=list(range(8)))
    out = np.empty((2, 8192, 1024), f32)
    for c in range(8):
        b, j = divmod(c, 4)
        oT = res.results[c]["outT"]
        for pp in range(4):
            m = 4 * pp + j
            out[b, m * 512:(m + 1) * 512, :] = oT[pp].transpose(2, 1, 0).reshape(512, 1024)
    return out


def make_inputs(x, p, norm_mix_g, w_in, w_sgu_spatial, b_sgu_spatial, ln_v_g, ln_v_b, rel_bias,
                w_out, norm_ffn_g, w_ff1, w_ff2, norm_ple_g, w_ple_gate, w_ple_proj, norm_final_g):
    f32 = np.float32
    x = np.asarray(x, f32)
    p = np.asarray(p, f32)
    rel_bias = np.asarray(rel_bias, f32)

    def fm(a):
        t, F = a.shape
        return np.ascontiguousarray(a.T.reshape(F // 128, 128, t).transpose(1, 0, 2))

    gcols = np.zeros((128, 32), f32)
    for i, g in enumerate((norm_mix_g[0], norm_ffn_g[0], norm_ple_g[0], norm_final_g)):
        gcols[:, i * 8:(i + 1) * 8] = np.asarray(g, f32).reshape(8, 128).T
    lnv = np.stack([np.asarray(ln_v_g[0], f32), np.asarray(ln_v_b[0], f32)])
    bsg = np.ascontiguousarray(np.asarray(b_sgu_spatial[0], f32).reshape(1, 1024))
    wsT = np.ascontiguousarray(np.asarray(w_sgu_spatial[0], f32).transpose(2, 0, 1).reshape(128, 1024))
    trilT = (np.arange(128)[:, None] <= np.arange(128)[None, :]).astype(f32)
    ki = np.arange(128)[:, None]
    mm_ = np.arange(STRIPW)[None, :]
    dist = mm_ - 384 - ki
    bidx = _rel_bucket_np(dist)
    strip = np.empty((16, 128, STRIPW), f32)
    for h in range(16):
        strip[h] = np.where(dist >= 0, rel_bias[bidx, h], f32(NEG))
    import ml_dtypes
    onehot = (np.arange(8192)[None, :] // 256 == np.arange(32)[:, None]).astype(ml_dtypes.bfloat16)
    ident = np.eye(128, dtype=f32)
    cfar = np.ascontiguousarray(np.broadcast_to(rel_bias[31][None, :], (128, 16))).astype(f32)

    in_maps = []
    for c in range(8):
        b, j = divmod(c, 4)
        shift = 3 - j
        xkc = np.zeros((16, 128, 8, 512), f32)
        for m in range(16):
            ms = m - shift
            if ms >= 0:
                xkc[m] = fm(x[b, ms * 512:(ms + 1) * 512, :])
        pTc = np.stack([fm(p[0, b, (4 * pp + j) * 512:(4 * pp + j + 1) * 512, :]) for pp in range(4)])
        gbias = np.zeros((128, 16, 32), f32)
        omask = np.zeros((128, 16, 32), f32)
        n = np.arange(32)
        for pp in range(4):
            for qt in range(4):
                own = 8 * pp + 6 + qt // 2
                valid = (n >= 2 * shift) & (n < own)
                gbias[:, pp * 4 + qt, :] = np.where(valid, f32(0), f32(-1e30))[None, :]
                omask[:, pp * 4 + qt, :] = (n == own).astype(f32)[None, :]
        in_maps.append({
            "xk": xkc, "pT": pTc,
            "w_in": np.asarray(w_in[0], f32), "w_out": np.asarray(w_out[0], f32),
            "w_ff1": np.asarray(w_ff1[0], f32), "w_ff2": np.asarray(w_ff2[0], f32),
            "w_pg": np.asarray(w_ple_gate[0], f32), "w_pp": np.asarray(w_ple_proj[0], f32),
            "gcols": gcols, "lnv": lnv, "bsg": bsg, "wsT": wsT, "trilT": trilT, "strip": strip,
            "onehot": onehot, "ident": ident, "cfar": cfar,
            "gatebias": gbias.reshape(128, 512), "ownmask": omask.reshape(128, 512),
        })
    return in_maps
```

```python
import math
import os
from contextlib import ExitStack
import numpy as np
import concourse.bass as bass
import concourse.mybir as mybir
from concourse.bass_utils import run_bass_kernel_spmd

F32 = mybir.dt.float32
BF16 = mybir.dt.bfloat16
AF = mybir.ActivationFunctionType
ALU = mybir.AluOpType
AX = mybir.AxisListType

NEG = -30000.0
STRIPW = 1792


class _Op:
    __slots__ = ("eng", "fn", "waits", "signal", "count", "dma_sem", "dma_count")

    def __init__(self, eng, fn):
        self.eng = eng
        self.fn = fn
        self.waits = []
        self.signal = False
        self.count = None
        self.dma_sem = None
        self.dma_count = None


class Prog:
    STREAMS = ("pe", "act", "dve", "pool", "sp")

    def __init__(self):
        self.ops = []
        self.stream_ops = {s: [] for s in self.STREAMS}
        self.last_write = {}
        self.readers = {}
        self.dma_groups = {}

    def _tok(self, i):
        op = self.ops[i]
        if op.dma_sem is not None:
            return ("dma:" + op.dma_sem, op.dma_count)
        op.signal = True
        return ("eng:" + op.eng, i)

    MAXOPS = int(os.environ.get("K_MAXOPS", "100000000"))

    def add(self, eng, fn, reads=(), writes=(), dsem=None):
        if len(self.ops) >= self.MAXOPS:
            return None
        op = _Op(eng, fn)
        idx = len(self.ops)
        deps = set()
        for k in reads:
            w = self.last_write.get(k)
            if w is not None:
                deps.add(w)
        for k in writes:
            w = self.last_write.get(k)
            if w is not None:
                deps.add(w)
            for r in self.readers.get(k, ()):
                deps.add(r)
        if dsem is not None:
            c = self.dma_groups.get(dsem, 0) + 16
            self.dma_groups[dsem] = c
            op.dma_sem = dsem
            op.dma_count = c
        self.ops.append(op)
        self.stream_ops[eng].append(idx)
        for d in deps:
            dop = self.ops[d]
            if dop.dma_sem is None and dop.eng == eng and eng in ("pe", "sp"):
                continue
            op.waits.append(self._tok(d))
        for k in reads:
            self.readers.setdefault(k, []).append(idx)
        for k in writes:
            self.last_write[k] = idx
            self.readers[k] = []
        return idx

    def barrier(self, exclude=()):
        toks = []
        for s, lst in self.stream_ops.items():
            for i in reversed(lst):
                if self.ops[i].fn is not None and self.ops[i].dma_sem is None:
                    toks.append(self._tok(i))
                    break
        for name, c in self.dma_groups.items():
            if name in exclude:
                continue
            toks.append(("dma:" + name, c))
        for s in self.STREAMS:
            op = _Op(s, None)
            op.waits = list(toks)
            self.ops.append(op)
            self.stream_ops[s].append(len(self.ops) - 1)
        self.last_write = {}
        self.readers = {}

    def finalize(self):
        cnt = {s: 0 for s in self.STREAMS}
        for op in self.ops:
            if op.dma_sem is None and op.signal:
                cnt[op.eng] += 1
                op.count = cnt[op.eng]

    def sem_keys(self):
        return ["eng:" + s for s in self.STREAMS] + ["dma:" + n for n in self.dma_groups]

    def emit_stream(self, stream, eng, sems):
        waited = {}
        for i in self.stream_ops[stream]:
            op = self.ops[i]
            need = {}
            for (k, v) in op.waits:
                if k.startswith("eng:"):
                    v = self.ops[v].count
                if v > need.get(k, 0):
                    need[k] = v
            for k, v in need.items():
                if waited.get(k, 0) >= v:
                    continue
                if k == "eng:" + stream and stream in ("pe", "sp"):
                    continue
                eng.wait_ge(sems[k], v)
                waited[k] = v
            if op.fn is None:
                continue
            ins = op.fn(eng)
            if op.dma_sem is not None:
                ins.then_inc(sems["dma:" + op.dma_sem], 16)
            elif op.signal:
                ins.then_inc(sems["eng:" + stream], 1)


class Arena:
    def __init__(self, ap, nwords):
        self.ar = ap
        self.n = nwords
        self.off = 0

    def f32(self, cols, parts=128):
        a = self.ar[0:parts, self.off:self.off + cols]
        self.off += cols
        assert self.off <= self.n, (self.off, self.n)
        return a

    def bf16(self, cols, parts=128):
        w = (cols + 1) // 2
        a = self.ar[0:parts, self.off:self.off + w].bitcast(BF16)
        self.off += w
        assert self.off <= self.n, (self.off, self.n)
        return a


def build_program(stop_phase=None, dbg=False):
    nc = bass.Bass("TRN2", target_bir_lowering=False)

    def din(name, shape, dt=F32):
        return nc.dram_tensor(name, shape, dt, kind="ExternalInput").ap()

    def dscr(name, shape, dt=BF16):
        ext = dbg and name in ("Kscr", "Vscr", "Qscr", "Yscr")
        return nc.dram_tensor(name, shape, dt, kind=("ExternalOutput" if ext else "Internal")).ap()

    xk = din("xk", [16, 128, 8, 512])
    pT = din("pT", [4, 128, 2, 512])
    w_in = din("w_in", [1024, 7168])
    w_out = din("w_out", [1024, 1024])
    w_ff1 = din("w_ff1", [1024, 4096])
    w_ff2 = din("w_ff2", [4096, 1024])
    w_pg = din("w_pg", [1024, 1024])
    w_pp = din("w_pp", [256, 1024])
    gcols = din("gcols", [128, 32])
    lnv = din("lnv", [2, 1024])
    bsg = din("bsg", [1, 1024])
    wsT = din("wsT", [128, 1024])
    trilT = din("trilT", [128, 128])
    strip = din("strip", [16, 128, STRIPW])
    onehot = din("onehot", [32, 8192], BF16)
    ident = din("ident", [128, 128])
    cfar = din("cfar", [128, 16])
    gatebias = din("gatebias", [128, 512])
    ownmask = din("ownmask", [128, 512])
    outT = nc.dram_tensor("outT", [4, 128, 8, 512], F32, kind="ExternalOutput").ap()

    Kscr = dscr("Kscr", [8, 128, 8192])
    Vscr = dscr("Vscr", [128, 16, 64, 128])
    Qscr = dscr("Qscr", [16, 96, 2048])
    Yscr = dscr("Yscr", [8, 128, 2048])
    Wa = dscr("Wa", [1024, 4096])
    Wo = dscr("Wo", [1024, 1024])
    W1 = dscr("W1", [1024, 4096])
    W2 = dscr("W2", [4096, 1024])
    Wg = dscr("Wg", [1024, 1024])
    Wp = dscr("Wp", [256, 1024])

    P = Prog()
    es = ExitStack()
    NW = 53000
    arena_t = es.enter_context(nc.sbuf_tensor("arena", [128, NW], F32))
    A = Arena(arena_t, NW)
    ps = []
    psall = es.enter_context(nc.psum_tensor("psall", [128, 4096], F32))
    for i in range(8):
        ps.append((psall[:, i * 512:(i + 1) * 512], "ps%d" % i))
    bank_ctr = [0]

    def nextbank(lo=int(os.environ.get("K_BANKLO", "0")), hi=8):
        n = hi - lo
        b = ps[lo + bank_ctr[0] % n]
        bank_ctr[0] += 1
        return b

    def mm(out, lhsT, rhs, start, stop, reads, writes):
        P.add("pe", lambda e: e.matmul(out, lhsT=lhsT, rhs=rhs, start=start, stop=stop), reads, writes)

    def act(out, in_, func, reads, writes, bias=None, scale=1.0):
        if bias is None:
            P.add("act", lambda e: e.activation(out=out, in_=in_, func=func, scale=scale), reads, writes)
        else:
            P.add("act", lambda e: e.activation(out=out, in_=in_, func=func, bias=bias, scale=scale), reads, writes)

    def tt(eng, out, in0, in1, op, reads, writes):
        P.add(eng, lambda e: e.tensor_tensor(out=out, in0=in0, in1=in1, op=op), reads, writes)

    def ts(eng, out, in0, s1, s2, op0, op1, reads, writes):
        if s2 is None:
            P.add(eng, lambda e: e.tensor_scalar(out=out, in0=in0, scalar1=s1, scalar2=None, op0=op0), reads, writes)
        else:
            P.add(eng, lambda e: e.tensor_scalar(out=out, in0=in0, scalar1=s1, scalar2=s2, op0=op0, op1=op1), reads, writes)

    def stt(eng, out, in0, scalar, in1, op0, op1, reads, writes):
        P.add(eng, lambda e: e.scalar_tensor_tensor(out=out, in0=in0, scalar=scalar, in1=in1, op0=op0, op1=op1), reads, writes)

    def cp(eng, out, in_, reads, writes):
        if eng == "act":
            P.add(eng, lambda e: e.activation(out=out, in_=in_, func=AF.Copy), reads, writes)
        else:
            P.add(eng, lambda e: e.tensor_copy(out=out, in_=in_), reads, writes)

    def dma(q, out, in_, reads, writes, dsem):
        if q == "pool":
            writes = list(writes) + ["swq"]
        P.add(q, lambda e: e.dma_start(out=out, in_=in_), reads, writes, dsem=dsem)

    ones_bf = A.bf16(128)
    ident_bf = A.bf16(128)
    epsc = A.f32(1)
    zeroc = A.f32(1)
    gcols_sb = A.f32(32)
    cfar_sb = A.f32(16)
    base = A.off

    P.add("pool", lambda e: e.memset(ones_bf, 1.0), writes=["ones"])
    P.add("pool", lambda e: e.memset(epsc, 1e-6), writes=["consts"])
    P.add("pool", lambda e: e.memset(zeroc, 0.0), writes=["consts"])
    ident_f = A.f32(128)
    base = A.off
    dma("sp", ident_f, ident[:, :], [], ["ident_f"], "c_ident")
    cp("pool", ident_bf, ident_f, ["ident_f"], ["ident"])
    dma("sp", gcols_sb, gcols[:, :], [], ["gcols"], "c_gcols")
    dma("sp", cfar_sb, cfar[:, :], [], ["cfar"], "c_cfar")

    def rmsnorm(X, xkeys, gi, out, outkeys, sq, sd, rstd):
        for q4 in range(4):
            act(sq[:, 2 * q4:2 * q4 + 2, :], X[:, 2 * q4:2 * q4 + 2, :], AF.Square, xkeys[2 * q4:2 * q4 + 2], ["sq%d" % q4])
        b, bk = nextbank()
        for c in range(8):
            mm(b, ones_bf, sq[:, c, :], c == 0, c == 7, ["sq%d" % (c // 2), "ones"], [bk])
        act(sd, b, AF.Sqrt, [bk, "consts"], ["sd"], bias=epsc[:, 0:1], scale=1.0 / 1024.0)
        P.add("dve", lambda e: e.reciprocal(out=rstd, in_=sd), ["sd"], ["rstd"])
        for c in range(8):
            stt("dve", out[:, c, :], X[:, c, :], gcols_sb[:, gi * 8 + c: gi * 8 + c + 1], rstd, ALU.mult, ALU.mult,
                [xkeys[c], "rstd", "gcols"], [outkeys[c]])

    Wqkv = A.bf16(8 * 3072).rearrange("p (k c) -> p k c", k=8)
    Xk = [A.f32(4096).rearrange("p (k t) -> p k t", k=8) for _ in range(2)]
    sq = A.bf16(4096).rearrange("p (k t) -> p k t", k=8)
    hTs = [A.bf16(4096).rearrange("p (k t) -> p k t", k=8) for _ in range(2)]
    Kst = [A.bf16(4096).rearrange("p (k t) -> p k t", k=8) for _ in range(2)]
    Vst = [A.bf16(8192).rearrange("p (j two a d) -> p j two a d", j=8, two=2, a=4) for _ in range(2)]
    Qst = A.bf16(8192, parts=96)
    Qst3 = Qst.rearrange("p (h t) -> p h t", h=16)
    Qst4 = Qst.rearrange("p (g a t) -> p g a t", g=4, a=4)
    sd = A.f32(512)
    rstd = A.f32(512)
    ksum = A.f32(256).rearrange("p (j n) -> p j n", j=8)
    kmeanH = A.bf16(512, parts=64).rearrange("p (j a n) -> p j a n", j=8, a=2)
    gb_sb = A.f32(512).rearrange("p (i n) -> p i n", i=16)
    om_sb = A.f32(512).rearrange("p (i n) -> p i n", i=16)
    gms = [A.f32(512) for _ in range(4)]
    m8 = A.f32(128).rearrange("p (h n) -> p h n", h=16)
    thr = A.f32(16)
    sel = A.f32(512)
    sel3 = sel.rearrange("p (h n) -> p h n", h=16)
    selbs = [A.bf16(512) for _ in range(4)]
    if os.environ.get("K_TRACE"): print("phase1 arena words", A.off)

    w_in_v = w_in.rearrange("(k p) c -> p k c", p=128)
    wstg = [A.f32(1024) for _ in range(2)]
    n_ = 0
    for i, nm in (1, "wk"), (2, "wv"), (0, "wq"):
        for k8 in range(8):
            st_ = n_ % 2
            n_ += 1
            dma("sp", wstg[st_], w_in_v[:, k8, 2048 + i * 1024: 2048 + (i + 1) * 1024], [], ["wstg%d" % st_], "c_wstg%d" % st_)
            cp("pool" if k8 % 2 else "act", Wqkv[:, k8, i * 1024:(i + 1) * 1024], wstg[st_], ["wstg%d" % st_], [nm + str(k8)])
    dma("sp", gb_sb.rearrange("p i n -> p (i n)"), gatebias[:, :], [], ["gconst"], "c_gb")
    dma("sp", om_sb.rearrange("p i n -> p (i n)"), ownmask[:, :], [], ["gconst2"], "c_om")
    P.add("pool", lambda e: e.memset(ksum.rearrange("p j n -> p (j n)"), 0.0), writes=["ksum%d" % j for j in range(8)])
    for s_ in range(2):
        P.add("pool", (lambda o: (lambda e: e.memset(o, 1.0)))(Vst[s_].rearrange("p j two a d -> p (j two a d)")),
              writes=["Vst%d_%d_%d" % (s_, a, h) for a in range(4) for h in range(2)])

    def wcast(dst, src, rows, cols, rblk):
        for r0 in range(0, rows, rblk):
            dma("pool", dst[r0:r0 + rblk, :], src[r0:r0 + rblk, :], [], [], "wcast")

    def xkeys_of(s):
        return ["Xk%d_%d" % (s, c) for c in range(8)]

    def hkeys_of(s):
        return ["hT%d_%d" % (s, c) for c in range(8)]

    NT1 = int(os.environ.get('K_NT1', '16'))
    for m0 in range(min(2, NT1)):
        dma("sp", Xk[m0].rearrange("p k t -> p (k t)"), xk[m0].rearrange("p k t -> p (k t)"), [], xkeys_of(m0), "xk%d" % m0)
    wcast_done = False
    rmsnorm(Xk[0], xkeys_of(0), 0, hTs[0], hkeys_of(0), sq, sd, rstd)
    for m in range(NT1):
        s = m % 2
        if False:
            wcast_done = True
            wcast(Wa[:, 0:2048], w_in[:, 0:2048], 1024, 2048, 512)
            wcast(Wa[:, 2048:4096], w_in[:, 5120:7168], 1024, 2048, 512)
            wcast(Wo, w_out, 1024, 1024, 1024)
            wcast(W1, w_ff1, 1024, 4096, 256)
            wcast(W2, w_ff2, 4096, 1024, 1024)
            wcast(Wg, w_pg, 1024, 1024, 1024)
            wcast(Wp, w_pp, 256, 1024, 256)
        hT = hTs[s]
        hk = hkeys_of(s)
        own = (m % 4 == 3)
        p = m // 4
        for j in range(8):
            b, bk = nextbank()
            for c in range(8):
                mm(b, Wqkv[:, c, 1024 + j * 128: 1024 + (j + 1) * 128], hT[:, c, :], c == 0, c == 7, [hk[c], "wk%d" % c], [bk])
            act(Kst[s][:, j, :], b, AF.Copy, [bk], ["Kst%d_%d" % (s, j)])
            P.add("dve", (lambda o, i: (lambda e: e.tensor_reduce(out=o, in_=i, axis=AX.X, op=ALU.add)))(
                ksum[:, j, 2 * m:2 * m + 2], Kst[s][:, j, :].rearrange("p (a t) -> p a t", a=2)), ["Kst%d_%d" % (s, j)], ["ksum%d" % j])
        if m + 1 < NT1:
            s1 = (m + 1) % 2
            rmsnorm(Xk[s1], xkeys_of(s1), 0, hTs[s1], hkeys_of(s1), sq, sd, rstd)
        if m + 2 < NT1:
            dma("sp", Xk[s].rearrange("p k t -> p (k t)"), xk[m + 2].rearrange("p k t -> p (k t)"), [], xkeys_of(s), "xk%d" % s)
        def emit_V(ev):
            for t4 in range(4):
                for half in range(2):
                    b, bk = nextbank()
                    for c in range(8):
                        mm(b, hT[:, c, t4 * 128:(t4 + 1) * 128], Wqkv[:, c, 2048 + half * 512: 2048 + (half + 1) * 512],
                           c == 0, c == 7, [hk[c], "wv%d" % c], [bk])
                    b4 = b.rearrange("p (j two d) -> p j two d", j=4, two=2)
                    cp(ev, Vst[s][:, half * 4:(half + 1) * 4, 0, t4, 0:64], b4[:, :, 0, :], [bk], ["Vst%d_%d_%d" % (s, t4, half)])
                    cp(ev, Vst[s][:, half * 4:(half + 1) * 4, 1, t4, 64:128], b4[:, :, 1, :], [bk], ["Vst%d_%d_%d" % (s, t4, half)])

        if not own:
            emit_V("dve")
        else:
            for j in range(8):
                b, bk = nextbank()
                for c in range(8):
                    mm(b, Wqkv[:, c, j * 128:(j + 1) * 128], hT[:, c, :], c == 0, c == 7, [hk[c], "wq%d" % c], [bk])
                act(Qst3[0:64, 2 * j, :], b[0:64, :], AF.Copy, [bk], ["Qst_%d" % (2 * j)], scale=0.125)
                act(Qst3[0:64, 2 * j + 1, :], b[64:128, :], AF.Copy, [bk], ["Qst_%d" % (2 * j + 1)], scale=0.125)
            cp("dve", kmeanH[:, :, 0, :], ksum[0:64, :, :], ["ksum%d" % j for j in range(8)], ["kmeanH0"])
            cp("dve", kmeanH[:, :, 1, :], ksum[64:128, :, :], ["ksum%d" % j for j in range(8)], ["kmeanH1"])
            for qt in range(4):
                idx = p * 4 + qt
                b, bk = nextbank()
                for h in range(16):
                    mm(b[:, h * 32:(h + 1) * 32], Qst3[0:64, h, qt * 128:(qt + 1) * 128], kmeanH[0:64, h // 2, h % 2, :],
                       True, True, ["Qst_%d" % h, "kmeanH%d" % (h % 2)], [bk])
                tt("dve", gms[qt].rearrange("p (h n) -> p h n", h=16), b.rearrange("p (h n) -> p h n", h=16),
                   gb_sb[:, idx, :].unsqueeze(1).to_broadcast([128, 16, 32]), ALU.add, [bk, "gconst"], ["gm%d" % qt])
            emit_V("act")
            for qt in range(4):
                idx = p * 4 + qt
                gm3 = gms[qt].rearrange("p (h n) -> p h n", h=16)
                gk = "gm%d" % qt
                for h in range(16):
                    P.add("dve", (lambda o, i: (lambda e: e.max(out=o, in_=i)))(m8[:, h, :], gm3[:, h, :]), [gk], ["m8_%d" % h])
                ts("dve", thr.unsqueeze(2), m8[:, :, 2:3], -1e29, None, ALU.max, None, ["m8_%d" % h for h in range(16)], ["thr"])
                tt("dve", sel3, gm3, thr.unsqueeze(2).to_broadcast([128, 16, 32]), ALU.is_ge, [gk, "thr"], ["sel"])
                tt("dve", sel3, sel3, om_sb[:, idx, :].unsqueeze(1).to_broadcast([128, 16, 32]), ALU.add, ["sel", "gconst2"], ["sel"])
                ts("dve", selbs[qt], sel, -NEG, NEG, ALU.mult, ALU.add, ["sel"], ["selb%d" % qt])
            for qt in range(4):
                b2, bk2 = nextbank()
                for g in range(4):
                    mm(b2[:, g * 128:(g + 1) * 128], selbs[qt][:, g * 128:(g + 1) * 128], ident_bf, True, True, ["selb%d" % qt, "ident"], [bk2])
                for hp in range(4):
                    act(Qst4[64:96, :, hp, qt * 128:(qt + 1) * 128], b2[32 * hp:32 * hp + 32, :].rearrange("p (g q) -> p g q", g=4),
                        AF.Copy, [bk2], ["Qsel_%d" % hp])
        if os.environ.get("K_TRACE"): print("tile", m, "after V/Q/gate", len(P.ops))
        if os.environ.get('K_SKIP_ST'):
            continue
        dma("sp", Kscr[:, :, m * 512:(m + 1) * 512].rearrange("j p t -> p j t"), Kst[s],
            ["Kst%d_%d" % (s, j) for j in range(8)], [], "kst%d" % s)
        dma("sp", Vscr[:, :, 4 * m:4 * m + 4, :], Vst[s].rearrange("p j two a d -> p (j two) a d"),
            ["Vst%d_%d_%d" % (s, a, h) for a in range(4) for h in range(2)], [], "vst%d" % s)
        if own:
            dma("sp", Qscr[:, :, p * 512:(p + 1) * 512].rearrange("h r t -> r h t"), Qst3,
                ["Qst_%d" % h for h in range(16)] + ["Qsel_%d" % h for h in range(4)], [], "qst")

    P.barrier(exclude=("wcast",))
    if stop_phase == 1:
        P.barrier()
        return _finish(nc, P, es)

    A.off = base
    KK = [[A.bf16(2048, parts=96) for c in range(5)] for hd in range(2)]
    VV = [[A.bf16(2048).rearrange("p (k d) -> p k d", k=16) for c in range(5)] for hd in range(2)]
    QQ = [[A.bf16(2048, parts=96) for sl in range(2)] for hd in range(2)]
    stripf = [A.f32(STRIPW) for hd in range(2)]
    stripb = [[A.bf16(STRIPW) for sl in range(2)] for hd in range(2)]
    NPT = 6
    PT = [A.bf16(1024) for _ in range(NPT)]
    yst = [A.bf16(512) for _ in range(2)]
    Rr = A.f32(512)
    Ocp = [A.f32(512) for _ in range(2)]

    for c in range(5):
        cc = c % 4
        for hd in range(2):
            dma("sp" if hd == 0 else "act", KK[hd][c][64:96, :], onehot[:, cc * 2048:(cc + 1) * 2048], [], ["KOH%d%d" % (hd, c)], "oh%d%d" % (hd, c))
    wcf = [A.f32(4096) for _ in range(2)]
    wcb = [A.bf16(4096) for _ in range(2)]
    wpieces = []
    for r in range(8):
        wpieces.append([(Wa[r * 128:(r + 1) * 128, hf * 2048:(hf + 1) * 2048], w_in[r * 128:(r + 1) * 128, c0:c0 + 2048], hf * 2048, 2048, None)
                        for hf, c0 in ((0, 0), (1, 5120))])
    for r in range(8):
        wpieces.append([(W1[r * 128:(r + 1) * 128, :], w_ff1[r * 128:(r + 1) * 128, :], 0, 4096, None)])
    for r in range(8):
        wpieces.append([(W2[r * 512:(r + 1) * 512, :], w_ff2[r * 512:(r + 1) * 512, :], 0, 4096, 4)])
    for r in range(2):
        wpieces.append([(Wo[r * 512:(r + 1) * 512, :], w_out[r * 512:(r + 1) * 512, :], 0, 4096, 4)])
        wpieces.append([(Wg[r * 512:(r + 1) * 512, :], w_pg[r * 512:(r + 1) * 512, :], 0, 4096, 4)])
    wpieces.append([(Wp[:, :], w_pp[:, :], 0, 2048, 2)])
    wp_ctr = [0]
    cur_it = [0]

    def emit_wpiece():
        if wp_ctr[0] >= len(wpieces):
            return
        st_ = wp_ctr[0] % 2
        parts_ = wpieces[wp_ctr[0]]
        wp_ctr[0] += 1
        ntot = 0
        outs_ = []
        for (dst, src, c0, n, a) in parts_:
            if a is not None:
                dst = dst.rearrange("(a p) c -> p a c", p=128)
                src = src.rearrange("(a p) c -> p a c", p=128)
                f_ = wcf[st_][:, c0:c0 + n].rearrange("p (a c) -> p a c", a=a)
                b_ = wcb[st_][:, c0:c0 + n].rearrange("p (a c) -> p a c", a=a)
            else:
                f_ = wcf[st_][:, c0:c0 + n]
                b_ = wcb[st_][:, c0:c0 + n]
            dma("sp", f_, src, [], ["wcf%d" % st_], "wcf%d" % st_)
            outs_.append((dst, b_))
            ntot = max(ntot, c0 + n)
        hn = ntot // 2
        cp("dve", wcb[st_][:, 0:hn], wcf[st_][:, 0:hn], ["wcf%d" % st_], ["wcb%da" % st_])
        cp("dve", wcb[st_][:, hn:ntot], wcf[st_][:, hn:ntot], ["wcf%d" % st_], ["wcb%db" % st_])
        for (dst, b_) in outs_:
            deferred.append((cur_it[0] + 1, (lambda dst=dst, b_=b_, st_=st_: dma("act", dst, b_, ["wcb%da" % st_, "wcb%db" % st_], [], "wcb%d" % st_))))

    deferred = []

    def flush_deferred(force=False):
        keep = []
        while deferred:
            rdy, fn_ = deferred.pop(0)
            if force or rdy <= cur_it[0]:
                fn_()
            else:
                keep.append((rdy, fn_))
        deferred.extend(keep)

    pt_ctr = [0]
    sp_ctr = [0]
    for j in range(8):
        sl = j % 2
        for hd in range(2):
            h = 2 * j + hd
            dma("sp", QQ[hd][sl][0:96, :], Qscr[h], [], ["Q%d%d" % (hd, sl)], "q%d%d" % (hd, sl))
            dma("sp", stripf[hd], strip[h], [], ["stripf%d" % hd], "sf%d" % hd)
            cp("dve", stripb[hd][sl], stripf[hd], ["stripf%d" % hd], ["stripb%d%d" % (hd, sl)])
        cslot = [4 if (j % 2) else 0, 1, 2, 3]
        for c in range(4):
            cs = cslot[c]
            for hd in range(2):
                dma("sp", KK[hd][cs][0:64, :], Kscr[j, 64 * hd:64 * hd + 64, c * 2048:(c + 1) * 2048], [],
                    ["K%d%d" % (hd, cs)], "k%d%d" % (hd, cs))
                dma("sp", VV[hd][cs], Vscr[:, 2 * j + hd, 16 * c:16 * c + 16, :], [],
                    ["V%d%d" % (hd, cs)], "v%d%d" % (hd, cs))
        for p in range(4):
            it = j * 4 + p
            cur_it[0] = it
            emit_wpiece()
            Ob = [ps[0], ps[1]]
            ng = 16 * (p + 1)
            LAG = 1
            pend = []

            nfar = 16 * p + 5
            groups = [[2 * i, 2 * i + 1] for i in range(nfar // 2)] + [[nfar - 1]]
            groups += [[g] for g in range(nfar, ng - 8)] + [[g] for g in range(ng - 4, ng)] + [[g] for g in range(ng - 8, ng - 4)]
            first_g = groups[0][0]
            last_g = groups[-1][-1]

            def tinfo(g, p=p, cslot=cslot):
                c, t = divmod(g, 16)
                d = 4 * (4 * p + 3) - g
                return cslot[c], t, d, (128 * (-d) if d < 0 else 0)

            def emit_pv(hd, grp, pts, Ob=Ob, first_g=first_g, last_g=last_g):
                for idx, g in enumerate(grp):
                    c, t, d, off = tinfo(g)
                    mm(Ob[hd][0][:, off:512], VV[hd][c][:, t, :], PT[pts][:, idx * 512 + off:(idx + 1) * 512], g == first_g, g == last_g,
                       ["V%d%d" % (hd, c), "PT%d" % pts], [Ob[hd][1]])

            for gi, grp in enumerate(groups):
                if gi == 6:
                    flush_deferred()
                for hd in range(2):
                    h = 2 * j + hd
                    b0 = 2 + 2 * (sp_ctr[0] % 3)
                    sp_ctr[0] += 1
                    bkeys = []
                    near = False
                    for idx, g in enumerate(grp):
                        c, t, d, off = tinfo(g)
                        near = d <= 7
                        sb_, sbk = ps[b0 + idx]
                        bkeys.append(sbk)
                        mm(sb_[:, off:512], KK[hd][c][0:96, t * 128:(t + 1) * 128], QQ[hd][sl][0:96, p * 512 + off:(p + 1) * 512], True, not near,
                           ["K%d%d" % (hd, c), "KOH%d%d" % (hd, c), "Q%d%d" % (hd, sl)], [sbk])
                        if near:
                            mm(sb_[:, off:512], ident_bf, stripb[hd][sl][:, 128 * (d + 3) + off:128 * (d + 3) + 512], False, True,
                               ["stripb%d%d" % (hd, sl), "ident"], [sbk])
                    pts = pt_ctr[0] % NPT
                    pt_ctr[0] += 1
                    if len(grp) == 2:
                        act(PT[pts][:, 0:1024], psall[:, b0 * 512:(b0 + 2) * 512], AF.Exp, bkeys + ["consts", "cfar"], ["PT%d" % pts],
                            bias=cfar_sb[:, h:h + 1])
                    else:
                        bias = zeroc[:, 0:1] if near else cfar_sb[:, h:h + 1]
                        act(PT[pts][:, off:512], ps[b0][0][:, off:512], AF.Exp, bkeys + ["consts", "cfar"], ["PT%d" % pts], bias=bias)
                    pend.append((hd, gi, grp, pts))
                while pend and pend[0][1] <= gi - LAG:
                    hd_, gi_, grp_, pts_ = pend.pop(0)
                    emit_pv(hd_, grp_, pts_)
            while pend:
                hd_, gi_, grp_, pts_ = pend.pop(0)
                emit_pv(hd_, grp_, pts_)
            ys = it % 2
            cp("dve", Ocp[0], Ob[0][0], [Ob[0][1]], ["Ocp0"])
            cp("dve", Ocp[1], Ob[1][0], [Ob[1][1]], ["Ocp1"])
            P.add("dve", (lambda o, i: (lambda e: e.reciprocal(out=o, in_=i)))(Rr[0:64, :], Ocp[0][64:128, :]), ["Ocp0"], ["R0"])
            P.add("dve", (lambda o, i: (lambda e: e.reciprocal(out=o, in_=i)))(Rr[64:128, :], Ocp[1][0:64, :]), ["Ocp1"], ["R1"])
            tt("dve", yst[ys][0:64, :], Ocp[0][0:64, :], Rr[0:64, :], ALU.mult, ["Ocp0", "R0"], ["yst%da" % ys])
            tt("dve", yst[ys][64:128, :], Ocp[1][64:128, :], Rr[64:128, :], ALU.mult, ["Ocp1", "R1"], ["yst%db" % ys])
            deferred.append((it + 1, (lambda j=j, p=p, ys=ys: dma("act", Yscr[j, :, p * 512:(p + 1) * 512], yst[ys],
                                                                   ["yst%da" % ys, "yst%db" % ys], [], "yst%d" % ys))))

    while wp_ctr[0] < len(wpieces):
        emit_wpiece()
    flush_deferred(force=True)
    P.barrier()
    if stop_phase == 3:
        return _finish(nc, P, es)

    A.off = base
    NS = 5
    slots = [A.bf16(4096).rearrange("p (k c) -> p k c", k=8) for _ in range(NS)]
    hT = A.bf16(4096).rearrange("p (k t) -> p k t", k=8)
    sq = A.bf16(4096).rearrange("p (k t) -> p k t", k=8)
    yb = A.bf16(4096).rearrange("p (k t) -> p k t", k=8)
    vn = A.bf16(4096).rearrange("p (a c) -> p a c", a=4)
    merged = A.bf16(4096).rearrange("p (k t) -> p k t", k=8)
    aT = A.bf16(16384).rearrange("p (k t) -> p k t", k=32)
    pTb = A.bf16(1024).rearrange("p (k t) -> p k t", k=2)
    WsTb = A.bf16(1024).rearrange("p (g t) -> p g t", g=8)
    X = A.f32(4096).rearrange("p (k t) -> p k t", k=8)
    M = A.f32(4096).rearrange("p (k t) -> p k t", k=8)
    vg = [A.f32(1024) for _ in range(2)]
    vc = [A.f32(1024) for _ in range(2)]
    sqv = A.f32(1024)
    lng = A.f32(1024)
    lnb = A.f32(1024)
    bs_sb = A.f32(1024).rearrange("p (g t) -> p g t", g=8)
    NTMP = 6
    tmps = [A.f32(512) for _ in range(NTMP)]
    sd = A.f32(512)
    rstd = A.f32(512)
    wsf = A.f32(1024).rearrange("p (g t) -> p g t", g=8)
    trl = A.f32(128)
    pTf = A.f32(1024)
    s1 = A.f32(8)
    nmn = A.f32(8)
    s2 = A.f32(8)
    sdv = A.f32(8)
    rv = A.f32(8)

    dma("sp", lng, lnv[0:1, :].partition_broadcast(128), [], ["lng"], "c_lng")
    dma("sp", lnb, lnv[1:2, :].partition_broadcast(128), [], ["lnb"], "c_lnb")
    dma("sp", bs_sb.rearrange("p g t -> p (g t)"), bsg[0:1, :].partition_broadcast(128), [], ["bs"], "c_bs")
    dma("sp", wsf.rearrange("p g t -> p (g t)"), wsT[:, :], [], ["wsf"], "c_wsf")
    dma("sp", trl, trilT[:, :], [], ["trl"], "c_trl")
    tt("dve", WsTb, wsf, trl.unsqueeze(1).to_broadcast([128, 8, 128]), ALU.mult, ["wsf", "trl"], ["WsT"])

    slot_ctr = [0]

    def load_piece(src, nk):
        s = slot_ctr[0] % NS
        slot_ctr[0] += 1
        dma("sp", slots[s][:, 0:nk, :], src.rearrange("(k p) c -> p k c", p=128), [], ["ws%d" % s], "ws%d" % s)
        return slots[s], "ws%d" % s

    tmp_ctr = [0]

    def nexttmp():
        i = tmp_ctr[0] % NTMP
        tmp_ctr[0] += 1
        return tmps[i], "tmp%d" % i

    Xk_ = ["X_%d" % c for c in range(8)]
    hk = ["hT_%d" % c for c in range(8)]
    Mk = ["M_%d" % c for c in range(8)]
    mgk = ["mg_%d" % c for c in range(8)]

    def proj_fm(piece, pkey, nk, rhs3, rhs_keys, evac):
        for o in range(4):
            b, bk = nextbank()
            for k in range(nk):
                mm(b, piece[:, k, o * 128:(o + 1) * 128], rhs3[:, k, :], k == 0, k == nk - 1, [pkey, rhs_keys[k]], [bk])
            evac(o, b, bk)

    for p in range(4):
        dma("sp", X.rearrange("p k t -> p (k t)"), xk[4 * p + 3].rearrange("p k t -> p (k t)"), [], Xk_, "x4")
        dma("sp", yb, Yscr[:, :, p * 512:(p + 1) * 512].rearrange("j p t -> p j t"), [], ["yb"], "yb")
        dma("sp", pTf, pT[p].rearrange("p k t -> p (k t)"), [], ["pTf"], "ptf")
        cp("pool", pTb.rearrange("p k t -> p (k t)"), pTf, ["pTf"], ["pTb"])
        rmsnorm(X, Xk_, 0, hT, hk, sq, sd, rstd)
        pv = [load_piece(Wa[:, 1024 + half * 512: 1024 + (half + 1) * 512], 8) for half in range(2)]
        for t4 in range(4):
            vgi = vg[t4 % 2]
            vci = vc[t4 % 2]
            vgk = "vg%d" % (t4 % 2)
            vck = "vc%d" % (t4 % 2)
            for half in range(2):
                b, bk = nextbank()
                for c in range(8):
                    mm(b, hT[:, c, t4 * 128:(t4 + 1) * 128], pv[half][0][:, c, :], c == 0, c == 7, [hk[c], pv[half][1]], [bk])
                act(vgi[:, half * 512:(half + 1) * 512], b, AF.Gelu_apprx_tanh, [bk], [vgk + "_%d" % half])
            vgks = [vgk + "_0", vgk + "_1"]
            P.add("dve", (lambda o, i: (lambda e: e.reduce_sum(out=o, in_=i, axis=AX.X)))(s1[:, t4:t4 + 1], vgi), vgks, ["s1"])
            ts("dve", nmn[:, t4:t4 + 1], s1[:, t4:t4 + 1], -1.0 / 1024.0, None, ALU.mult, None, ["s1"], ["nmn"])
            act(vci, vgi, AF.Identity, vgks + ["nmn"], [vck], bias=nmn[:, t4:t4 + 1])
            tt("dve", sqv, vci, vci, ALU.mult, [vck], ["sqv"])
            P.add("dve", (lambda o, i: (lambda e: e.reduce_sum(out=o, in_=i, axis=AX.X)))(s2[:, t4:t4 + 1], sqv), ["sqv"], ["s2"])
            act(sdv[:, t4:t4 + 1], s2[:, t4:t4 + 1], AF.Sqrt, ["s2", "consts"], ["sdv"], bias=epsc[:, 0:1], scale=1.0 / 1024.0)
            P.add("dve", (lambda o, i: (lambda e: e.reciprocal(out=o, in_=i)))(rv[:, t4:t4 + 1], sdv[:, t4:t4 + 1]), ["sdv"], ["rv"])
            stt("dve", vci, vci, rv[:, t4:t4 + 1], lng, ALU.mult, ALU.mult, [vck, "rv", "lng"], [vck])
            tt("dve", vn[:, t4, :], vci, lnb, ALU.add, [vck, "lnb"], ["vn%d" % t4])
        for half in range(2):
            pc, pk = load_piece(Wa[:, half * 512:(half + 1) * 512], 8)

            def ev_u(o, b, bk, half=half):
                g = half * 4 + o
                act(M[:, g, :], b, AF.Gelu_apprx_tanh, [bk], [Mk[g]])
                pass
            proj_fm(pc, pk, 8, hT, hk, ev_u)
        for half in range(2):
            pc, pk = load_piece(Wa[:, 2048 + half * 512: 2048 + (half + 1) * 512], 8)

            def ev_ga(o, b, bk, half=half):
                g = half * 4 + o
                tm, tk = nexttmp()
                act(tm, b, AF.Sigmoid, [bk], [tk])
                tt("dve", M[:, g, :], M[:, g, :], tm, ALU.mult, [Mk[g], tk], [Mk[g]])
            proj_fm(pc, pk, 8, hT, hk, ev_ga)
        for g in range(8):
            b, bk = nextbank()
            for t4 in range(4):
                mm(b[:, t4 * 128:(t4 + 1) * 128], vn[:, t4, g * 128:(g + 1) * 128], WsTb[:, g, :], True, True,
                   ["vn%d" % t4, "WsT"], [bk])
            tm, tk = nexttmp()
            tt("dve", tm.rearrange("p (a t) -> p a t", a=4), b.rearrange("p (a t) -> p a t", a=4),
               bs_sb[:, g, :].unsqueeze(1).to_broadcast([128, 4, 128]), ALU.add, [bk, "bs"], [tk])
            tt("dve", M[:, g, :], M[:, g, :], tm, ALU.mult, [Mk[g], tk], [Mk[g]])
        for half in range(2):
            pc, pk = load_piece(Wa[:, 3072 + half * 512: 3072 + (half + 1) * 512], 8)

            def ev_gb(o, b, bk, half=half):
                g = half * 4 + o
                tm, tk = nexttmp()
                act(tm, b, AF.Sigmoid, [bk], [tk])
                tt("pool", tm, tm, yb[:, g, :], ALU.mult, [tk, "yb"], [tk])
                tt("dve", merged[:, g, :], M[:, g, :], tm, ALU.add, [Mk[g], tk], [mgk[g]])
            proj_fm(pc, pk, 8, hT, hk, ev_gb)
        for half in range(2):
            pc, pk = load_piece(Wo[:, half * 512:(half + 1) * 512], 8)

            def ev_o(o, b, bk, half=half):
                g = half * 4 + o
                tt("dve", X[:, g, :], X[:, g, :], b, ALU.add, [Xk_[g], bk], [Xk_[g]])
            proj_fm(pc, pk, 8, merged, mgk, ev_o)
        rmsnorm(X, Xk_, 1, hT, hk, sq, sd, rstd)
        for q in range(8):
            pc, pk = load_piece(W1[:, q * 512:(q + 1) * 512], 8)

            def ev_f1(o, b, bk, q=q):
                fc = q * 4 + o
                tm, tk = nexttmp()
                act(tm, b, AF.Relu, [bk], [tk])
                tt("pool" if (o % 2) else "dve", aT[:, fc, :], tm, tm, ALU.mult, [tk], ["aT_%d" % fc])
            proj_fm(pc, pk, 8, hT, hk, ev_f1)
        for i in range(4):
            for half in range(2):
                pc, pk = load_piece(W2[i * 1024:(i + 1) * 1024, half * 512:(half + 1) * 512], 8)
                for o in range(4):
                    b, bk = ps[half * 4 + o]
                    for k in range(8):
                        mm(b, pc[:, k, o * 128:(o + 1) * 128], aT[:, i * 8 + k, :], (i == 0 and k == 0), (i == 3 and k == 7),
                           [pk, "aT_%d" % (i * 8 + k)], [bk])
        for oc in range(8):
            b, bk = ps[oc]
            tt("dve", X[:, oc, :], X[:, oc, :], b, ALU.add, [Xk_[oc], bk], [Xk_[oc]])
        bank_ctr[0] = 0
        rmsnorm(X, Xk_, 2, hT, hk, sq, sd, rstd)
        for half in range(2):
            pg_, pgk = load_piece(Wg[:, half * 512:(half + 1) * 512], 8)
            pp_, ppk = load_piece(Wp[:, half * 512:(half + 1) * 512], 2)
            for o in range(4):
                g = half * 4 + o
                b, bk = nextbank()
                for k in range(8):
                    mm(b, pg_[:, k, o * 128:(o + 1) * 128], hT[:, k, :], k == 0, k == 7, [pgk, hk[k]], [bk])
                tm, tk = nexttmp()
                act(tm, b, AF.Sigmoid, [bk], [tk])
                b2, bk2 = nextbank()
                for k in range(2):
                    mm(b2, pp_[:, k, o * 128:(o + 1) * 128], pTb[:, k, :], k == 0, k == 1, [ppk, "pTb"], [bk2])
                tt("dve", tm, tm, b2, ALU.mult, [tk, bk2], [tk])
                tt("dve", M[:, g, :], X[:, g, :], tm, ALU.add, [Xk_[g], tk], [Mk[g]])
        rmsnorm(M, Mk, 3, M, Mk, sq, sd, rstd)
        dma("act", outT[p].rearrange("p k t -> p (k t)"), M.rearrange("p k t -> p (k t)"), Mk, [], "outst")

    P.barrier()
    return _finish(nc, P, es)


def _finish(nc, P, es):
    P.finalize()
    if os.environ.get("K_TRACE"):
        cnt = {}
        for op in P.ops:
            if op.count is not None:
                cnt[op.eng] = max(cnt.get(op.eng, 0), op.count)
        print("sem counts", cnt, "dma", P.dma_groups, "nops", {k: len(v) for k, v in P.stream_ops.items()})

    sems = {}
    for k in P.sem_keys():
        sems[k] = es.enter_context(nc.semaphore(k.replace(":", "_")))
    block = es.enter_context(nc.Block())

    @block.sync
    def _(e):
        P.emit_stream("sp", e, sems)

    @block.vector
    def _(e):
        P.emit_stream("dve", e, sems)

    @block.scalar
    def _(e):
        P.emit_stream("act", e, sems)

    @block.gpsimd
    def _(e):
        P.emit_stream("pool", e, sems)

    @block.tensor
    def _(e):
        P.emit_stream("pe", e, sems)

    es.close()
    return nc


def _rel_bucket_np(dist):
    n = np.maximum(dist, 0)
    nf = np.maximum(n, 16).astype(np.float32)
    large = 16 + (np.log(nf / np.float32(16)) / np.float32(math.log(1024 / 16)) * np.float32(16)).astype(np.int32)
    large = np.minimum(large, 31)
    return np.where(n < 16, n, large)


_NC_CACHE = {}


def kernel(x, p, norm_mix_g, w_in, w_sgu_spatial, b_sgu_spatial, ln_v_g, ln_v_b, rel_bias,
           w_out, norm_ffn_g, w_ff1, w_ff2, norm_ple_g, w_ple_gate, w_ple_proj, norm_final_g):
    f32 = np.float32
    x = np.asarray(x, f32)
    p = np.asarray(p, f32)
    rel_bias = np.asarray(rel_bias, f32)
    in_maps = make_inputs(x, p, norm_mix_g, w_in, w_sgu_spatial, b_sgu_spatial, ln_v_g, ln_v_b, rel_bias,
                          w_out, norm_ffn_g, w_ff1, w_ff2, norm_ple_g, w_ple_gate, w_ple_proj, norm_final_g)
    if "nc" not in _NC_CACHE:
        _NC_CACHE["nc"] = build_program()
    nc = _NC_CACHE["nc"]
    res = run_bass_kernel_spmd(nc, in_maps, core_ids=list(range(8)))
    out = np.empty((2, 8192, 1024), f32)
    for c in range(8):
        b, j = divmod(c, 4)
        oT = res.results[c]["outT"]
        for pp in range(4):
            m = 4 * pp + j
            out[b, m * 512:(m + 1) * 512, :] = oT[pp].transpose(2, 1, 0).reshape(512, 1024)
    return out


def make_inputs(x, p, norm_mix_g, w_in, w_sgu_spatial, b_sgu_spatial, ln_v_g, ln_v_b, rel_bias,
                w_out, norm_ffn_g, w_ff1, w_ff2, norm_ple_g, w_ple_gate, w_ple_proj, norm_final_g):
    f32 = np.float32
    x = np.asarray(x, f32)
    p = np.asarray(p, f32)
    rel_bias = np.asarray(rel_bias, f32)

    def fm(a):
        t, F = a.shape
        return np.ascontiguousarray(a.T.reshape(F // 128, 128, t).transpose(1, 0, 2))

    gcols = np.zeros((128, 32), f32)
    for i, g in enumerate((norm_mix_g[0], norm_ffn_g[0], norm_ple_g[0], norm_final_g)):
        gcols[:, i * 8:(i + 1) * 8] = np.asarray(g, f32).reshape(8, 128).T
    lnv = np.stack([np.asarray(ln_v_g[0], f32), np.asarray(ln_v_b[0], f32)])
    bsg = np.ascontiguousarray(np.asarray(b_sgu_spatial[0], f32).reshape(1, 1024))
    wsT = np.ascontiguousarray(np.asarray(w_sgu_spatial[0], f32).transpose(2, 0, 1).reshape(128, 1024))
    trilT = (np.arange(128)[:, None] <= np.arange(128)[None, :]).astype(f32)
    ki = np.arange(128)[:, None]
    mm_ = np.arange(STRIPW)[None, :]
    dist = mm_ - 384 - ki
    bidx = _rel_bucket_np(dist)
    strip = np.empty((16, 128, STRIPW), f32)
    for h in range(16):
        strip[h] = np.where(dist >= 0, rel_bias[bidx, h], f32(NEG))
    import ml_dtypes
    onehot = (np.arange(8192)[None, :] // 256 == np.arange(32)[:, None]).astype(ml_dtypes.bfloat16)
    ident = np.eye(128, dtype=f32)
    cfar = np.ascontiguousarray(np.broadcast_to(rel_bias[31][None, :], (128, 16))).astype(f32)

    in_maps = []
    for c in range(8):
        b, j = divmod(c, 4)
        shift = 3 - j
        xkc = np.zeros((16, 128, 8, 512), f32)
        for m in range(16):
            ms = m - shift
            if ms >= 0:
                xkc[m] = fm(x[b, ms * 512:(ms + 1) * 512, :])
        pTc = np.stack([fm(p[0, b, (4 * pp + j) * 512:(4 * pp + j + 1) * 512, :]) for pp in range(4)])
        gbias = np.zeros((128, 16, 32), f32)
        omask = np.zeros((128, 16, 32), f32)
        n = np.arange(32)
        for pp in range(4):
            for qt in range(4):
                own = 8 * pp + 6 + qt // 2
                valid = (n >= 2 * shift) & (n < own)
                gbias[:, pp * 4 + qt, :] = np.where(valid, f32(0), f32(-1e30))[None, :]
                omask[:, pp * 4 + qt, :] = (n == own).astype(f32)[None, :]
        in_maps.append({
            "xk": xkc, "pT": pTc,
            "w_in": np.asarray(w_in[0], f32), "w_out": np.asarray(w_out[0], f32),
            "w_ff1": np.asarray(w_ff1[0], f32), "w_ff2": np.asarray(w_ff2[0], f32),
            "w_pg": np.asarray(w_ple_gate[0], f32), "w_pp": np.asarray(w_ple_proj[0], f32),
            "gcols": gcols, "lnv": lnv, "bsg": bsg, "wsT": wsT, "trilT": trilT, "strip": strip,
            "onehot": onehot, "ident": ident, "cfar": cfar,
            "gatebias": gbias.reshape(128, 512), "ownmask": omask.reshape(128, 512),
        })
    return in_maps
```
